# Optimizing a Trainium2 kernel written in Bass

```python
import jax, jax.numpy as jnp
from jax import lax
import numpy as np

D_MODEL = 2048
BATCH = 2
SEQ = 4096
DEPTH = 4

CHUNK = 64
PLE_DIM = 256
EPS = 1e-6
MLA_HEADS = 16
Q_LORA = 512
KV_LORA = 512
QK_NOPE = 128
QK_ROPE = 64
V_DIM = 128
QK_DIM = QK_NOPE + QK_ROPE
ROPE_THETA = 10000.0
Q_BLOCK = 128
SSM_EXPAND = 2
D_INNER = SSM_EXPAND * D_MODEL
SSM_HEADDIM = 64
SSM_HEADS = D_INNER // SSM_HEADDIM
SSM_GROUPS = 8
SSM_STATE = 128
HEADS_PER_GROUP = SSM_HEADS // SSM_GROUPS
CONV_WIDTH = 4
CONV_DIM = D_INNER + 2 * SSM_GROUPS * SSM_STATE
SSD_CHUNK = CHUNK
D_FF = 4 * D_MODEL
N_BRANCH = 2
SPLIT_POINTS = (
    Q_LORA,
    Q_LORA + KV_LORA,
    Q_LORA + KV_LORA + QK_ROPE,
    Q_LORA + KV_LORA + QK_ROPE + D_INNER,
    Q_LORA + KV_LORA + QK_ROPE + D_INNER + CONV_DIM,
    Q_LORA + KV_LORA + QK_ROPE + D_INNER + CONV_DIM + SSM_HEADS,
)
D_IN_PROJ = Q_LORA + KV_LORA + QK_ROPE + D_INNER + CONV_DIM + SSM_HEADS + N_BRANCH * D_MODEL

kernel_name = 'hybrid_mla_ssd_parallel_gated_trunk'


def rms_norm(x, w):
    xf = x.astype(jnp.float32)
    y = xf * lax.rsqrt(jnp.mean(xf * xf, axis=-1, keepdims=True) + EPS)
    return (y * w.astype(jnp.float32)).astype(x.dtype)


def rope_tables(positions):
    inv_freq = 1.0 / (ROPE_THETA ** (jnp.arange(0, QK_ROPE, 2, dtype=jnp.float32) / QK_ROPE))
    ang = positions.astype(jnp.float32)[..., None] * inv_freq
    return jnp.cos(ang), jnp.sin(ang)


def apply_rope(x, cos, sin):
    xf = x.astype(jnp.float32)
    x1, x2 = jnp.split(xf, 2, axis=-1)
    c = cos[:, :, None, :]
    s = sin[:, :, None, :]
    return jnp.concatenate([x1 * c - x2 * s, x1 * s + x2 * c], axis=-1).astype(x.dtype)


def chunk_causal_attention(q, k, v):
    B, S, H, Dh = q.shape
    n_blk = S // Q_BLOCK
    scale = Dh ** -0.5
    qb = q.reshape(B, n_blk, Q_BLOCK, H, Dh).transpose(1, 0, 3, 2, 4)
    kh = k.transpose(0, 2, 1, 3)
    vh = v.transpose(0, 2, 1, 3)
    key_chunk = jnp.arange(S) // CHUNK

    def one_block(args):
        q_blk, blk = args
        s = jnp.einsum('bhqd,bhkd->bhqk', q_blk, kh, preferred_element_type=jnp.float32) * scale
        q_chunk = (blk * Q_BLOCK + jnp.arange(Q_BLOCK)) // CHUNK
        mask = key_chunk[None, :] <= q_chunk[:, None]
        s = jnp.where(mask, s, -jnp.inf)
        pr = jax.nn.softmax(s, axis=-1)
        return jnp.einsum('bhqk,bhkd->bhqd', pr.astype(vh.dtype), vh)

    o = lax.map(one_block, (qb, jnp.arange(n_blk)))
    return o.transpose(1, 0, 3, 2, 4).reshape(B, S, H, V_DIM)


def mla_branch(c_q, c_kv, k_r, cos, sin, q_a_norm_w, w_uq, kv_a_norm_w, w_ukv, q_norm_w, k_norm_w):
    B, S, _ = c_q.shape
    q = (rms_norm(c_q, q_a_norm_w) @ w_uq).reshape(B, S, MLA_HEADS, QK_DIM)
    kv = (rms_norm(c_kv, kv_a_norm_w) @ w_ukv).reshape(B, S, MLA_HEADS, QK_NOPE + V_DIM)
    k_nope, v = kv[..., :QK_NOPE], kv[..., QK_NOPE:]
    k_rope = jnp.broadcast_to(k_r[:, :, None, :], (B, S, MLA_HEADS, QK_ROPE))
    k = jnp.concatenate([k_nope, k_rope], axis=-1)
    q = rms_norm(q, q_norm_w)
    k = rms_norm(k, k_norm_w)
    q = jnp.concatenate([q[..., :QK_NOPE], apply_rope(q[..., QK_NOPE:], cos, sin)], axis=-1)
    k = jnp.concatenate([k[..., :QK_NOPE], apply_rope(k[..., QK_NOPE:], cos, sin)], axis=-1)
    o = chunk_causal_attention(q, k, v)
    return o.reshape(B, S, MLA_HEADS * V_DIM)


def causal_depthwise_conv(x, w, b):
    S = x.shape[1]
    xp = jnp.pad(x, ((0, 0), (CONV_WIDTH - 1, 0), (0, 0)))
    y = xp[:, 0:S] * w[0]
    for tap in range(1, CONV_WIDTH):
        y = y + xp[:, tap:tap + S] * w[tap]
    return y + b


def ssd_branch(xbc_raw, z, dt_raw, conv_w, conv_b, dt_bias, a_log, d_skip, ssm_norm_w):
    B, S, _ = z.shape
    G, R, P, N, T = SSM_GROUPS, HEADS_PER_GROUP, SSM_HEADDIM, SSM_STATE, SSD_CHUNK
    nc = S // T
    xbc = jax.nn.silu(causal_depthwise_conv(xbc_raw, conv_w, conv_b)).astype(jnp.float32)
    x5 = xbc[..., :D_INNER].reshape(B, nc, T, G, R, P)
    Bm = xbc[..., D_INNER:D_INNER + G * N].reshape(B, nc, T, G, N)
    Cm = xbc[..., D_INNER + G * N:].reshape(B, nc, T, G, N)
    dt = jax.nn.softplus(dt_raw.astype(jnp.float32) + dt_bias.astype(jnp.float32))
    A = -jnp.exp(a_log.astype(jnp.float32))
    dt5 = dt.reshape(B, nc, T, G, R)
    a_cum = jnp.cumsum((dt * A).reshape(B, nc, T, G, R), axis=2)
    seg = a_cum[:, :, :, None] - a_cum[:, :, None]
    tri = jnp.tril(jnp.ones((T, T), dtype=bool))
    decay = jnp.exp(jnp.where(tri[:, :, None, None], seg, -jnp.inf))
    cb = jnp.einsum('bctgn,bcsgn->bctsg', Cm, Bm)
    m = cb[..., None] * decay * dt5[:, :, None]
    y_diag = jnp.einsum('bctsgr,bcsgrp->bctgrp', m, x5)
    decay_to_end = jnp.exp(a_cum[:, :, -1:] - a_cum)
    states = jnp.einsum('bcsgn,bcsgr,bcsgrp->bcgrpn', Bm, decay_to_end * dt5, x5)
    chunk_decay = jnp.exp(a_cum[:, :, -1])

    def step(h, inp):
        st, dec = inp
        return h * dec[..., None, None] + st, h

    h0 = jnp.zeros((B, G, R, P, N), jnp.float32)
    _, prev = lax.scan(step, h0, (jnp.moveaxis(states, 1, 0), jnp.moveaxis(chunk_decay, 1, 0)))
    prev = jnp.moveaxis(prev, 0, 1)
    y_off = jnp.einsum('bctgn,bcgrpn,bctgr->bctgrp', Cm, prev, jnp.exp(a_cum))
    y = y_diag + y_off + d_skip.astype(jnp.float32).reshape(G, R)[..., None] * x5
    y = y.reshape(B, S, D_INNER) * jax.nn.silu(z.astype(jnp.float32))
    yg = y.reshape(B, S, G, D_INNER // G)
    yg = yg * lax.rsqrt(jnp.mean(yg * yg, axis=-1, keepdims=True) + EPS)
    y = yg.reshape(B, S, D_INNER) * ssm_norm_w.astype(jnp.float32)
    return y.astype(z.dtype)


def setup_inputs(seed: int = 0) -> dict:
    key = jax.random.key(seed)
    ks = jax.random.split(key, 32)
    f32 = jnp.float32

    def nrm(k, shape, fan_in):
        return jax.random.normal(k, shape, f32) * (fan_in ** -0.5)

    def gain(k, dim):
        return 1.0 + 0.02 * jax.random.normal(k, (DEPTH, dim), f32)

    x = jax.random.normal(ks[0], (BATCH, SEQ, D_MODEL), f32)
    p = jax.random.normal(ks[1], (DEPTH, BATCH, SEQ, PLE_DIM), f32)
    start = jax.random.randint(ks[2], (BATCH, 1), 0, 16384, dtype=jnp.int32)
    positions = (start + jnp.arange(SEQ, dtype=jnp.int32)[None, :]).astype(jnp.int32)
    dt_init = jnp.exp(jax.random.uniform(ks[15], (DEPTH, SSM_HEADS), f32, np.log(1e-3), np.log(1e-1)))
    dt_bias = dt_init + jnp.log(-jnp.expm1(-dt_init))
    a_log = jnp.log(jax.random.uniform(ks[16], (DEPTH, SSM_HEADS), f32, 1.0, 16.0))
    return {
        'x': x,
        'p': p,
        'positions': positions,
        'norm_mix_w': gain(ks[3], D_MODEL),
        'w_in': nrm(ks[4], (DEPTH, D_MODEL, D_IN_PROJ), D_MODEL),
        'q_a_norm_w': gain(ks[5], Q_LORA),
        'w_uq': nrm(ks[6], (DEPTH, Q_LORA, MLA_HEADS * QK_DIM), Q_LORA),
        'kv_a_norm_w': gain(ks[7], KV_LORA),
        'w_ukv': nrm(ks[8], (DEPTH, KV_LORA, MLA_HEADS * (QK_NOPE + V_DIM)), KV_LORA),
        'q_norm_w': gain(ks[9], QK_DIM),
        'k_norm_w': gain(ks[10], QK_DIM),
        'w_o_mla': nrm(ks[11], (DEPTH, MLA_HEADS * V_DIM, D_MODEL), MLA_HEADS * V_DIM),
        'conv_w': nrm(ks[12], (DEPTH, CONV_WIDTH, CONV_DIM), CONV_WIDTH),
        'conv_b': 0.01 * jax.random.normal(ks[13], (DEPTH, CONV_DIM), f32),
        'dt_bias': dt_bias,
        'a_log': a_log,
        'd_skip': 1.0 + 0.1 * jax.random.normal(ks[17], (DEPTH, SSM_HEADS), f32),
        'ssm_norm_w': gain(ks[18], D_INNER),
        'w_o_ssm': nrm(ks[19], (DEPTH, D_INNER, D_MODEL), D_INNER),
        'w_out': nrm(ks[20], (DEPTH, D_MODEL, D_MODEL), D_MODEL),
        'norm_mlp_w': gain(ks[21], D_MODEL),
        'w_up': nrm(ks[22], (DEPTH, D_MODEL, D_FF), D_MODEL),
        'w_down': nrm(ks[23], (DEPTH, D_FF, D_MODEL), D_FF),
        'ple_norm_w': gain(ks[24], D_MODEL),
        'w_ple_gate': nrm(ks[25], (DEPTH, D_MODEL, D_MODEL), D_MODEL),
        'w_ple': nrm(ks[26], (DEPTH, PLE_DIM, D_MODEL), PLE_DIM),
    }


def reference(x, p, positions, norm_mix_w, w_in, q_a_norm_w, w_uq, kv_a_norm_w, w_ukv,
              q_norm_w, k_norm_w, w_o_mla, conv_w, conv_b, dt_bias, a_log, d_skip,
              ssm_norm_w, w_o_ssm, w_out, norm_mlp_w, w_up, w_down, ple_norm_w,
              w_ple_gate, w_ple):
    B, S, _ = x.shape
    cos, sin = rope_tables(positions)
    for i in range(DEPTH):
        h = rms_norm(x, norm_mix_w[i])
        proj = h @ w_in[i]
        c_q, c_kv, k_r, z, xbc, dt_raw, gate_logits = jnp.split(proj, SPLIT_POINTS, axis=-1)
        y_a = mla_branch(c_q, c_kv, k_r, cos, sin, q_a_norm_w[i], w_uq[i], kv_a_norm_w[i],
                         w_ukv[i], q_norm_w[i], k_norm_w[i]) @ w_o_mla[i]
        y_b = ssd_branch(xbc, z, dt_raw, conv_w[i], conv_b[i], dt_bias[i], a_log[i],
                         d_skip[i], ssm_norm_w[i]) @ w_o_ssm[i]
        g = jax.nn.sigmoid(gate_logits.astype(jnp.float32)).reshape(B, S, N_BRANCH, D_MODEL)
        merged = (g[:, :, 0] * y_a + g[:, :, 1] * y_b).astype(x.dtype)
        x = x + (merged @ w_out[i]).astype(x.dtype)
        h2 = rms_norm(x, norm_mlp_w[i])
        x = x + (jnp.square(jax.nn.relu(h2 @ w_up[i])) @ w_down[i]).astype(x.dtype)
        ple_gate = jax.nn.sigmoid((rms_norm(x, ple_norm_w[i]) @ w_ple_gate[i]).astype(jnp.float32))
        x = x + ((p[i] @ w_ple[i]) * ple_gate).astype(x.dtype)
    return x
```

```python
import math
import numpy as np
import concourse.bass as bass
import concourse.mybir as mybir
from concourse.bass_utils import run_bass_kernel_spmd

F32 = mybir.dt.float32
BF16 = mybir.dt.bfloat16
I32 = mybir.dt.int32
AF = mybir.ActivationFunctionType
ALU = mybir.AluOpType
PG = 512
NSLOT = 8
DTSZ = {F32: 4, BF16: 2, I32: 4}

D_MODEL = 2048
SEQ = 4096
DEPTH = 4
EPS = 1e-6
HEADS = 16
Q_LORA = 512
KV_LORA = 512
QK_NOPE = 128
QK_ROPE = 64
V_DIM = 128
QK_DIM = 192
D_INNER = 4096
SSM_HEADS = 64
SSM_GROUPS = 8
SSM_STATE = 128
CONV_DIM = 6144
D_FF = 8192
PLE_DIM = 256
D_IN_PROJ = 15488
TB = 512
STOP = None
SM_SCALE = QK_DIM ** -0.5
SM_SHIFT = 14.0


class Page:
    __slots__ = ("w", "r")

    def __init__(self):
        self.w = {}
        self.r = {}


class V:
    __slots__ = ("ap", "pages", "excl")

    def __init__(self, ap, pages, excl=False):
        self.ap = ap
        self.pages = pages
        self.excl = excl


class Tile:
    def __init__(self, prog, name, shape, dtype, space="sbuf", off=None):
        self.name, self.shape, self.dtype = name, list(shape), dtype
        nc = prog.nc
        self.esz = DTSZ[dtype]
        self.nbytes = int(np.prod(shape[1:])) * self.esz
        if space == "sbuf":
            assert off is not None and off % 32 == 0
            assert off + self.nbytes <= prog.sb_limit, f"SBUF overflow at {name}: {off + self.nbytes}"
            self.off = off
            self.t = nc.alloc_sbuf_tensor_at(name, list(shape), dtype, offset=off)
            self.pg = prog.sb_pages
        else:
            self.t = nc.alloc_psum_tensor(name, list(shape), dtype)
            self.off = 0
            self.pg = [Page()]
        self.excl = (space != "sbuf")
        self._cache = {}

    def _pages(self, lo, hi):
        if self.excl:
            return self.pg
        a, b = (self.off + lo) // PG, (self.off + hi - 1) // PG
        k = (a, b)
        p = self._cache.get(k)
        if p is None:
            p = self.pg[a:b + 1]
            self._cache[k] = p
        return p

    def all(self):
        return V(self.t[:], self._pages(0, self.nbytes), self.excl)

    def v(self, ap, lo=None, hi=None):
        lo = 0 if lo is None else lo * self.esz
        hi = self.nbytes if hi is None else hi * self.esz
        return V(ap, self._pages(lo, hi), self.excl)

    def c(self, i, a=None, b=None, p0=0, p1=None):
        n = self.shape[2]
        a = 0 if a is None else a
        b = n if b is None else b
        p1 = self.shape[0] if p1 is None else p1
        return V(self.t[p0:p1, i, a:b], self._pages((i * n + a) * self.esz, (i * n + b) * self.esz), self.excl)

    def s(self, a=None, b=None, p0=0, p1=None):
        a = 0 if a is None else a
        b = self.shape[1] if b is None else b
        p1 = self.shape[0] if p1 is None else p1
        return V(self.t[p0:p1, a:b], self._pages(a * self.esz, b * self.esz), self.excl)


class Dram:
    def __init__(self, ap):
        self.ap = ap
        self.pg = [Page()]

    def v(self, ap=None):
        return V(self.ap if ap is None else ap, self.pg)


class Op:
    __slots__ = ("eng", "fn", "is_dma", "tl", "cnt", "waits", "need_inc", "snap", "val")


class Prog:
    def __init__(self, nc, same_engine_sync=True, sb_base=20992, sb_limit=229376):
        self.nc = nc
        self.ops = []
        self.same_engine_sync = same_engine_sync
        self.sb_base = sb_base
        self.sb_limit = sb_limit
        self.sb_pages = [Page() for _ in range(sb_limit // PG + 2)]
        self.known = {e: {} for e in ("tensor", "vector", "scalar", "gpsimd", "sync")}
        self.tl_count = {}
        self.tl_last = {}
        self.dma_rr = {"sync": 0, "gpsimd": 0, "scalar": 0}
        self.epoch = 0

    def op(self, eng, fn, reads=(), writes=(), dma=False):
        o = Op()
        o.eng, o.fn, o.is_dma, o.need_inc = eng, fn, dma, False
        if dma:
            slot = self.dma_rr[eng]
            self.dma_rr[eng] = (slot + 1) % NSLOT
            o.tl = ("dma", eng, slot, self.epoch)
            lastkey = ("dma", eng, slot)
        else:
            o.tl = ("eng", eng, self.epoch)
        xr = [v for v in reads if v.excl]
        if xr:
            reads = [v for v in reads if not v.excl]
            writes = list(writes) + xr
        k = self.known[eng]
        need = {}
        own = ("eng", eng)
        skip_own = (not dma) and (eng == "tensor" or not self.same_engine_sync)

        def consider(d):
            tl = d.tl
            if skip_own and tl[:2] == own:
                return
            if k.get(tl, 0) >= d.cnt:
                return
            cur = need.get(tl)
            if cur is None or cur.cnt < d.cnt:
                need[tl] = d

        if dma:
            prev = self.tl_last.get(lastkey)
            if prev is not None:
                consider(prev)
        for v in reads:
            for p in v.pages:
                for d in p.w.values():
                    consider(d)
        for v in writes:
            for p in v.pages:
                for d in p.w.values():
                    consider(d)
                for d in p.r.values():
                    consider(d)
        wl = []
        for tl, d in need.items():
            if k.get(tl, 0) >= d.cnt:
                continue
            wl.append(d)
            d.need_inc = True
            k[tl] = d.cnt
            for t2, c2 in d.snap.items():
                if k.get(t2, 0) < c2:
                    k[t2] = c2
        o.waits = wl
        c = self.tl_count.get(o.tl, 0) + 1
        self.tl_count[o.tl] = c
        o.cnt = c
        if dma:
            self.tl_last[lastkey] = o
        o.snap = dict(k)
        for v in reads:
            for p in v.pages:
                p.r[o.tl] = o
        for v in writes:
            for p in v.pages:
                if p.r:
                    p.r = {}
                    p.w = {o.tl: o}
                else:
                    p.w[o.tl] = o
        self.ops.append(o)
        return o

    def dma(self, eng, out, in_, **kw):
        oa, ia = out.ap, in_.ap
        return self.op(eng, lambda e: e.dma_start(out=oa, in_=ia, **kw), [in_], [out], dma=True)

    def emit(self, final_waits=()):
        nc = self.nc
        for d in final_waits:
            d.need_inc = True
        tl_val = {}
        for o in self.ops:
            if o.is_dma:
                o.val = tl_val[o.tl] = tl_val.get(o.tl, 0) + 16
            elif o.need_inc:
                o.val = tl_val[o.tl] = tl_val.get(o.tl, 0) + 1
        sems = {tl: nc.alloc_semaphore("s_" + "_".join(str(x) for x in tl)) for tl in tl_val}
        per_eng = {e: [] for e in self.known}
        for o in self.ops:
            per_eng[o.eng].append(o)

        def run(engname, e):
            for o in per_eng[engname]:
                for d in o.waits:
                    e.wait_ge(sems[d.tl], d.val)
                ins = o.fn(e)
                if o.is_dma:
                    ins.then_inc(sems[o.tl], 16)
                elif o.need_inc:
                    ins.then_inc(sems[o.tl], 1)
            if engname == "sync":
                for d in final_waits:
                    e.wait_ge(sems[d.tl], d.val)

        with nc.Block() as block:
            @block.sync
            def _(e):
                run("sync", e)

            @block.scalar
            def _(e):
                run("scalar", e)

            @block.vector
            def _(e):
                run("vector", e)

            @block.gpsimd
            def _(e):
                run("gpsimd", e)

            @block.tensor
            def _(e):
                run("tensor", e)


WNAMES = ["norm_mix_w", "w_in", "q_a_norm_w", "w_uq", "kv_a_norm_w", "w_ukv", "q_norm_w", "k_norm_w",
          "w_o_mla", "conv_w", "conv_b", "dt_bias", "a_log", "d_skip", "ssm_norm_w", "w_o_ssm", "w_out",
          "norm_mlp_w", "w_up", "w_down", "ple_norm_w", "w_ple_gate", "w_ple"]
WSHAPES = {
    "norm_mix_w": [D_MODEL], "w_in": [D_MODEL, D_IN_PROJ], "q_a_norm_w": [Q_LORA], "w_uq": [Q_LORA, HEADS * QK_DIM],
    "kv_a_norm_w": [KV_LORA], "w_ukv": [KV_LORA, HEADS * 256], "q_norm_w": [QK_DIM], "k_norm_w": [QK_DIM],
    "w_o_mla": [HEADS * V_DIM, D_MODEL], "conv_w": [4, CONV_DIM], "conv_b": [CONV_DIM], "dt_bias": [SSM_HEADS],
    "a_log": [SSM_HEADS], "d_skip": [SSM_HEADS], "ssm_norm_w": [D_INNER], "w_o_ssm": [D_INNER, D_MODEL],
    "w_out": [D_MODEL, D_MODEL], "norm_mlp_w": [D_MODEL], "w_up": [D_MODEL, D_FF], "w_down": [D_FF, D_MODEL],
    "ple_norm_w": [D_MODEL], "w_ple_gate": [D_MODEL, D_MODEL], "w_ple": [PLE_DIM, D_MODEL],
}


def build(L=DEPTH, S=SEQ, dbg=False):
    NBLK = S // TB
    nc = bass.Bass("TRN2", target_bir_lowering=False)
    P = Prog(nc)

    def din(name, shape, dt=F32):
        return Dram(nc.dram_tensor(name, shape, dt, kind="ExternalInput").ap())

    def dscr(name, shape, dt=F32):
        return Dram(nc.dram_tensor(name, shape, dt).ap())

    x_d = din("x", [S, D_MODEL])
    p_d = din("p", [L, S, PLE_DIM])
    pos_d = din("positions", [1, S], I32)
    invf_d = din("invf", [1, 128])
    W = {n: din(n, [L] + WSHAPES[n]) for n in WNAMES}
    out_d = Dram(nc.dram_tensor("out", [S, D_MODEL], F32, kind="ExternalOutput").ap())
    dbg_outs = {}

    xT_d = dscr("xT_s", [D_MODEL, S])
    xbc_d = dscr("xbc_s", [CONV_DIM, 3 + S])
    zs_d = dscr("zs_s", [D_INNER, TB])
    g_d = dscr("g_s", [2 * D_MODEL, TB])
    knT_d = dscr("knT_s", [HEADS, 128, S], BF16)
    krT_d = dscr("krT_s", [HEADS, 64, S], BF16)
    v_d = dscr("v_s", [S, HEADS * V_DIM], BF16)
    oT_d = dscr("oT_s", [HEADS * V_DIM, TB], BF16)
    cos_d = dscr("cos_s", [64, S])
    sin_d = dscr("sin_s", [64, S])

    cur = [P.sb_base]

    def alloc(name, shape, dt, at=None):
        if at is None:
            off = (cur[0] + PG - 1) // PG * PG
            t = Tile(P, name, shape, dt, off=off)
            cur[0] = off + t.nbytes
        else:
            t = Tile(P, name, shape, dt, off=at)
        return t

    class Region:
        def __init__(self, base, size):
            self.base, self.size, self.cur = base, size, base

        def reset(self):
            self.cur = self.base

        def alloc(self, name, shape, dt):
            off = (self.cur + PG - 1) // PG * PG
            t = Tile(P, name, shape, dt, off=off)
            self.cur = off + t.nbytes
            assert self.cur <= self.base + self.size, f"region overflow {name} {self.cur - self.base} > {self.size}"
            return t

    ident_f = alloc("ident_f", [128, 128], F32)
    ident_b = alloc("ident_b", [128, 128], BF16)
    ones_b = alloc("ones_b", [128, 128], BF16)
    ones_f = alloc("ones_f", [128, 128], F32)
    tri = alloc("tri", [128, 128], F32)
    negmask = alloc("negmask", [128, 128], F32)
    sel_last = alloc("sel_last", [128, 128], F32)
    sel_c0 = alloc("sel_c0", [128, 128], F32)
    sel_c1 = alloc("sel_c1", [128, 128], F32)
    rrot_b = alloc("rrot_b", [128, 128], BF16)
    invf = alloc("invf", [128, 1], F32)
    nmw = alloc("nmw", [128, 16], F32)
    nlw = alloc("nlw", [128, 16], F32)
    npw = alloc("npw", [128, 16], F32)
    qaw = alloc("qaw", [128, 4], F32)
    kvaw = alloc("kvaw", [128, 4], F32)
    qnw = alloc("qnw", [128, 2], F32)
    knw = alloc("knw", [128, 2], F32)
    convw = alloc("convw", [128, 4, 48], F32)
    convb = alloc("convb", [128, 48], F32)
    ssmw = alloc("ssmw", [128, 32], F32)
    dtb = alloc("dtb", [128, 1], F32)
    A_bc = alloc("A_bc", [128, 64], F32)
    D_bc = alloc("D_bc", [128, 64], F32)
    X = alloc("X", [128, 16, TB], F32)
    hS = alloc("hS", [128, 8, 512], F32)
    wbuf = [alloc(f"wbuf{i}", [128, 16 * 512], BF16) for i in range(2)]
    cosb = alloc("cosb", [128, TB], F32)
    sinb = alloc("sinb", [128, TB], F32)
    dtraw = alloc("dtraw", [128, TB], F32)
    dtvT = alloc("dtvT", [128, TB], F32)
    gt = alloc("gt", [128, TB], F32)
    rstd = alloc("rstd", [128, TB], F32)
    tmpA = alloc("tmpA", [128, TB], F32)
    tmpB = alloc("tmpB", [128, TB], F32)
    tmpC = alloc("tmpC", [128, TB], F32)
    base1 = (cur[0] + PG - 1) // PG * PG
    R1 = Region(base1, 32 * 1024)
    R2 = Region(base1 + 32 * 1024, P.sb_limit - (base1 + 32 * 1024))

    ps = [Tile(P, f"psb{i}", [128, 512], F32, space="psum") for i in range(8)]
    pmm_rr = [0]

    def next_pmm():
        b = ps[pmm_rr[0] % 4]
        pmm_rr[0] += 1
        return b

    tmp_rr = [0]

    def next_tmp():
        t = (tmpA, tmpB, tmpC)[tmp_rr[0] % 3]
        tmp_rr[0] += 1
        return t

    def mm(out, lhsT, rhs, start=True, stop=True):
        oa, la, ra = out.ap, lhsT.ap, rhs.ap
        return P.op("tensor", lambda e: e.matmul(oa, la, ra, start=start, stop=stop), [lhsT, rhs], [out])

    def tr(out, in_, ident):
        oa, ia, da = out.ap, in_.ap, ident.ap
        return P.op("tensor", lambda e: e.transpose(oa, ia, da), [in_, ident], [out])

    def act(out, in_, func, bias=None, scale=1.0, eng="scalar"):
        oa, ia = out.ap, in_.ap
        rd = [in_]
        kw = {}
        if isinstance(bias, V):
            rd.append(bias)
            kw["bias"] = bias.ap
        elif bias is not None:
            kw["bias"] = float(bias)
        return P.op("scalar", lambda e: e.activation(out=oa, in_=ia, func=func, scale=scale, **kw), rd, [out])

    def vcopy(out, in_, eng="vector"):
        oa, ia = out.ap, in_.ap
        return P.op(eng, lambda e: e.tensor_copy(oa, ia), [in_], [out])

    def tt(out, a, b, op, eng="vector"):
        oa, aa, ba = out.ap, a.ap, b.ap
        return P.op(eng, lambda e: e.tensor_tensor(oa, aa, ba, op), [a, b], [out])

    def ts(out, a, s1, s2, op0, op1=None, eng="vector"):
        oa, aa = out.ap, a.ap
        rd = [a]
        s1a = s1
        if isinstance(s1, V):
            rd.append(s1)
            s1a = s1.ap
        s2a = s2
        if isinstance(s2, V):
            rd.append(s2)
            s2a = s2.ap
        if op1 is None:
            return P.op(eng, lambda e: e.tensor_scalar(oa, aa, s1a, s2a, op0), rd, [out])
        return P.op(eng, lambda e: e.tensor_scalar(oa, aa, s1a, s2a, op0, op1), rd, [out])

    def stt(out, a, sc, b, op0, op1, eng="vector"):
        oa, aa, ba = out.ap, a.ap, b.ap
        rd = [a, b]
        sa = sc
        if isinstance(sc, V):
            rd.append(sc)
            sa = sc.ap
        return P.op(eng, lambda e: e.scalar_tensor_tensor(oa, aa, sa, ba, op0, op1), rd, [out])

    def memset(out, val, eng="vector"):
        oa = out.ap
        return P.op(eng, lambda e: e.memset(oa, val), [], [out])

    def recip(out, in_):
        oa, ia = out.ap, in_.ap
        return P.op("vector", lambda e: e.reciprocal(oa, ia), [in_], [out])

    def dma(out, in_, eng="sync"):
        return P.dma(eng, out, in_)

    def bc_mid(v, tile, n):
        ap = v.ap
        return V(ap.unsqueeze(1).to_broadcast([ap.shape[0], n, ap.shape[1]]), v.pages)

    def bc_last(v, n):
        ap = v.ap
        return V(ap.unsqueeze(2).to_broadcast([ap.shape[0], ap.shape[1], n]), v.pages)

    def setup_consts():
        io = R2.alloc("c_io", [128, 128], F32)
        ip = R2.alloc("c_ip", [128, 1], F32)
        jh = R2.alloc("c_jh", [128, 128], F32)
        ph = R2.alloc("c_ph", [128, 1], F32)
        t1 = R2.alloc("c_t1", [128, 128], F32)
        t2 = R2.alloc("c_t2", [128, 128], F32)
        P.op("gpsimd", lambda e: e.iota(io.t[:], pattern=[[1, 128]], base=0, channel_multiplier=0,
                                        allow_small_or_imprecise_dtypes=True), [], [io.all()])
        P.op("gpsimd", lambda e: e.iota(ip.t[:], pattern=[[0, 1]], base=0, channel_multiplier=1,
                                        allow_small_or_imprecise_dtypes=True), [], [ip.all()])
        ts(ident_f.all(), io.all(), ip.all(), None, ALU.is_equal)
        vcopy(ident_b.all(), ident_f.all())
        memset(ones_b.all(), 1.0)
        memset(ones_f.all(), 1.0)
        ts(jh.all(), io.all(), 64.0, None, ALU.is_ge)
        ts(ph.all(), ip.all(), 64.0, None, ALU.is_ge)
        ts(t1.all(), io.all(), ip.all(), None, ALU.is_ge)
        ts(t2.all(), jh.all(), ph.all(), None, ALU.is_equal)
        tt(tri.all(), t1.all(), t2.all(), ALU.mult)
        ts(negmask.all(), tri.all(), -1.0, 30000.0, ALU.add, ALU.mult)
        ts(t1.all(), jh.all(), 64.0, 63.0, ALU.mult, ALU.add)
        ts(sel_last.all(), t1.all(), ip.all(), None, ALU.is_equal)
        memset(t1.all(), 63.0)
        ts(sel_c0.all(), t1.all(), ip.all(), None, ALU.is_equal)
        memset(t1.all(), 127.0)
        ts(sel_c1.all(), t1.all(), ip.all(), None, ALU.is_equal)
        ts(t1.all(), io.all(), -32.0, None, ALU.add)
        ts(t1.all(), t1.all(), ip.all(), None, ALU.is_equal)
        ts(t2.all(), io.all(), 32.0, None, ALU.add)
        ts(t2.all(), t2.all(), ip.all(), None, ALU.is_equal)
        tt(t1.all(), t1.all(), t2.all(), ALU.subtract)
        vcopy(rrot_b.all(), t1.all())
        memset(dtraw.all(), 0.0)
        memset(hS.all(), 0.0)
        st = R2.alloc("c_st", [128, 128], F32)
        memset(st.all(), 0.0)
        dma(st.s(0, 128, 0, 1), invf_d.v())
        tr(ps[6].s(0, 128), st.all(), ident_f.all())
        vcopy(invf.all(), ps[6].s(0, 1))
        memset(tmpA.all(), 0.0)
        for j in range(CONV_DIM // 128):
            dma(xbc_d.v(xbc_d.ap[j * 128:(j + 1) * 128, 0:3]), tmpA.s(0, 3))
        R2.reset()
        TWO_PI = 2.0 * math.pi
        C1 = 6.28125
        C2 = TWO_PI - C1
        MAGIC = 12582912.0
        for c0 in range(0, S, 2048):
            n = min(2048, S - c0)
            pi_ = R2.alloc("r_pi", [64, 2048], I32)
            ang = R2.alloc("r_ang", [64, 2048], F32)
            nn = R2.alloc("r_n", [64, 2048], F32)
            rr = R2.alloc("r_r", [64, 2048], F32)
            dma(pi_.s(0, n), pos_d.v(pos_d.ap[0, c0:c0 + n].partition_broadcast(64)))
            vcopy(ang.s(0, n), pi_.s(0, n))
            ts(ang.s(0, n), ang.s(0, n), invf.s(0, 1, 0, 64), None, ALU.mult)
            ts(nn.s(0, n), ang.s(0, n), 1.0 / TWO_PI, MAGIC, ALU.mult, ALU.add)
            ts(nn.s(0, n), nn.s(0, n), -MAGIC, None, ALU.add)
            stt(rr.s(0, n), nn.s(0, n), -C1, ang.s(0, n), ALU.mult, ALU.add)
            stt(rr.s(0, n), nn.s(0, n), -C2, rr.s(0, n), ALU.mult, ALU.add)
            ts(rr.s(0, n), rr.s(0, n), 3.1415925, -3.1415925, ALU.min, ALU.max)
            act(ang.s(0, n), rr.s(0, n), AF.Sin)
            dma(sin_d.v(sin_d.ap[:, c0:c0 + n]), ang.s(0, n))
            ts(nn.s(0, n), rr.s(0, n), -1.0, None, ALU.mult)
            tt(nn.s(0, n), nn.s(0, n), rr.s(0, n), ALU.max)
            act(ang.s(0, n), nn.s(0, n), AF.Sin, bias=math.pi / 2, scale=-1.0)
            dma(cos_d.v(cos_d.ap[:, c0:c0 + n]), ang.s(0, n))
            R2.reset()

    def load_cols(dst_v, src_ap_rows, nrows, ncols=128):
        st = R2.alloc("lc_st", [128, 128], F32)
        memset(st.all(), 0.0)
        dma(st.s(0, ncols, 0, nrows), Dram(src_ap_rows).v())
        tr(ps[6].s(0, 128), st.all(), ident_f.all())
        vcopy(dst_v, ps[6].s(0, nrows))
        R2.cur -= 0

    def layer_consts(l):
        R2.reset()
        for (t, nm, n) in ((nmw, "norm_mix_w", 16), (nlw, "norm_mlp_w", 16), (npw, "ple_norm_w", 16),
                           (qaw, "q_a_norm_w", 4), (kvaw, "kv_a_norm_w", 4), (ssmw, "ssm_norm_w", 32),
                           (convb, "conv_b", 48)):
            load_cols(t.all(), W[nm].ap[l].rearrange("(r c) -> r c", c=128), n)
            R2.reset()
        for tap in range(4):
            load_cols(convw.c(tap), W["conv_w"].ap[l, tap].rearrange("(r c) -> r c", c=128), 48)
            R2.reset()
        for (t, nm) in ((qnw, "q_norm_w"), (knw, "k_norm_w")):
            load_cols(t.s(0, 1), W[nm].ap[l, 0:128].rearrange("(r c) -> r c", c=128), 1)
            R2.reset()
            load_cols(t.s(1, 2), W[nm].ap[l, 128:192].rearrange("(r c) -> r c", c=64), 1, ncols=64)
            R2.reset()
        load_cols(dtb.all(), W["dt_bias"].ap[l].rearrange("(r c) -> r c", c=64), 1, ncols=64)
        R2.reset()
        dma(A_bc.all(), W["a_log"].v(W["a_log"].ap[l].partition_broadcast(128)))
        act(A_bc.all(), A_bc.all(), AF.Exp)
        ts(A_bc.all(), A_bc.all(), -1.0, None, ALU.mult)
        dma(D_bc.all(), W["d_skip"].v(W["d_skip"].ap[l].partition_broadcast(128)))

    wb_rr = [0]

    def dense(Wd, l, row0, KC, in_chunk, segs, consumer, gcols=None):
        if gcols is None:
            gcols = min(512, 8192 // KC)
        groups = []
        for sg in segs:
            c0, n, tag = sg
            if groups and groups[-1][0] + groups[-1][1] == c0 and (groups[-1][1] + n) <= gcols:
                groups[-1][1] += n
                groups[-1][2].append(sg)
            else:
                groups.append([c0, n, [sg]])
        for (g0, gw, sgs) in groups:
            wb = wbuf[wb_rr[0] % 2]
            wb_rr[0] += 1
            src = Wd.ap[l, row0:row0 + KC * 128, g0:g0 + gw].rearrange("(kc p) f -> p kc f", p=128)
            dst_ap = wb.t[:, 0:KC * gw].rearrange("p (kc f) -> p kc f", f=gw)
            dstv = wb.v(dst_ap, 0, KC * gw)
            P.dma("gpsimd", dstv, Wd.v(src))
            for (c0, n, tag) in sgs:
                pb = next_pmm()
                o = c0 - g0
                for kc in range(KC):
                    lw = wb.v(wb.t[:, kc * gw + o: kc * gw + o + n], kc * gw + o, kc * gw + o + n)
                    mm(pb.s(0, TB, 0, n), lw, in_chunk(kc), start=(kc == 0), stop=(kc == KC - 1))
                consumer(tag, pb)

    def rmsnorm_big(src, wcols, dst, nch, D):
        sqt = R2sq[0]
        for kc in range(nch):
            act(sqt.c(kc), src.c(kc), AF.Square)
        for kc in range(nch):
            mm(ps[7].all(), ones_b.all(), sqt.c(kc), start=(kc == 0), stop=(kc == nch - 1))
        act(rstd.all(), ps[7].all(), AF.Sqrt, bias=EPS, scale=1.0 / D)
        recip(rstd.all(), rstd.all())
        for kc in range(nch):
            stt(dst.c(kc), src.c(kc), wcols.s(kc, kc + 1), rstd.all(), ALU.mult, ALU.mult)

    R2sq = [None]
    fin_ops = []

    def rope(dst_bf, src_f32, tmp_b, tmp_f):
        vcopy(tmp_b, src_f32)
        mm(ps[6].s(0, TB, 0, 64), rrot_b.s(0, 64, 0, 64), tmp_b)
        tt(tmp_f, ps[6].s(0, TB, 0, 64), sinb.s(0, TB, 0, 64), ALU.mult)
        tt(src_f32, src_f32, cosb.s(0, TB, 0, 64), ALU.mult)
        tt(dst_bf, src_f32, tmp_f, ALU.add)

    def block(l, tb):
        t0 = tb * TB
        last = (l == L - 1)
        R1.reset()
        R2.reset()
        if l == 0:
            stg = R2.alloc("xs", [128, D_MODEL], F32)
            for j in range(4):
                dma(stg.all(), x_d.v(x_d.ap[t0 + j * 128: t0 + (j + 1) * 128, :]))
                for q in range(4):
                    pb = next_pmm()
                    for i in range(4):
                        kc = q * 4 + i
                        tr(pb.s(i * 128, (i + 1) * 128), stg.s(kc * 128, (kc + 1) * 128), ident_f.all())
                    dst = X.v(X.t[:, q * 4:(q + 1) * 4, j * 128:(j + 1) * 128], q * 4 * TB, (q + 1) * 4 * TB)
                    src = pb.v(pb.t[:, :].rearrange("p (c t) -> p c t", t=128))
                    vcopy(dst, src, eng="scalar" if False else "vector")
            R2.reset()
        else:
            dma(X.all(), xT_d.v(xT_d.ap[:, t0:t0 + TB].rearrange("(kc p) t -> p kc t", p=128)))
        dma(cosb.s(0, TB, 0, 64), cos_d.v(cos_d.ap[:, t0:t0 + TB]))
        dma(sinb.s(0, TB, 0, 64), sin_d.v(sin_d.ap[:, t0:t0 + TB]))

        if STOP == 'load':
            return
        sqt = R1.alloc("sq", [128, 16, TB], BF16)
        hT = R1.alloc("hT", [128, 16, TB], BF16)
        R2sq[0] = sqt
        cq = R2.alloc("cq", [128, 4, TB], F32)
        ckv = R2.alloc("ckv", [128, 4, TB], F32)
        kr = R2.alloc("kr", [128, TB], F32)
        rmsnorm_big(X, nmw, hT, 16, D_MODEL)
        segs = []
        for j in range(4):
            segs.append((j * 128, 128, ("cq", j)))
        for j in range(4):
            segs.append((512 + j * 128, 128, ("ckv", j)))
        segs.append((1024, 64, ("kr", 0)))
        for j in range(32):
            segs.append((1088 + j * 128, 128, ("z", j)))
        for j in range(48):
            segs.append((5184 + j * 128, 128, ("xbc", j)))
        segs.append((11328, 64, ("dt", 0)))
        for j in range(32):
            segs.append((11392 + j * 128, 128, ("g", j)))

        def cons_in(tag, pb):
            kind, j = tag
            if kind == "cq":
                act(cq.c(j), pb.all(), AF.Copy)
            elif kind == "ckv":
                act(ckv.c(j), pb.all(), AF.Copy)
            elif kind == "kr":
                act(kr.s(0, TB, 0, 64), pb.s(0, TB, 0, 64), AF.Copy)
            elif kind == "dt":
                vcopy(dtraw.s(0, TB, 0, 64), pb.s(0, TB, 0, 64))
            elif kind == "z":
                t = next_tmp()
                act(t.all(), pb.all(), AF.Silu)
                dma(zs_d.v(zs_d.ap[j * 128:(j + 1) * 128, :]), t.all())
            elif kind == "xbc":
                t = next_tmp()
                vcopy(t.all(), pb.all())
                dma(xbc_d.v(xbc_d.ap[j * 128:(j + 1) * 128, 3 + t0:3 + t0 + TB]), t.all())
            elif kind == "g":
                t = next_tmp()
                act(t.all(), pb.all(), AF.Sigmoid)
                dma(g_d.v(g_d.ap[j * 128:(j + 1) * 128, :]), t.all())

        dense(W["w_in"], l, 0, 16, lambda kc: hT.c(kc), segs, cons_in)

        if STOP == 'A':
            return
        R1.reset()
        QnT = R1.alloc("QnT", [128, 16, TB], BF16)
        QrT = R1.alloc("QrT", [128, 16, TB], BF16)
        cqn = R2.alloc("cqn", [128, 4, TB], BF16)
        ckvn = R2.alloc("ckvn", [128, 4, TB], BF16)
        sq4 = R2.alloc("sq4", [128, 4, TB], BF16)
        R2sq[0] = sq4
        hf = [R2.alloc(f"hf{i}", [128, TB], F32) for i in range(2)]
        hr = R2.alloc("hr", [128, TB], F32)
        vf = R2.alloc("vf", [128, TB], F32)
        tb16 = R2.alloc("tb16", [128, TB], BF16)
        tf32 = R2.alloc("tf32", [128, TB], F32)
        kob = R2.alloc("kob", [128, TB], BF16)
        krb = R2.alloc("krb", [128, TB], BF16)
        vtm = R2.alloc("vtm", [128, 4, 128], BF16)
        kr2 = R2.alloc("kr2", [128, TB], BF16)

        rmsnorm_big(cq, qaw, cqn, 4, Q_LORA)
        hf_rr = [0]
        cur_hf = [None]

        def head_stats(nope_v, rope_sq_v, rope_src_v):
            sq = next_tmp()
            sqb = V(sq.t[:, :].bitcast(BF16)[:, 0:TB], sq._pages(0, TB * 2))
            act(sqb, nope_v, AF.Square)
            mm(ps[7].all(), ones_b.all(), sqb, start=True, stop=False)
            if rope_sq_v is None:
                sq2 = V(sq.t[0:64, :].bitcast(BF16)[:, TB:2 * TB], sq._pages(TB * 2, TB * 4))
                act(sq2, rope_src_v, AF.Square)
                rope_sq_v = sq2
            mm(ps[7].all(), ones_b.s(0, 128, 0, 64), rope_sq_v, start=False, stop=True)
            act(rstd.all(), ps[7].all(), AF.Sqrt, bias=EPS, scale=1.0 / QK_DIM)
            recip(rstd.all(), rstd.all())

        def cons_q(tag, pb):
            kind, h = tag
            if kind == "qn":
                cur_hf[0] = hf[hf_rr[0] % 2]
                hf_rr[0] += 1
                act(cur_hf[0].all(), pb.all(), AF.Copy)
            else:
                act(hr.s(0, TB, 0, 64), pb.s(0, TB, 0, 64), AF.Copy)
                head_stats(cur_hf[0].all(), None, hr.s(0, TB, 0, 64))
                stt(QnT.c(h), cur_hf[0].all(), qnw.s(0, 1), rstd.all(), ALU.mult, ALU.mult)
                stt(hr.s(0, TB, 0, 64), hr.s(0, TB, 0, 64), qnw.s(1, 2, 0, 64), rstd.s(0, TB, 0, 64), ALU.mult, ALU.mult)
                rope(QrT.c(h, None, None, 0, 64), hr.s(0, TB, 0, 64), tb16.s(0, TB, 0, 64), tf32.s(0, TB, 0, 64))

        segs = []
        for h in range(HEADS):
            segs.append((h * 192, 128, ("qn", h)))
            segs.append((h * 192 + 128, 64, ("qr", h)))
        dense(W["w_uq"], l, 0, 4, lambda kc: cqn.c(kc), segs, cons_q)

        rmsnorm_big(ckv, kvaw, ckvn, 4, KV_LORA)
        act(kr2.s(0, TB, 0, 64), kr.s(0, TB, 0, 64), AF.Square)

        def cons_kv(tag, pb):
            kind, h = tag
            if kind == "kn":
                cur_hf[0] = hf[hf_rr[0] % 2]
                hf_rr[0] += 1
                act(cur_hf[0].all(), pb.all(), AF.Copy)
                head_stats(cur_hf[0].all(), kr2.s(0, TB, 0, 64), None)
                stt(kob.all(), cur_hf[0].all(), knw.s(0, 1), rstd.all(), ALU.mult, ALU.mult)
                dma(knT_d.v(knT_d.ap[h, :, t0:t0 + TB]), kob.all())
                stt(hr.s(0, TB, 0, 64), kr.s(0, TB, 0, 64), knw.s(1, 2, 0, 64), rstd.s(0, TB, 0, 64), ALU.mult, ALU.mult)
                rope(krb.s(0, TB, 0, 64), hr.s(0, TB, 0, 64), tb16.s(0, TB, 0, 64), tf32.s(0, TB, 0, 64))
                dma(krT_d.v(krT_d.ap[h, :, t0:t0 + TB]), krb.s(0, TB, 0, 64))
            else:
                act(vf.all(), pb.all(), AF.Copy)
                pt_ = ps[6]
                for j in range(4):
                    tr(pt_.s(j * 128, (j + 1) * 128), vf.s(j * 128, (j + 1) * 128), ident_f.all())
                vcopy(vtm.all(), pt_.v(pt_.t[:, :].rearrange("p (j v) -> p j v", v=128)))
                dst = v_d.ap[t0:t0 + TB, h * 128:(h + 1) * 128].rearrange("(j p) v -> p j v", p=128)
                dma(v_d.v(dst), vtm.all())

        segs = []
        for h in range(HEADS):
            segs.append((h * 256, 128, ("kn", h)))
            segs.append((h * 256 + 128, 128, ("v", h)))
        dense(W["w_ukv"], l, 0, 4, lambda kc: ckvn.c(kc), segs, cons_kv)

        if STOP == 'A2':
            return
        R2.reset()
        nk = (tb + 1) * TB
        ntile = nk // 128
        Kn = R2.alloc("Kn", [128, S], BF16)
        Kr = R2.alloc("Kr", [128, S], BF16)
        Vh = R2.alloc("Vh", [128, S // 128, 128], BF16)
        pts = [R2.alloc(f"pt{i}", [128, TB], BF16) for i in range(3)]
        oTt = R2.alloc("oT", [128, 16, TB], BF16)
        rden = R2.alloc("rden", [128, TB], F32)
        O_ps, D_ps = ps[4], ps[5]
        for h in range(HEADS):
            dma(Kn.s(0, nk), knT_d.v(knT_d.ap[h, :, 0:nk]))
            dma(Kr.s(0, nk, 0, 64), krT_d.v(krT_d.ap[h, :, 0:nk]))
            dma(Vh.v(Vh.t[:, 0:ntile, :], 0, ntile * 128),
                v_d.v(v_d.ap[0:nk, h * 128:(h + 1) * 128].rearrange("(kt p) v -> p kt v", p=128)))
            for kt in range(ntile):
                j = kt - 4 * tb
                c0 = 128 * j if j >= 0 else 0
                sp = next_pmm()
                mm(sp.s(c0, TB), Kn.s(kt * 128, (kt + 1) * 128), QnT.c(h, c0, TB), start=True, stop=False)
                mm(sp.s(c0, TB), Kr.s(kt * 128, (kt + 1) * 128, 0, 64), QrT.c(h, c0, TB, 0, 64), start=False, stop=True)
                pt = pts[kt % 3]
                act(pt.s(c0, TB), sp.s(c0, TB), AF.Exp, bias=-SM_SHIFT, scale=SM_SCALE)
                if j >= 0:
                    memset(pt.s(c0, c0 + 64, 64, 128), 0.0)
                mm(O_ps.s(c0, TB), Vh.c(kt), pt.s(c0, TB), start=(kt == 0), stop=(kt == ntile - 1))
                mm(D_ps.s(c0, TB), ones_b.all(), pt.s(c0, TB), start=(kt == 0), stop=(kt == ntile - 1))
            recip(rden.all(), D_ps.all())
            tt(oTt.c(h), O_ps.all(), rden.all(), ALU.mult)
        dma(oT_d.v(oT_d.ap.rearrange("(h p) t -> p h t", p=128)), oTt.all())

        if STOP == 'B':
            return
        R1.reset()
        R2.reset()
        ybT = R1.alloc("ybT", [128, 32, TB], BF16)
        class _Sm:
            def __init__(self, t):
                self.t = t

            def __getitem__(self, j):
                t = self.t

                class _J:
                    def all(self_):
                        return t.c(j)

                    def s(self_, a, b):
                        return t.c(j, a, b)
                return _J()
        sm = {nm: _Sm(R2.alloc(nm, [128, 4, 64], F32)) for nm in ("dt_tm", "acum", "wdec", "ea", "cd0", "cd1")}
        xg = R2.alloc("xg", [128, 4, TB], F32)
        win = R2.alloc("win", [128, TB + 3], F32)
        acc = R2.alloc("acc", [128, TB], F32)
        Bgf = R2.alloc("Bgf", [128, TB], F32)
        BgT = R2.alloc("BgT", [128, TB], BF16)
        CgT = R2.alloc("CgT", [128, TB], BF16)
        xtm = R2.alloc("xtm", [128, TB], BF16)
        skipt = R2.alloc("skipt", [128, TB], F32)
        Blo = R2.alloc("Blo", [128, 128], BF16)
        Bhi = R2.alloc("Bhi", [128, 128], BF16)
        Clo = R2.alloc("Clo", [128, 128], BF16)
        Chi = R2.alloc("Chi", [128, 128], BF16)
        cbs = R2.alloc("cbs", [128, 128], F32)
        rdg = R2.alloc("rdg", [128, 8, 128], F32)
        Dm = R2.alloc("Dm", [128, 8, 128], F32)
        mT = R2.alloc("mT", [128, 8, 128], BF16)
        xw = R2.alloc("xw", [128, TB], BF16)
        hb0 = R2.alloc("hb0", [128, TB], BF16)
        hb1 = R2.alloc("hb1", [128, TB], BF16)
        ysb = R2.alloc("ysb", [128, TB], F32)
        yTg = R2.alloc("yTg", [128, 4, TB], F32)
        sqg = R2.alloc("sqg", [128, 4, TB], BF16)
        zt = R2.alloc("zt", [128, TB], F32)

        act(dtvT.all(), dtraw.all(), AF.Exp, bias=dtb.all())
        act(dtvT.all(), dtvT.all(), AF.Ln, bias=1.0)
        if STOP == 'D0a':
            return
        for j in range(4):
            tr(ps[6].s(0, 128), dtvT.s(j * 128, (j + 1) * 128), ident_f.all())
            vcopy(sm["dt_tm"][j].all(), ps[6].s(0, 64))
            if STOP == 'D0b':
                return
            a_t = next_tmp()
            tt(a_t.s(0, 64), sm["dt_tm"][j].all(), A_bc.all(), ALU.mult)
            mm(ps[7].s(0, 64), tri.all(), a_t.s(0, 64))
            if STOP == 'D0c':
                return
            vcopy(sm["acum"][j].all(), ps[7].s(0, 64))
            act(sm["ea"][j].all(), ps[7].s(0, 64), AF.Exp)
            if STOP == 'D0d':
                return
            mm(ps[6].s(0, 64), sel_last.all(), sm["acum"][j].all())
            tt(sm["wdec"][j].all(), ps[6].s(0, 64), sm["acum"][j].all(), ALU.subtract)
            act(sm["wdec"][j].all(), sm["wdec"][j].all(), AF.Exp)
            tt(sm["wdec"][j].all(), sm["wdec"][j].all(), sm["dt_tm"][j].all(), ALU.mult)
            mm(ps[7].s(0, 64), sel_c0.all(), sm["acum"][j].all())
            act(sm["cd0"][j].all(), ps[7].s(0, 64), AF.Exp)
            mm(ps[6].s(0, 64), sel_c1.all(), sm["acum"][j].all())
            act(sm["cd1"][j].all(), ps[6].s(0, 64), AF.Exp)

        if STOP == 'D0':
            return
        def conv_chunk(ci, out_v):
            dma(win.all(), xbc_d.v(xbc_d.ap[ci * 128:(ci + 1) * 128, t0:t0 + TB + 3]))
            ts(acc.all(), win.s(0, TB), convw.c(0, ci, ci + 1), None, ALU.mult)
            for tap in range(1, 4):
                stt(acc.all(), win.s(tap, tap + TB), convw.c(tap, ci, ci + 1), acc.all(), ALU.mult, ALU.add)
            act(out_v, acc.all(), AF.Silu, bias=convb.s(ci, ci + 1))

        for g in range(SSM_GROUPS):
            g8 = g * 8
            for c in range(4):
                conv_chunk(g * 4 + c, xg.c(c))
            conv_chunk(32 + g, Bgf.all())
            vcopy(BgT.all(), Bgf.all())
            conv_chunk(40 + g, CgT.all())
            if STOP == 'D1':
                return
            hs_g = hS.c(g)
            for j in range(4):
                js = slice(j * 128, (j + 1) * 128)
                a_cum, dtt, wd, eav, c0v, c1v = (sm[n][j] for n in ("acum", "dt_tm", "wdec", "ea", "cd0", "cd1"))
                for c in range(4):
                    tr(ps[0].s(c * 128, (c + 1) * 128), xg.c(c, j * 128, (j + 1) * 128), ident_f.all())
                vcopy(xtm.all(), ps[0].all())
                x3 = ps[0].v(ps[0].t[:, :].rearrange("p (r q) -> p r q", q=64))
                tt(skipt.v(skipt.t[:, :].rearrange("p (r q) -> p r q", q=64)), x3,
                   bc_last(D_bc.s(g8, g8 + 8), 64), ALU.mult)
                tt(xw.v(xw.t[:, :].rearrange("p (r q) -> p r q", q=64)), x3,
                   bc_last(wd.s(g8, g8 + 8), 64), ALU.mult)
                if STOP == 'D2':
                    return
                tr(ps[1].s(0, 128), Bgf.s(j * 128, (j + 1) * 128), ident_f.all())
                memset(Blo.all(), 0.0, eng="gpsimd")
                memset(Bhi.all(), 0.0, eng="gpsimd")
                vcopy(Blo.s(0, 128, 0, 64), ps[1].s(0, 128, 0, 64))
                vcopy(Bhi.s(0, 128, 64, 128), ps[1].s(0, 128, 64, 128))
                memset(Clo.all(), 0.0, eng="gpsimd")
                memset(Chi.all(), 0.0, eng="gpsimd")
                vcopy(Clo.s(0, 64), CgT.s(j * 128, j * 128 + 64), eng="gpsimd")
                vcopy(Chi.s(64, 128), CgT.s(j * 128 + 64, (j + 1) * 128), eng="gpsimd")
                if STOP == 'D3':
                    return
                mm(ps[2].s(0, 128), BgT.s(j * 128, (j + 1) * 128), CgT.s(j * 128, (j + 1) * 128))
                vcopy(cbs.all(), ps[2].s(0, 128))
                tt(rdg.all(), bc_mid(ident_f.all(), ident_f, 8), bc_last(a_cum.s(g8, g8 + 8), 128), ALU.mult)
                for q in range(2):
                    rb = ps[3] if q == 0 else ps[6]
                    mm(rb.all(), ones_f.all(), rdg.v(rdg.t[:, q * 4:(q + 1) * 4, :], q * 512, (q + 1) * 512))
                    tt(Dm.v(Dm.t[:, q * 4:(q + 1) * 4, :], q * 512, (q + 1) * 512),
                       rb.v(rb.t[:, :].rearrange("p (h t) -> p h t", t=128)),
                       bc_last(a_cum.s(g8 + q * 4, g8 + q * 4 + 4), 128), ALU.subtract)
                tt(Dm.all(), Dm.all(), bc_mid(negmask.all(), negmask, 8), ALU.add, eng="gpsimd")
                act(Dm.all(), Dm.all(), AF.Exp)
                tt(Dm.all(), Dm.all(), bc_last(dtt.s(g8, g8 + 8), 128), ALU.mult)
                tt(mT.all(), Dm.all(), bc_mid(cbs.all(), cbs, 8), ALU.mult, eng="gpsimd")
                if STOP == 'D4':
                    return
                for r in range(8):
                    mm(ps[4].s(r * 64, (r + 1) * 64), mT.c(r), xtm.s(r * 64, (r + 1) * 64))
                mm(ps[7].all(), Blo.all(), xw.all())
                mm(ps[1].all(), Bhi.all(), xw.all())
                vcopy(hb0.all(), hs_g)
                h3 = V(hs_g.ap.rearrange("p (r q) -> p r q", q=64), hs_g.pages)
                tt(h3, h3, bc_last(c0v.s(g8, g8 + 8), 64), ALU.mult)
                tt(hs_g, hs_g, ps[7].all(), ALU.add)
                vcopy(hb1.all(), hs_g)
                tt(h3, h3, bc_last(c1v.s(g8, g8 + 8), 64), ALU.mult)
                tt(hs_g, hs_g, ps[1].all(), ALU.add)
                if STOP == 'D5':
                    return
                mm(ps[5].all(), Clo.all(), hb0.all(), start=True, stop=False)
                mm(ps[5].all(), Chi.all(), hb1.all(), start=False, stop=True)
                y3 = ysb.v(ysb.t[:, :].rearrange("p (r q) -> p r q", q=64))
                tt(y3, ps[5].v(ps[5].t[:, :].rearrange("p (r q) -> p r q", q=64)), bc_last(eav.s(g8, g8 + 8), 64), ALU.mult)
                tt(ysb.all(), ysb.all(), ps[4].all(), ALU.add)
                tt(ysb.all(), ysb.all(), skipt.all(), ALU.add, eng="gpsimd")
                if STOP == 'D6':
                    return
                for c in range(4):
                    tr(ps[2].s(c * 128, (c + 1) * 128), ysb.s(c * 128, (c + 1) * 128), ident_f.all())
                act(yTg.v(yTg.t[:, :, j * 128:(j + 1) * 128]), ps[2].v(ps[2].t[:, :].rearrange("p (c t) -> p c t", t=128)), AF.Copy)
            if STOP == 'D7':
                return
            for c in range(4):
                ci = g * 4 + c
                dma(zt.all(), zs_d.v(zs_d.ap[ci * 128:(ci + 1) * 128, :]))
                tt(yTg.c(c), yTg.c(c), zt.all(), ALU.mult)
                act(sqg.c(c), yTg.c(c), AF.Square)
            for c in range(4):
                mm(ps[7].all(), ones_b.all(), sqg.c(c), start=(c == 0), stop=(c == 3))
            act(rstd.all(), ps[7].all(), AF.Sqrt, bias=EPS, scale=1.0 / 512)
            recip(rstd.all(), rstd.all())
            for c in range(4):
                ci = g * 4 + c
                stt(ybT.c(ci), yTg.c(c), ssmw.s(ci, ci + 1), rstd.all(), ALU.mult, ALU.mult)

        if STOP == 'D':
            return
        R2.reset()
        oTt = R2.alloc("oT2", [128, 16, TB], BF16)
        mga = R2.alloc("mga", [128, 16, TB], BF16)
        mgb = R2.alloc("mgb", [128, 16, TB], BF16)
        dma(oTt.all(), oT_d.v(oT_d.ap.rearrange("(h p) t -> p h t", p=128)))

        def cons_ya(tag, pb):
            c = tag
            dma(gt.all(), g_d.v(g_d.ap[c * 128:(c + 1) * 128, :]))
            tt(mga.c(c), pb.all(), gt.all(), ALU.mult)

        dense(W["w_o_mla"], l, 0, 16, lambda kc: oTt.c(kc), [(c * 128, 128, c) for c in range(16)], cons_ya)

        def cons_yb(tag, pb):
            c = tag
            dma(gt.all(), g_d.v(g_d.ap[(16 + c) * 128:(17 + c) * 128, :]))
            t = next_tmp()
            tt(t.all(), pb.all(), gt.all(), ALU.mult)
            tt(mgb.c(c), t.all(), mga.c(c), ALU.add)

        dense(W["w_o_ssm"], l, 0, 32, lambda kc: ybT.c(kc), [(c * 128, 128, c) for c in range(16)], cons_yb)

        def cons_res(tag, pb):
            tt(X.c(tag), X.c(tag), pb.all(), ALU.add)

        dense(W["w_out"], l, 0, 16, lambda kc: mgb.c(kc), [(c * 128, 128, c) for c in range(16)], cons_res)

        if STOP == 'E':
            return
        R1.reset()
        R2.reset()
        uT = R1.alloc("uT", [128, 32, TB], BF16)
        sqt = R2.alloc("sq", [128, 16, TB], BF16)
        hT = R2.alloc("hT", [128, 16, TB], BF16)
        R2sq[0] = sqt
        rmsnorm_big(X, nlw, hT, 16, D_MODEL)
        for half in range(2):
            def cons_up(tag, pb):
                t = next_tmp()
                act(t.all(), pb.all(), AF.Relu)
                tt(uT.c(tag), t.all(), t.all(), ALU.mult)

            dense(W["w_up"], l, 0, 16, lambda kc: hT.c(kc),
                  [(half * 4096 + c * 128, 128, c) for c in range(32)], cons_up)
            dense(W["w_down"], l, half * 4096, 32, lambda kc: uT.c(kc), [(c * 128, 128, c) for c in range(16)], cons_res)

        if STOP == 'F':
            return
        R1.reset()
        eT = R1.alloc("eT", [128, 16, TB], F32)
        pst = R2.alloc("pst", [128, PLE_DIM], F32)
        pTb = R2.alloc("pTb", [128, 2, TB], BF16)
        rmsnorm_big(X, npw, hT, 16, D_MODEL)
        for j in range(4):
            dma(pst.all(), p_d.v(p_d.ap[l, t0 + j * 128:t0 + (j + 1) * 128, :]))
            for c in range(2):
                tr(ps[6].s(c * 128, (c + 1) * 128), pst.s(c * 128, (c + 1) * 128), ident_f.all())
            vcopy(pTb.v(pTb.t[:, :, j * 128:(j + 1) * 128]), ps[6].v(ps[6].t[:, 0:256].rearrange("p (c t) -> p c t", t=128)))

        def cons_e(tag, pb):
            act(eT.c(tag), pb.all(), AF.Copy)

        dense(W["w_ple"], l, 0, 2, lambda kc: pTb.c(kc), [(c * 128, 128, c) for c in range(16)], cons_e)

        def cons_pg(tag, pb):
            t = next_tmp()
            act(t.all(), pb.all(), AF.Sigmoid)
            tt(t.all(), t.all(), eT.c(tag), ALU.mult)
            tt(X.c(tag), X.c(tag), t.all(), ALU.add)

        dense(W["w_ple_gate"], l, 0, 16, lambda kc: hT.c(kc), [(c * 128, 128, c) for c in range(16)], cons_pg)

        if STOP == 'G':
            return
        if not last:
            dma(xT_d.v(xT_d.ap[:, t0:t0 + TB].rearrange("(kc p) t -> p kc t", p=128)), X.all())
        else:
            R2.reset()
            ost = R2.alloc("ost", [128, D_MODEL], F32)
            for j in range(4):
                for q in range(4):
                    pb = next_pmm()
                    for i in range(4):
                        kc = q * 4 + i
                        tr(pb.s(i * 128, (i + 1) * 128), X.c(kc, j * 128, (j + 1) * 128), ident_f.all())
                    vcopy(ost.s(q * 512, (q + 1) * 512), pb.all())
                fin_ops.append(dma(out_d.v(out_d.ap[t0 + j * 128:t0 + (j + 1) * 128, :]), ost.all()))

    setup_consts()
    for l in range(L):
        if STOP == 'setup':
            break
        P.epoch = l
        layer_consts(l)
        if STOP == 'lconsts':
            break
        memset(hS.all(), 0.0)
        for tb in range(NBLK):
            block(l, tb)
    P.emit(final_waits=fin_ops)
    return nc, len(P.ops)


_CACHE = {}


def _invf_table():
    j = np.arange(0, QK_ROPE, 2, dtype=np.float32) / np.float32(QK_ROPE)
    inv = (np.float32(1.0) / (np.float32(10000.0) ** j)).astype(np.float32)
    t = np.zeros((1, 128), np.float32)
    t[0, 0:32] = inv
    t[0, 32:64] = inv
    return t


def kernel(**inputs):
    x = np.ascontiguousarray(inputs["x"], dtype=np.float32)
    B, S, _ = x.shape
    L = inputs["w_in"].shape[0]
    key = (L, S)
    if key not in _CACHE:
        _CACHE[key] = build(L=L, S=S)[0]
    nc = _CACHE[key]
    in_maps = []
    for b in range(B):
        m = {"x": x[b], "p": np.ascontiguousarray(inputs["p"][:, b]),
             "positions": np.ascontiguousarray(inputs["positions"][b:b + 1]).astype(np.int32),
             "invf": _invf_table()}
        for n in WNAMES:
            m[n] = np.ascontiguousarray(inputs[n], dtype=np.float32)
        in_maps.append(m)
    res = run_bass_kernel_spmd(nc, in_maps, core_ids=list(range(B)))
    return np.stack([res.results[b]["out"] for b in range(B)], axis=0).astype(np.float32)
```

```python
import math
import numpy as np
import concourse.bass as bass
import concourse.mybir as mybir
from concourse.bass_utils import run_bass_kernel_spmd

F32 = mybir.dt.float32
BF16 = mybir.dt.bfloat16
I32 = mybir.dt.int32
AF = mybir.ActivationFunctionType
ALU = mybir.AluOpType
PG = 512
NSLOT = 8
DTSZ = {F32: 4, BF16: 2, I32: 4}

D_MODEL = 2048
SEQ = 4096
DEPTH = 4
EPS = 1e-6
HEADS = 16
Q_LORA = 512
KV_LORA = 512
QK_NOPE = 128
QK_ROPE = 64
V_DIM = 128
QK_DIM = 192
D_INNER = 4096
SSM_HEADS = 64
SSM_GROUPS = 8
SSM_STATE = 128
CONV_DIM = 6144
D_FF = 8192
PLE_DIM = 256
D_IN_PROJ = 15488
TB = 512
STOP = None
SM_SCALE = QK_DIM ** -0.5
SM_SHIFT = 14.0


class Page:
    __slots__ = ("w", "r")

    def __init__(self):
        self.w = {}
        self.r = {}


class V:
    __slots__ = ("ap", "pages", "excl")

    def __init__(self, ap, pages, excl=False):
        self.ap = ap
        self.pages = pages
        self.excl = excl


class Tile:
    def __init__(self, prog, name, shape, dtype, space="sbuf", off=None):
        self.name, self.shape, self.dtype = name, list(shape), dtype
        nc = prog.nc
        self.esz = DTSZ[dtype]
        self.nbytes = int(np.prod(shape[1:])) * self.esz
        if space == "sbuf":
            assert off is not None and off % 32 == 0
            assert off + self.nbytes <= prog.sb_limit, f"SBUF overflow at {name}: {off + self.nbytes}"
            self.off = off
            self.t = nc.alloc_sbuf_tensor_at(name, list(shape), dtype, offset=off)
            self.pg = prog.sb_pages
        else:
            self.t = nc.alloc_psum_tensor(name, list(shape), dtype)
            self.off = 0
            self.pg = [Page()]
        self.excl = (space != "sbuf")
        self._cache = {}

    def _pages(self, lo, hi):
        if self.excl:
            return self.pg
        a, b = (self.off + lo) // PG, (self.off + hi - 1) // PG
        k = (a, b)
        p = self._cache.get(k)
        if p is None:
            p = self.pg[a:b + 1]
            self._cache[k] = p
        return p

    def all(self):
        return V(self.t[:], self._pages(0, self.nbytes), self.excl)

    def v(self, ap, lo=None, hi=None):
        lo = 0 if lo is None else lo * self.esz
        hi = self.nbytes if hi is None else hi * self.esz
        return V(ap, self._pages(lo, hi), self.excl)

    def c(self, i, a=None, b=None, p0=0, p1=None):
        n = self.shape[2]
        a = 0 if a is None else a
        b = n if b is None else b
        p1 = self.shape[0] if p1 is None else p1
        return V(self.t[p0:p1, i, a:b], self._pages((i * n + a) * self.esz, (i * n + b) * self.esz), self.excl)

    def s(self, a=None, b=None, p0=0, p1=None):
        a = 0 if a is None else a
        b = self.shape[1] if b is None else b
        p1 = self.shape[0] if p1 is None else p1
        return V(self.t[p0:p1, a:b], self._pages(a * self.esz, b * self.esz), self.excl)


class Dram:
    def __init__(self, ap):
        self.ap = ap
        self.pg = [Page()]

    def v(self, ap=None):
        return V(self.ap if ap is None else ap, self.pg)


class Op:
    __slots__ = ("eng", "fn", "is_dma", "tl", "cnt", "waits", "need_inc", "snap", "val")


class Prog:
    def __init__(self, nc, same_engine_sync=True, sb_base=20992, sb_limit=229376):
        self.nc = nc
        self.ops = []
        self.same_engine_sync = same_engine_sync
        self.sb_base = sb_base
        self.sb_limit = sb_limit
        self.sb_pages = [Page() for _ in range(sb_limit // PG + 2)]
        self.known = {e: {} for e in ("tensor", "vector", "scalar", "gpsimd", "sync")}
        self.tl_count = {}
        self.tl_last = {}
        self.dma_rr = {"sync": 0, "gpsimd": 0, "scalar": 0}
        self.epoch = 0

    def op(self, eng, fn, reads=(), writes=(), dma=False):
        o = Op()
        o.eng, o.fn, o.is_dma, o.need_inc = eng, fn, dma, False
        if dma:
            slot = self.dma_rr[eng]
            self.dma_rr[eng] = (slot + 1) % NSLOT
            o.tl = ("dma", eng, slot, self.epoch)
            lastkey = ("dma", eng, slot)
        else:
            o.tl = ("eng", eng, self.epoch)
        xr = [v for v in reads if v.excl]
        if xr:
            reads = [v for v in reads if not v.excl]
            writes = list(writes) + xr
        k = self.known[eng]
        need = {}
        own = ("eng", eng)
        skip_own = (not dma) and (eng == "tensor" or not self.same_engine_sync)

        def consider(d):
            tl = d.tl
            if skip_own and tl[:2] == own:
                return
            if k.get(tl, 0) >= d.cnt:
                return
            cur = need.get(tl)
            if cur is None or cur.cnt < d.cnt:
                need[tl] = d

        if dma:
            prev = self.tl_last.get(lastkey)
            if prev is not None:
                consider(prev)
        for v in reads:
            for p in v.pages:
                for d in p.w.values():
                    consider(d)
        for v in writes:
            for p in v.pages:
                for d in p.w.values():
                    consider(d)
                for d in p.r.values():
                    consider(d)
        wl = []
        for tl, d in need.items():
            if k.get(tl, 0) >= d.cnt:
                continue
            wl.append(d)
            d.need_inc = True
            k[tl] = d.cnt
            for t2, c2 in d.snap.items():
                if k.get(t2, 0) < c2:
                    k[t2] = c2
        o.waits = wl
        c = self.tl_count.get(o.tl, 0) + 1
        self.tl_count[o.tl] = c
        o.cnt = c
        if dma:
            self.tl_last[lastkey] = o
        o.snap = dict(k)
        for v in reads:
            for p in v.pages:
                p.r[o.tl] = o
        for v in writes:
            for p in v.pages:
                if p.r:
                    p.r = {}
                    p.w = {o.tl: o}
                else:
                    p.w[o.tl] = o
        self.ops.append(o)
        return o

    def dma(self, eng, out, in_, **kw):
        oa, ia = out.ap, in_.ap
        return self.op(eng, lambda e: e.dma_start(out=oa, in_=ia, **kw), [in_], [out], dma=True)

    def emit(self, final_waits=()):
        nc = self.nc
        for d in final_waits:
            d.need_inc = True
        tl_val = {}
        for o in self.ops:
            if o.is_dma:
                o.val = tl_val[o.tl] = tl_val.get(o.tl, 0) + 16
            elif o.need_inc:
                o.val = tl_val[o.tl] = tl_val.get(o.tl, 0) + 1
        sems = {tl: nc.alloc_semaphore("s_" + "_".join(str(x) for x in tl)) for tl in tl_val}
        per_eng = {e: [] for e in self.known}
        for o in self.ops:
            per_eng[o.eng].append(o)

        def run(engname, e):
            for o in per_eng[engname]:
                for d in o.waits:
                    e.wait_ge(sems[d.tl], d.val)
                ins = o.fn(e)
                if o.is_dma:
                    ins.then_inc(sems[o.tl], 16)
                elif o.need_inc:
                    ins.then_inc(sems[o.tl], 1)
            if engname == "sync":
                for d in final_waits:
                    e.wait_ge(sems[d.tl], d.val)

        with nc.Block() as block:
            @block.sync
            def _(e):
                run("sync", e)

            @block.scalar
            def _(e):
                run("scalar", e)

            @block.vector
            def _(e):
                run("vector", e)

            @block.gpsimd
            def _(e):
                run("gpsimd", e)

            @block.tensor
            def _(e):
                run("tensor", e)


WNAMES = ["norm_mix_w", "w_in", "q_a_norm_w", "w_uq", "kv_a_norm_w", "w_ukv", "q_norm_w", "k_norm_w",
          "w_o_mla", "conv_w", "conv_b", "dt_bias", "a_log", "d_skip", "ssm_norm_w", "w_o_ssm", "w_out",
          "norm_mlp_w", "w_up", "w_down", "ple_norm_w", "w_ple_gate", "w_ple"]
WSHAPES = {
    "norm_mix_w": [D_MODEL], "w_in": [D_MODEL, D_IN_PROJ], "q_a_norm_w": [Q_LORA], "w_uq": [Q_LORA, HEADS * QK_DIM],
    "kv_a_norm_w": [KV_LORA], "w_ukv": [KV_LORA, HEADS * 256], "q_norm_w": [QK_DIM], "k_norm_w": [QK_DIM],
    "w_o_mla": [HEADS * V_DIM, D_MODEL], "conv_w": [4, CONV_DIM], "conv_b": [CONV_DIM], "dt_bias": [SSM_HEADS],
    "a_log": [SSM_HEADS], "d_skip": [SSM_HEADS], "ssm_norm_w": [D_INNER], "w_o_ssm": [D_INNER, D_MODEL],
    "w_out": [D_MODEL, D_MODEL], "norm_mlp_w": [D_MODEL], "w_up": [D_MODEL, D_FF], "w_down": [D_FF, D_MODEL],
    "ple_norm_w": [D_MODEL], "w_ple_gate": [D_MODEL, D_MODEL], "w_ple": [PLE_DIM, D_MODEL],
}


SAME_ENGINE_SYNC = True


def build(L=DEPTH, S=SEQ, dbg=False):
    NBLK = S // TB
    nc = bass.Bass("TRN2", target_bir_lowering=False)
    P = Prog(nc, same_engine_sync=SAME_ENGINE_SYNC)

    def din(name, shape, dt=F32):
        return Dram(nc.dram_tensor(name, shape, dt, kind="ExternalInput").ap())

    def dscr(name, shape, dt=F32):
        return Dram(nc.dram_tensor(name, shape, dt).ap())

    x_d = din("x", [S, D_MODEL])
    p_d = din("p", [L, S, PLE_DIM])
    pos_d = din("positions", [1, S], I32)
    invf_d = din("invf", [1, 128])
    W = {n: din(n, [L] + WSHAPES[n]) for n in WNAMES}
    out_d = Dram(nc.dram_tensor("out", [S, D_MODEL], F32, kind="ExternalOutput").ap())
    dbg_outs = {}

    xT_d = dscr("xT_s", [D_MODEL, S])
    xbc_d = dscr("xbc_s", [CONV_DIM, 3 + S])
    zs_d = dscr("zs_s", [D_INNER, TB])
    g_d = dscr("g_s", [2 * D_MODEL, TB])
    knT_d = dscr("knT_s", [HEADS, 128, S], BF16)
    krT_d = dscr("krT_s", [HEADS, 64, S], BF16)
    v_d = dscr("v_s", [S, HEADS * V_DIM], BF16)
    oT_d = dscr("oT_s", [HEADS * V_DIM, TB], BF16)
    cos_d = dscr("cos_s", [64, S])
    sin_d = dscr("sin_s", [64, S])

    cur = [P.sb_base]

    def alloc(name, shape, dt, at=None):
        if at is None:
            off = (cur[0] + PG - 1) // PG * PG
            t = Tile(P, name, shape, dt, off=off)
            cur[0] = off + t.nbytes
        else:
            t = Tile(P, name, shape, dt, off=at)
        return t

    class Region:
        def __init__(self, base, size):
            self.base, self.size, self.cur = base, size, base

        def reset(self):
            self.cur = self.base

        def alloc(self, name, shape, dt):
            off = (self.cur + PG - 1) // PG * PG
            t = Tile(P, name, shape, dt, off=off)
            self.cur = off + t.nbytes
            assert self.cur <= self.base + self.size, f"region overflow {name} {self.cur - self.base} > {self.size}"
            return t

    ident_f = alloc("ident_f", [128, 128], F32)
    ident_b = alloc("ident_b", [128, 128], BF16)
    ones_b = alloc("ones_b", [128, 128], BF16)
    ones_f = alloc("ones_f", [128, 128], F32)
    tri = alloc("tri", [128, 128], F32)
    negmask = alloc("negmask", [128, 128], F32)
    sel_last = alloc("sel_last", [128, 128], F32)
    sel_c0 = alloc("sel_c0", [128, 128], F32)
    sel_c1 = alloc("sel_c1", [128, 128], F32)
    rrot_b = alloc("rrot_b", [128, 128], BF16)
    invf = alloc("invf", [128, 1], F32)
    nmw = alloc("nmw", [128, 16], F32)
    nlw = alloc("nlw", [128, 16], F32)
    npw = alloc("npw", [128, 16], F32)
    qaw = alloc("qaw", [128, 4], F32)
    kvaw = alloc("kvaw", [128, 4], F32)
    qnw = alloc("qnw", [128, 2], F32)
    knw = alloc("knw", [128, 2], F32)
    convw = alloc("convw", [128, 4, 48], F32)
    convb = alloc("convb", [128, 48], F32)
    ssmw = alloc("ssmw", [128, 32], F32)
    dtb = alloc("dtb", [128, 1], F32)
    A_bc = alloc("A_bc", [128, 64], F32)
    D_bc = alloc("D_bc", [128, 64], F32)
    X = alloc("X", [128, 16, TB], F32)
    hS = alloc("hS", [128, 8, 512], F32)
    wbuf = [alloc(f"wbuf{i}", [128, 16 * 512], BF16) for i in range(2)]
    cosb = alloc("cosb", [128, TB], F32)
    sinb = alloc("sinb", [128, TB], F32)
    dtraw = alloc("dtraw", [128, TB], F32)
    dtvT = alloc("dtvT", [128, TB], F32)
    gt = alloc("gt", [128, TB], F32)
    rstd = alloc("rstd", [128, TB], F32)
    tmpA = alloc("tmpA", [128, TB], F32)
    tmpB = alloc("tmpB", [128, TB], F32)
    tmpC = alloc("tmpC", [128, TB], F32)
    base1 = (cur[0] + PG - 1) // PG * PG
    R1 = Region(base1, 32 * 1024)
    R2 = Region(base1 + 32 * 1024, P.sb_limit - (base1 + 32 * 1024))

    ps = [Tile(P, f"psb{i}", [128, 512], F32, space="psum") for i in range(8)]
    pmm_rr = [0]

    def next_pmm():
        b = ps[pmm_rr[0] % 4]
        pmm_rr[0] += 1
        return b

    tmp_rr = [0]

    def next_tmp():
        t = (tmpA, tmpB, tmpC)[tmp_rr[0] % 3]
        tmp_rr[0] += 1
        return t

    def mm(out, lhsT, rhs, start=True, stop=True):
        oa, la, ra = out.ap, lhsT.ap, rhs.ap
        return P.op("tensor", lambda e: e.matmul(oa, la, ra, start=start, stop=stop), [lhsT, rhs], [out])

    def tr(out, in_, ident):
        oa, ia, da = out.ap, in_.ap, ident.ap
        return P.op("tensor", lambda e: e.transpose(oa, ia, da), [in_, ident], [out])

    def act(out, in_, func, bias=None, scale=1.0, eng="scalar"):
        oa, ia = out.ap, in_.ap
        rd = [in_]
        kw = {}
        if isinstance(bias, V):
            rd.append(bias)
            kw["bias"] = bias.ap
        elif bias is not None:
            kw["bias"] = float(bias)
        return P.op("scalar", lambda e: e.activation(out=oa, in_=ia, func=func, scale=scale, **kw), rd, [out])

    def vcopy(out, in_, eng="vector"):
        oa, ia = out.ap, in_.ap
        return P.op(eng, lambda e: e.tensor_copy(oa, ia), [in_], [out])

    def tt(out, a, b, op, eng="vector"):
        oa, aa, ba = out.ap, a.ap, b.ap
        return P.op(eng, lambda e: e.tensor_tensor(oa, aa, ba, op), [a, b], [out])

    def ts(out, a, s1, s2, op0, op1=None, eng="vector"):
        oa, aa = out.ap, a.ap
        rd = [a]
        s1a = s1
        if isinstance(s1, V):
            rd.append(s1)
            s1a = s1.ap
        s2a = s2
        if isinstance(s2, V):
            rd.append(s2)
            s2a = s2.ap
        if op1 is None:
            return P.op(eng, lambda e: e.tensor_scalar(oa, aa, s1a, s2a, op0), rd, [out])
        return P.op(eng, lambda e: e.tensor_scalar(oa, aa, s1a, s2a, op0, op1), rd, [out])

    def stt(out, a, sc, b, op0, op1, eng="vector"):
        oa, aa, ba = out.ap, a.ap, b.ap
        rd = [a, b]
        sa = sc
        if isinstance(sc, V):
            rd.append(sc)
            sa = sc.ap
        return P.op(eng, lambda e: e.scalar_tensor_tensor(oa, aa, sa, ba, op0, op1), rd, [out])

    def memset(out, val, eng="vector"):
        oa = out.ap
        return P.op(eng, lambda e: e.memset(oa, val), [], [out])

    def recip(out, in_):
        oa, ia = out.ap, in_.ap
        return P.op("vector", lambda e: e.reciprocal(oa, ia), [in_], [out])

    def dma(out, in_, eng="sync"):
        return P.dma(eng, out, in_)

    def bc_mid(v, tile, n):
        ap = v.ap
        return V(ap.unsqueeze(1).to_broadcast([ap.shape[0], n, ap.shape[1]]), v.pages)

    def bc_last(v, n):
        ap = v.ap
        return V(ap.unsqueeze(2).to_broadcast([ap.shape[0], ap.shape[1], n]), v.pages)

    def setup_consts():
        io = R2.alloc("c_io", [128, 128], F32)
        ip = R2.alloc("c_ip", [128, 1], F32)
        jh = R2.alloc("c_jh", [128, 128], F32)
        ph = R2.alloc("c_ph", [128, 1], F32)
        t1 = R2.alloc("c_t1", [128, 128], F32)
        t2 = R2.alloc("c_t2", [128, 128], F32)
        P.op("gpsimd", lambda e: e.iota(io.t[:], pattern=[[1, 128]], base=0, channel_multiplier=0,
                                        allow_small_or_imprecise_dtypes=True), [], [io.all()])
        P.op("gpsimd", lambda e: e.iota(ip.t[:], pattern=[[0, 1]], base=0, channel_multiplier=1,
                                        allow_small_or_imprecise_dtypes=True), [], [ip.all()])
        ts(ident_f.all(), io.all(), ip.all(), None, ALU.is_equal)
        vcopy(ident_b.all(), ident_f.all())
        memset(ones_b.all(), 1.0)
        memset(ones_f.all(), 1.0)
        ts(jh.all(), io.all(), 64.0, None, ALU.is_ge)
        ts(ph.all(), ip.all(), 64.0, None, ALU.is_ge)
        ts(t1.all(), io.all(), ip.all(), None, ALU.is_ge)
        ts(t2.all(), jh.all(), ph.all(), None, ALU.is_equal)
        tt(tri.all(), t1.all(), t2.all(), ALU.mult)
        ts(negmask.all(), tri.all(), -1.0, 30000.0, ALU.add, ALU.mult)
        ts(t1.all(), jh.all(), 64.0, 63.0, ALU.mult, ALU.add)
        ts(sel_last.all(), t1.all(), ip.all(), None, ALU.is_equal)
        memset(t1.all(), 63.0)
        ts(sel_c0.all(), t1.all(), ip.all(), None, ALU.is_equal)
        memset(t1.all(), 127.0)
        ts(sel_c1.all(), t1.all(), ip.all(), None, ALU.is_equal)
        ts(t1.all(), io.all(), -32.0, None, ALU.add)
        ts(t1.all(), t1.all(), ip.all(), None, ALU.is_equal)
        ts(t2.all(), io.all(), 32.0, None, ALU.add)
        ts(t2.all(), t2.all(), ip.all(), None, ALU.is_equal)
        tt(t1.all(), t1.all(), t2.all(), ALU.subtract)
        vcopy(rrot_b.all(), t1.all())
        memset(dtraw.all(), 0.0)
        memset(hS.all(), 0.0)
        st = R2.alloc("c_st", [128, 128], F32)
        memset(st.all(), 0.0)
        dma(st.s(0, 128, 0, 1), invf_d.v())
        tr(ps[6].s(0, 128), st.all(), ident_f.all())
        vcopy(invf.all(), ps[6].s(0, 1))
        memset(tmpA.all(), 0.0)
        for j in range(CONV_DIM // 128):
            dma(xbc_d.v(xbc_d.ap[j * 128:(j + 1) * 128, 0:3]), tmpA.s(0, 3))
        R2.reset()
        TWO_PI = 2.0 * math.pi
        C1 = 6.28125
        C2 = TWO_PI - C1
        MAGIC = 12582912.0
        for c0 in range(0, S, 2048):
            n = min(2048, S - c0)
            pi_ = R2.alloc("r_pi", [64, 2048], I32)
            ang = R2.alloc("r_ang", [64, 2048], F32)
            nn = R2.alloc("r_n", [64, 2048], F32)
            rr = R2.alloc("r_r", [64, 2048], F32)
            dma(pi_.s(0, n), pos_d.v(pos_d.ap[0, c0:c0 + n].partition_broadcast(64)))
            vcopy(ang.s(0, n), pi_.s(0, n))
            ts(ang.s(0, n), ang.s(0, n), invf.s(0, 1, 0, 64), None, ALU.mult)
            ts(nn.s(0, n), ang.s(0, n), 1.0 / TWO_PI, MAGIC, ALU.mult, ALU.add)
            ts(nn.s(0, n), nn.s(0, n), -MAGIC, None, ALU.add)
            stt(rr.s(0, n), nn.s(0, n), -C1, ang.s(0, n), ALU.mult, ALU.add)
            stt(rr.s(0, n), nn.s(0, n), -C2, rr.s(0, n), ALU.mult, ALU.add)
            ts(rr.s(0, n), rr.s(0, n), 3.1415925, -3.1415925, ALU.min, ALU.max)
            act(ang.s(0, n), rr.s(0, n), AF.Sin)
            dma(sin_d.v(sin_d.ap[:, c0:c0 + n]), ang.s(0, n))
            ts(nn.s(0, n), rr.s(0, n), -1.0, None, ALU.mult)
            tt(nn.s(0, n), nn.s(0, n), rr.s(0, n), ALU.max)
            act(ang.s(0, n), nn.s(0, n), AF.Sin, bias=math.pi / 2, scale=-1.0)
            dma(cos_d.v(cos_d.ap[:, c0:c0 + n]), ang.s(0, n))
            R2.reset()

    def load_cols(dst_v, src_ap_rows, nrows, ncols=128):
        st = R2.alloc("lc_st", [128, 128], F32)
        memset(st.all(), 0.0)
        dma(st.s(0, ncols, 0, nrows), Dram(src_ap_rows).v())
        tr(ps[6].s(0, 128), st.all(), ident_f.all())
        vcopy(dst_v, ps[6].s(0, nrows))
        R2.cur -= 0

    def layer_consts(l):
        R2.reset()
        for (t, nm, n) in ((nmw, "norm_mix_w", 16), (nlw, "norm_mlp_w", 16), (npw, "ple_norm_w", 16),
                           (qaw, "q_a_norm_w", 4), (kvaw, "kv_a_norm_w", 4), (ssmw, "ssm_norm_w", 32),
                           (convb, "conv_b", 48)):
            load_cols(t.all(), W[nm].ap[l].rearrange("(r c) -> r c", c=128), n)
            R2.reset()
        for tap in range(4):
            load_cols(convw.c(tap), W["conv_w"].ap[l, tap].rearrange("(r c) -> r c", c=128), 48)
            R2.reset()
        for (t, nm) in ((qnw, "q_norm_w"), (knw, "k_norm_w")):
            load_cols(t.s(0, 1), W[nm].ap[l, 0:128].rearrange("(r c) -> r c", c=128), 1)
            R2.reset()
            load_cols(t.s(1, 2), W[nm].ap[l, 128:192].rearrange("(r c) -> r c", c=64), 1, ncols=64)
            R2.reset()
        load_cols(dtb.all(), W["dt_bias"].ap[l].rearrange("(r c) -> r c", c=64), 1, ncols=64)
        R2.reset()
        dma(A_bc.all(), W["a_log"].v(W["a_log"].ap[l].partition_broadcast(128)))
        act(A_bc.all(), A_bc.all(), AF.Exp)
        ts(A_bc.all(), A_bc.all(), -1.0, None, ALU.mult)
        dma(D_bc.all(), W["d_skip"].v(W["d_skip"].ap[l].partition_broadcast(128)))

    wb_rr = [0]

    def dense(Wd, l, row0, KC, in_chunk, segs, consumer, gcols=None):
        if gcols is None:
            gcols = min(512, 8192 // KC)
        groups = []
        for sg in segs:
            c0, n, tag = sg
            if groups and groups[-1][0] + groups[-1][1] == c0 and (groups[-1][1] + n) <= gcols:
                groups[-1][1] += n
                groups[-1][2].append(sg)
            else:
                groups.append([c0, n, [sg]])
        for (g0, gw, sgs) in groups:
            wb = wbuf[wb_rr[0] % 2]
            wb_rr[0] += 1
            src = Wd.ap[l, row0:row0 + KC * 128, g0:g0 + gw].rearrange("(kc p) f -> p kc f", p=128)
            dst_ap = wb.t[:, 0:KC * gw].rearrange("p (kc f) -> p kc f", f=gw)
            dstv = wb.v(dst_ap, 0, KC * gw)
            P.dma("gpsimd", dstv, Wd.v(src))
            for (c0, n, tag) in sgs:
                pb = next_pmm()
                o = c0 - g0
                for kc in range(KC):
                    lw = wb.v(wb.t[:, kc * gw + o: kc * gw + o + n], kc * gw + o, kc * gw + o + n)
                    mm(pb.s(0, TB, 0, n), lw, in_chunk(kc), start=(kc == 0), stop=(kc == KC - 1))
                consumer(tag, pb)

    def rmsnorm_big(src, wcols, dst, nch, D):
        sqt = R2sq[0]
        for kc in range(nch):
            act(sqt.c(kc), src.c(kc), AF.Square)
        for kc in range(nch):
            mm(ps[7].all(), ones_b.all(), sqt.c(kc), start=(kc == 0), stop=(kc == nch - 1))
        act(rstd.all(), ps[7].all(), AF.Sqrt, bias=EPS, scale=1.0 / D)
        recip(rstd.all(), rstd.all())
        for kc in range(nch):
            stt(dst.c(kc), src.c(kc), wcols.s(kc, kc + 1), rstd.all(), ALU.mult, ALU.mult)

    R2sq = [None]
    fin_ops = []

    def rope(dst_bf, src_f32, tmp_b, tmp_f):
        vcopy(tmp_b, src_f32)
        mm(ps[6].s(0, TB, 0, 64), rrot_b.s(0, 64, 0, 64), tmp_b)
        tt(tmp_f, ps[6].s(0, TB, 0, 64), sinb.s(0, TB, 0, 64), ALU.mult)
        tt(src_f32, src_f32, cosb.s(0, TB, 0, 64), ALU.mult)
        tt(dst_bf, src_f32, tmp_f, ALU.add)

    def block(l, tb):
        t0 = tb * TB
        last = (l == L - 1)
        R1.reset()
        R2.reset()
        if l == 0:
            stg = R2.alloc("xs", [128, D_MODEL], F32)
            for j in range(4):
                dma(stg.all(), x_d.v(x_d.ap[t0 + j * 128: t0 + (j + 1) * 128, :]))
                for q in range(4):
                    pb = next_pmm()
                    for i in range(4):
                        kc = q * 4 + i
                        tr(pb.s(i * 128, (i + 1) * 128), stg.s(kc * 128, (kc + 1) * 128), ident_f.all())
                    dst = X.v(X.t[:, q * 4:(q + 1) * 4, j * 128:(j + 1) * 128], q * 4 * TB, (q + 1) * 4 * TB)
                    src = pb.v(pb.t[:, :].rearrange("p (c t) -> p c t", t=128))
                    vcopy(dst, src, eng="scalar" if False else "vector")
            R2.reset()
        else:
            dma(X.all(), xT_d.v(xT_d.ap[:, t0:t0 + TB].rearrange("(kc p) t -> p kc t", p=128)))
        dma(cosb.s(0, TB, 0, 64), cos_d.v(cos_d.ap[:, t0:t0 + TB]))
        dma(sinb.s(0, TB, 0, 64), sin_d.v(sin_d.ap[:, t0:t0 + TB]))

        if STOP == 'load':
            return
        sqt = R1.alloc("sq", [128, 16, TB], BF16)
        hT = R1.alloc("hT", [128, 16, TB], BF16)
        R2sq[0] = sqt
        cq = R2.alloc("cq", [128, 4, TB], F32)
        ckv = R2.alloc("ckv", [128, 4, TB], F32)
        kr = R2.alloc("kr", [128, TB], F32)
        rmsnorm_big(X, nmw, hT, 16, D_MODEL)
        segs = []
        for j in range(4):
            segs.append((j * 128, 128, ("cq", j)))
        for j in range(4):
            segs.append((512 + j * 128, 128, ("ckv", j)))
        segs.append((1024, 64, ("kr", 0)))
        for j in range(32):
            segs.append((1088 + j * 128, 128, ("z", j)))
        for j in range(48):
            segs.append((5184 + j * 128, 128, ("xbc", j)))
        segs.append((11328, 64, ("dt", 0)))
        for j in range(32):
            segs.append((11392 + j * 128, 128, ("g", j)))

        def cons_in(tag, pb):
            kind, j = tag
            if kind == "cq":
                act(cq.c(j), pb.all(), AF.Copy)
            elif kind == "ckv":
                act(ckv.c(j), pb.all(), AF.Copy)
            elif kind == "kr":
                act(kr.s(0, TB, 0, 64), pb.s(0, TB, 0, 64), AF.Copy)
            elif kind == "dt":
                vcopy(dtraw.s(0, TB, 0, 64), pb.s(0, TB, 0, 64))
            elif kind == "z":
                t = next_tmp()
                act(t.all(), pb.all(), AF.Silu)
                dma(zs_d.v(zs_d.ap[j * 128:(j + 1) * 128, :]), t.all())
            elif kind == "xbc":
                t = next_tmp()
                vcopy(t.all(), pb.all())
                dma(xbc_d.v(xbc_d.ap[j * 128:(j + 1) * 128, 3 + t0:3 + t0 + TB]), t.all())
            elif kind == "g":
                t = next_tmp()
                act(t.all(), pb.all(), AF.Sigmoid)
                dma(g_d.v(g_d.ap[j * 128:(j + 1) * 128, :]), t.all())

        dense(W["w_in"], l, 0, 16, lambda kc: hT.c(kc), segs, cons_in)

        if STOP == 'A':
            return
        R1.reset()
        QnT = R1.alloc("QnT", [128, 16, TB], BF16)
        QrT = R1.alloc("QrT", [128, 16, TB], BF16)
        cqn = R2.alloc("cqn", [128, 4, TB], BF16)
        ckvn = R2.alloc("ckvn", [128, 4, TB], BF16)
        sq4 = R2.alloc("sq4", [128, 4, TB], BF16)
        R2sq[0] = sq4
        hf = [R2.alloc(f"hf{i}", [128, TB], F32) for i in range(2)]
        hr = R2.alloc("hr", [128, TB], F32)
        vf = R2.alloc("vf", [128, TB], F32)
        tb16 = R2.alloc("tb16", [128, TB], BF16)
        tf32 = R2.alloc("tf32", [128, TB], F32)
        kob = R2.alloc("kob", [128, TB], BF16)
        krb = R2.alloc("krb", [128, TB], BF16)
        vtm = R2.alloc("vtm", [128, 4, 128], BF16)
        kr2 = R2.alloc("kr2", [128, TB], BF16)

        rmsnorm_big(cq, qaw, cqn, 4, Q_LORA)
        hf_rr = [0]
        cur_hf = [None]

        def head_stats(nope_v, rope_sq_v, rope_src_v):
            sq = next_tmp()
            sqb = V(sq.t[:, :].bitcast(BF16)[:, 0:TB], sq._pages(0, TB * 2))
            act(sqb, nope_v, AF.Square)
            mm(ps[7].all(), ones_b.all(), sqb, start=True, stop=False)
            if rope_sq_v is None:
                sq2 = V(sq.t[0:64, :].bitcast(BF16)[:, TB:2 * TB], sq._pages(TB * 2, TB * 4))
                act(sq2, rope_src_v, AF.Square)
                rope_sq_v = sq2
            mm(ps[7].all(), ones_b.s(0, 128, 0, 64), rope_sq_v, start=False, stop=True)
            act(rstd.all(), ps[7].all(), AF.Sqrt, bias=EPS, scale=1.0 / QK_DIM)
            recip(rstd.all(), rstd.all())

        def cons_q(tag, pb):
            kind, h = tag
            if kind == "qn":
                cur_hf[0] = hf[hf_rr[0] % 2]
                hf_rr[0] += 1
                act(cur_hf[0].all(), pb.all(), AF.Copy)
            else:
                act(hr.s(0, TB, 0, 64), pb.s(0, TB, 0, 64), AF.Copy)
                head_stats(cur_hf[0].all(), None, hr.s(0, TB, 0, 64))
                stt(QnT.c(h), cur_hf[0].all(), qnw.s(0, 1), rstd.all(), ALU.mult, ALU.mult)
                stt(hr.s(0, TB, 0, 64), hr.s(0, TB, 0, 64), qnw.s(1, 2, 0, 64), rstd.s(0, TB, 0, 64), ALU.mult, ALU.mult)
                rope(QrT.c(h, None, None, 0, 64), hr.s(0, TB, 0, 64), tb16.s(0, TB, 0, 64), tf32.s(0, TB, 0, 64))

        segs = []
        for h in range(HEADS):
            segs.append((h * 192, 128, ("qn", h)))
            segs.append((h * 192 + 128, 64, ("qr", h)))
        dense(W["w_uq"], l, 0, 4, lambda kc: cqn.c(kc), segs, cons_q)

        rmsnorm_big(ckv, kvaw, ckvn, 4, KV_LORA)
        act(kr2.s(0, TB, 0, 64), kr.s(0, TB, 0, 64), AF.Square)

        def cons_kv(tag, pb):
            kind, h = tag
            if kind == "kn":
                cur_hf[0] = hf[hf_rr[0] % 2]
                hf_rr[0] += 1
                act(cur_hf[0].all(), pb.all(), AF.Copy)
                head_stats(cur_hf[0].all(), kr2.s(0, TB, 0, 64), None)
                stt(kob.all(), cur_hf[0].all(), knw.s(0, 1), rstd.all(), ALU.mult, ALU.mult)
                dma(knT_d.v(knT_d.ap[h, :, t0:t0 + TB]), kob.all())
                stt(hr.s(0, TB, 0, 64), kr.s(0, TB, 0, 64), knw.s(1, 2, 0, 64), rstd.s(0, TB, 0, 64), ALU.mult, ALU.mult)
                rope(krb.s(0, TB, 0, 64), hr.s(0, TB, 0, 64), tb16.s(0, TB, 0, 64), tf32.s(0, TB, 0, 64))
                dma(krT_d.v(krT_d.ap[h, :, t0:t0 + TB]), krb.s(0, TB, 0, 64))
            else:
                act(vf.all(), pb.all(), AF.Copy)
                pt_ = ps[6]
                for j in range(4):
                    tr(pt_.s(j * 128, (j + 1) * 128), vf.s(j * 128, (j + 1) * 128), ident_f.all())
                vcopy(vtm.all(), pt_.v(pt_.t[:, :].rearrange("p (j v) -> p j v", v=128)))
                dst = v_d.ap[t0:t0 + TB, h * 128:(h + 1) * 128].rearrange("(j p) v -> p j v", p=128)
                dma(v_d.v(dst), vtm.all())

        segs = []
        for h in range(HEADS):
            segs.append((h * 256, 128, ("kn", h)))
            segs.append((h * 256 + 128, 128, ("v", h)))
        dense(W["w_ukv"], l, 0, 4, lambda kc: ckvn.c(kc), segs, cons_kv)

        if STOP == 'A2':
            return
        R2.reset()
        nk = (tb + 1) * TB
        ntile = nk // 128
        Kn = R2.alloc("Kn", [128, S], BF16)
        Kr = R2.alloc("Kr", [128, S], BF16)
        Vh = R2.alloc("Vh", [128, S // 128, 128], BF16)
        pts = [R2.alloc(f"pt{i}", [128, TB], BF16) for i in range(3)]
        oTt = R2.alloc("oT", [128, 16, TB], BF16)
        rden = R2.alloc("rden", [128, TB], F32)
        O_ps, D_ps = ps[4], ps[5]
        for h in range(HEADS):
            dma(Kn.s(0, nk), knT_d.v(knT_d.ap[h, :, 0:nk]))
            dma(Kr.s(0, nk, 0, 64), krT_d.v(krT_d.ap[h, :, 0:nk]))
            dma(Vh.v(Vh.t[:, 0:ntile, :], 0, ntile * 128),
                v_d.v(v_d.ap[0:nk, h * 128:(h + 1) * 128].rearrange("(kt p) v -> p kt v", p=128)))
            for kt in range(ntile):
                j = kt - 4 * tb
                c0 = 128 * j if j >= 0 else 0
                sp = next_pmm()
                mm(sp.s(c0, TB), Kn.s(kt * 128, (kt + 1) * 128), QnT.c(h, c0, TB), start=True, stop=False)
                mm(sp.s(c0, TB), Kr.s(kt * 128, (kt + 1) * 128, 0, 64), QrT.c(h, c0, TB, 0, 64), start=False, stop=True)
                pt = pts[kt % 3]
                act(pt.s(c0, TB), sp.s(c0, TB), AF.Exp, bias=-SM_SHIFT, scale=SM_SCALE)
                if j >= 0:
                    memset(pt.s(c0, c0 + 64, 64, 128), 0.0)
                mm(O_ps.s(c0, TB), Vh.c(kt), pt.s(c0, TB), start=(kt == 0), stop=(kt == ntile - 1))
                mm(D_ps.s(c0, TB), ones_b.all(), pt.s(c0, TB), start=(kt == 0), stop=(kt == ntile - 1))
            recip(rden.all(), D_ps.all())
            tt(oTt.c(h), O_ps.all(), rden.all(), ALU.mult)
        dma(oT_d.v(oT_d.ap.rearrange("(h p) t -> p h t", p=128)), oTt.all())

        if STOP == 'B':
            return
        R1.reset()
        R2.reset()
        ybT = R1.alloc("ybT", [128, 32, TB], BF16)
        class _Sm:
            def __init__(self, t):
                self.t = t

            def __getitem__(self, j):
                t = self.t

                class _J:
                    def all(self_):
                        return t.c(j)

                    def s(self_, a, b):
                        return t.c(j, a, b)
                return _J()
        sm = {nm: _Sm(R2.alloc(nm, [128, 4, 64], F32)) for nm in ("dt_tm", "acum", "wdec", "ea", "cd0", "cd1")}
        win = R2.alloc("win", [128, TB + 3], F32)
        acc = R2.alloc("acc", [128, TB], F32)
        RW = Region(wbuf[0].off, 2 * wbuf[0].nbytes)

        def ssd_set(reg, sfx, banks, reg_y):
            d = {}
            d["xg"] = reg.alloc("xg" + sfx, [128, 4, TB], F32)
            d["Bgf"] = reg.alloc("Bgf" + sfx, [128, TB], F32)
            d["BgT"] = reg.alloc("BgT" + sfx, [128, TB], BF16)
            d["CgT"] = reg.alloc("CgT" + sfx, [128, TB], BF16)
            d["xtm"] = reg.alloc("xtm" + sfx, [128, TB], BF16)
            d["skipt"] = reg.alloc("skipt" + sfx, [128, TB], F32)
            d["Blo"] = reg.alloc("Blo" + sfx, [128, 128], BF16)
            d["Bhi"] = reg.alloc("Bhi" + sfx, [128, 128], BF16)
            d["Clo"] = reg.alloc("Clo" + sfx, [128, 128], BF16)
            d["Chi"] = reg.alloc("Chi" + sfx, [128, 128], BF16)
            d["cbs"] = reg.alloc("cbs" + sfx, [128, 128], F32)
            d["rdg"] = reg.alloc("rdg" + sfx, [128, 8, 128], F32)
            d["Dm"] = d["rdg"]
            d["mT"] = reg.alloc("mT" + sfx, [128, 8, 128], BF16)
            d["xw"] = reg.alloc("xw" + sfx, [128, TB], BF16)
            d["hb0"] = reg.alloc("hb0" + sfx, [128, TB], BF16)
            d["hb1"] = reg.alloc("hb1" + sfx, [128, TB], BF16)
            d["ysb"] = reg.alloc("ysb" + sfx, [128, TB], F32)
            d["yTg"] = reg_y.alloc("yTg" + sfx, [128, 4, TB], F32)
            d["sqg"] = Tile(P, "sqg" + sfx, [128, 4, TB], BF16, off=d["rdg"].off)
            d["zt"] = Tile(P, "zt" + sfx, [128, TB], F32, off=d["skipt"].off)
            d["rstd"] = Tile(P, "rstd" + sfx, [128, TB], F32, off=d["ysb"].off)
            d["banks"] = banks
            return d

        sets = [ssd_set(R2, "_a", [ps[0], ps[1], ps[2], ps[3]], R2), ssd_set(RW, "_b", [ps[4], ps[5], ps[6], ps[7]], R2)]

        act(dtvT.all(), dtraw.all(), AF.Exp, bias=dtb.all())
        act(dtvT.all(), dtvT.all(), AF.Ln, bias=1.0)
        if STOP == 'D0a':
            return
        for j in range(4):
            tr(ps[6].s(0, 128), dtvT.s(j * 128, (j + 1) * 128), ident_f.all())
            vcopy(sm["dt_tm"][j].all(), ps[6].s(0, 64))
            if STOP == 'D0b':
                return
            a_t = next_tmp()
            tt(a_t.s(0, 64), sm["dt_tm"][j].all(), A_bc.all(), ALU.mult)
            mm(ps[7].s(0, 64), tri.all(), a_t.s(0, 64))
            if STOP == 'D0c':
                return
            vcopy(sm["acum"][j].all(), ps[7].s(0, 64))
            act(sm["ea"][j].all(), ps[7].s(0, 64), AF.Exp)
            if STOP == 'D0d':
                return
            mm(ps[6].s(0, 64), sel_last.all(), sm["acum"][j].all())
            tt(sm["wdec"][j].all(), ps[6].s(0, 64), sm["acum"][j].all(), ALU.subtract)
            act(sm["wdec"][j].all(), sm["wdec"][j].all(), AF.Exp)
            tt(sm["wdec"][j].all(), sm["wdec"][j].all(), sm["dt_tm"][j].all(), ALU.mult)
            mm(ps[7].s(0, 64), sel_c0.all(), sm["acum"][j].all())
            act(sm["cd0"][j].all(), ps[7].s(0, 64), AF.Exp)
            mm(ps[6].s(0, 64), sel_c1.all(), sm["acum"][j].all())
            act(sm["cd1"][j].all(), ps[6].s(0, 64), AF.Exp)

        if STOP == 'D0':
            return
        def conv_chunk(ci, out_v):
            dma(win.all(), xbc_d.v(xbc_d.ap[ci * 128:(ci + 1) * 128, t0:t0 + TB + 3]))
            ts(acc.all(), win.s(0, TB), convw.c(0, ci, ci + 1), None, ALU.mult)
            for tap in range(1, 4):
                stt(acc.all(), win.s(tap, tap + TB), convw.c(tap, ci, ci + 1), acc.all(), ALU.mult, ALU.add)
            act(out_v, acc.all(), AF.Silu, bias=convb.s(ci, ci + 1))

        def ssd_group(g, d):
            g8 = g * 8
            b0, b1, b2, b3 = d["banks"]
            xg, Bgf, BgT, CgT, xtm, skipt = d["xg"], d["Bgf"], d["BgT"], d["CgT"], d["xtm"], d["skipt"]
            Blo, Bhi, Clo, Chi, cbs, rdg, Dm, mT = d["Blo"], d["Bhi"], d["Clo"], d["Chi"], d["cbs"], d["rdg"], d["Dm"], d["mT"]
            xw, hb0, hb1, ysb, yTg, sqg, zt, rstd_g = d["xw"], d["hb0"], d["hb1"], d["ysb"], d["yTg"], d["sqg"], d["zt"], d["rstd"]
            for c in range(4):
                conv_chunk(g * 4 + c, xg.c(c))
                yield
            conv_chunk(32 + g, Bgf.all())
            vcopy(BgT.all(), Bgf.all())
            conv_chunk(40 + g, CgT.all())
            yield
            hs_g = hS.c(g)
            for j in range(4):
                a_cum, dtt, wd, eav, c0v, c1v = (sm[n][j] for n in ("acum", "dt_tm", "wdec", "ea", "cd0", "cd1"))
                for c in range(4):
                    tr(b0.s(c * 128, (c + 1) * 128), xg.c(c, j * 128, (j + 1) * 128), ident_f.all())
                tr(b1.s(0, 128), Bgf.s(j * 128, (j + 1) * 128), ident_f.all())
                mm(b2.s(0, 128), BgT.s(j * 128, (j + 1) * 128), CgT.s(j * 128, (j + 1) * 128))
                yield
                vcopy(xtm.all(), b0.all())
                x3 = b0.v(b0.t[:, :].rearrange("p (r q) -> p r q", q=64))
                tt(skipt.v(skipt.t[:, :].rearrange("p (r q) -> p r q", q=64)), x3,
                   bc_last(D_bc.s(g8, g8 + 8), 64), ALU.mult)
                tt(xw.v(xw.t[:, :].rearrange("p (r q) -> p r q", q=64)), x3,
                   bc_last(wd.s(g8, g8 + 8), 64), ALU.mult)
                memset(Blo.all(), 0.0, eng="gpsimd")
                memset(Bhi.all(), 0.0, eng="gpsimd")
                act(Blo.s(0, 128, 0, 64), b1.s(0, 128, 0, 64), AF.Copy)
                act(Bhi.s(0, 128, 64, 128), b1.s(0, 128, 64, 128), AF.Copy)
                memset(Clo.all(), 0.0, eng="gpsimd")
                memset(Chi.all(), 0.0, eng="gpsimd")
                vcopy(Clo.s(0, 64), CgT.s(j * 128, j * 128 + 64), eng="gpsimd")
                vcopy(Chi.s(64, 128), CgT.s(j * 128 + 64, (j + 1) * 128), eng="gpsimd")
                vcopy(cbs.all(), b2.s(0, 128))
                tt(rdg.all(), bc_mid(ident_f.all(), ident_f, 8), bc_last(a_cum.s(g8, g8 + 8), 128), ALU.mult)
                yield
                mm(b3.all(), ones_f.all(), rdg.v(rdg.t[:, 0:4, :], 0, 512))
                mm(b1.all(), ones_f.all(), rdg.v(rdg.t[:, 4:8, :], 512, 1024))
                yield
                for q, rb in ((0, b3), (1, b1)):
                    tt(Dm.v(Dm.t[:, q * 4:(q + 1) * 4, :], q * 512, (q + 1) * 512),
                       rb.v(rb.t[:, :].rearrange("p (h t) -> p h t", t=128)),
                       bc_last(a_cum.s(g8 + q * 4, g8 + q * 4 + 4), 128), ALU.subtract)
                tt(Dm.all(), Dm.all(), bc_mid(negmask.all(), negmask, 8), ALU.add)
                yield
                act(Dm.all(), Dm.all(), AF.Exp)
                yield
                tt(Dm.all(), Dm.all(), bc_last(dtt.s(g8, g8 + 8), 128), ALU.mult)
                tt(mT.all(), Dm.all(), bc_mid(cbs.all(), cbs, 8), ALU.mult)
                yield
                for r in range(8):
                    mm(b0.s(r * 64, (r + 1) * 64), mT.c(r), xtm.s(r * 64, (r + 1) * 64))
                mm(b2.all(), Blo.all(), xw.all())
                mm(b3.all(), Bhi.all(), xw.all())
                yield
                act(hb0.all(), hs_g, AF.Copy)
                h3 = V(hs_g.ap.rearrange("p (r q) -> p r q", q=64), hs_g.pages)
                tt(h3, h3, bc_last(c0v.s(g8, g8 + 8), 64), ALU.mult)
                tt(hs_g, hs_g, b2.all(), ALU.add)
                act(hb1.all(), hs_g, AF.Copy)
                tt(h3, h3, bc_last(c1v.s(g8, g8 + 8), 64), ALU.mult)
                tt(hs_g, hs_g, b3.all(), ALU.add)
                yield
                mm(b1.all(), Clo.all(), hb0.all(), start=True, stop=False)
                mm(b1.all(), Chi.all(), hb1.all(), start=False, stop=True)
                yield
                y3 = ysb.v(ysb.t[:, :].rearrange("p (r q) -> p r q", q=64))
                tt(y3, b1.v(b1.t[:, :].rearrange("p (r q) -> p r q", q=64)), bc_last(eav.s(g8, g8 + 8), 64), ALU.mult)
                tt(ysb.all(), ysb.all(), b0.all(), ALU.add)
                tt(ysb.all(), ysb.all(), skipt.all(), ALU.add)
                yield
                for c in range(4):
                    tr(b2.s(c * 128, (c + 1) * 128), ysb.s(c * 128, (c + 1) * 128), ident_f.all())
                yield
                act(yTg.v(yTg.t[:, :, j * 128:(j + 1) * 128]), b2.v(b2.t[:, :].rearrange("p (c t) -> p c t", t=128)), AF.Copy)
                yield
            for c in range(4):
                ci = g * 4 + c
                dma(zt.all(), zs_d.v(zs_d.ap[ci * 128:(ci + 1) * 128, :]))
                tt(yTg.c(c), yTg.c(c), zt.all(), ALU.mult)
                act(sqg.c(c), yTg.c(c), AF.Square)
                yield
            for c in range(4):
                mm(b3.all(), ones_b.all(), sqg.c(c), start=(c == 0), stop=(c == 3))
            yield
            act(rstd_g.all(), b3.all(), AF.Sqrt, bias=EPS, scale=1.0 / 512)
            recip(rstd_g.all(), rstd_g.all())
            yield
            for c in range(4):
                ci = g * 4 + c
                stt(ybT.c(ci), yTg.c(c), ssmw.s(ci, ci + 1), rstd_g.all(), ALU.mult, ALU.mult)
            yield

        for gp in range(0, SSM_GROUPS, 2):
            gens = [ssd_group(gp, sets[0]), ssd_group(gp + 1, sets[1])]
            while gens:
                for gn in list(gens):
                    try:
                        next(gn)
                    except StopIteration:
                        gens.remove(gn)

        if STOP == 'D':
            return
        R2.reset()
        oTt = R2.alloc("oT2", [128, 16, TB], BF16)
        mga = R2.alloc("mga", [128, 16, TB], BF16)
        mgb = R2.alloc("mgb", [128, 16, TB], BF16)
        dma(oTt.all(), oT_d.v(oT_d.ap.rearrange("(h p) t -> p h t", p=128)))

        def cons_ya(tag, pb):
            c = tag
            dma(gt.all(), g_d.v(g_d.ap[c * 128:(c + 1) * 128, :]))
            tt(mga.c(c), pb.all(), gt.all(), ALU.mult)

        dense(W["w_o_mla"], l, 0, 16, lambda kc: oTt.c(kc), [(c * 128, 128, c) for c in range(16)], cons_ya)

        def cons_yb(tag, pb):
            c = tag
            dma(gt.all(), g_d.v(g_d.ap[(16 + c) * 128:(17 + c) * 128, :]))
            t = next_tmp()
            tt(t.all(), pb.all(), gt.all(), ALU.mult)
            tt(mgb.c(c), t.all(), mga.c(c), ALU.add)

        dense(W["w_o_ssm"], l, 0, 32, lambda kc: ybT.c(kc), [(c * 128, 128, c) for c in range(16)], cons_yb)

        def cons_res(tag, pb):
            tt(X.c(tag), X.c(tag), pb.all(), ALU.add)

        dense(W["w_out"], l, 0, 16, lambda kc: mgb.c(kc), [(c * 128, 128, c) for c in range(16)], cons_res)

        if STOP == 'E':
            return
        R1.reset()
        R2.reset()
        uT = R1.alloc("uT", [128, 32, TB], BF16)
        sqt = R2.alloc("sq", [128, 16, TB], BF16)
        hT = R2.alloc("hT", [128, 16, TB], BF16)
        R2sq[0] = sqt
        rmsnorm_big(X, nlw, hT, 16, D_MODEL)
        for half in range(2):
            def cons_up(tag, pb):
                t = next_tmp()
                act(t.all(), pb.all(), AF.Relu)
                tt(uT.c(tag), t.all(), t.all(), ALU.mult)

            dense(W["w_up"], l, 0, 16, lambda kc: hT.c(kc),
                  [(half * 4096 + c * 128, 128, c) for c in range(32)], cons_up)
            dense(W["w_down"], l, half * 4096, 32, lambda kc: uT.c(kc), [(c * 128, 128, c) for c in range(16)], cons_res)

        if STOP == 'F':
            return
        R1.reset()
        eT = R1.alloc("eT", [128, 16, TB], F32)
        pst = R2.alloc("pst", [128, PLE_DIM], F32)
        pTb = R2.alloc("pTb", [128, 2, TB], BF16)
        rmsnorm_big(X, npw, hT, 16, D_MODEL)
        for j in range(4):
            dma(pst.all(), p_d.v(p_d.ap[l, t0 + j * 128:t0 + (j + 1) * 128, :]))
            for c in range(2):
                tr(ps[6].s(c * 128, (c + 1) * 128), pst.s(c * 128, (c + 1) * 128), ident_f.all())
            vcopy(pTb.v(pTb.t[:, :, j * 128:(j + 1) * 128]), ps[6].v(ps[6].t[:, 0:256].rearrange("p (c t) -> p c t", t=128)))

        def cons_e(tag, pb):
            act(eT.c(tag), pb.all(), AF.Copy)

        dense(W["w_ple"], l, 0, 2, lambda kc: pTb.c(kc), [(c * 128, 128, c) for c in range(16)], cons_e)

        def cons_pg(tag, pb):
            t = next_tmp()
            act(t.all(), pb.all(), AF.Sigmoid)
            tt(t.all(), t.all(), eT.c(tag), ALU.mult)
            tt(X.c(tag), X.c(tag), t.all(), ALU.add)

        dense(W["w_ple_gate"], l, 0, 16, lambda kc: hT.c(kc), [(c * 128, 128, c) for c in range(16)], cons_pg)

        if STOP == 'G':
            return
        if not last:
            dma(xT_d.v(xT_d.ap[:, t0:t0 + TB].rearrange("(kc p) t -> p kc t", p=128)), X.all())
        else:
            R2.reset()
            ost = R2.alloc("ost", [128, D_MODEL], F32)
            for j in range(4):
                for q in range(4):
                    pb = next_pmm()
                    for i in range(4):
                        kc = q * 4 + i
                        tr(pb.s(i * 128, (i + 1) * 128), X.c(kc, j * 128, (j + 1) * 128), ident_f.all())
                    vcopy(ost.s(q * 512, (q + 1) * 512), pb.all())
                fin_ops.append(dma(out_d.v(out_d.ap[t0 + j * 128:t0 + (j + 1) * 128, :]), ost.all()))

    setup_consts()
    for l in range(L):
        if STOP == 'setup':
            break
        P.epoch = l
        layer_consts(l)
        if STOP == 'lconsts':
            break
        memset(hS.all(), 0.0)
        for tb in range(NBLK):
            block(l, tb)
    P.emit(final_waits=fin_ops)
    return nc, len(P.ops)


_CACHE = {}


def _invf_table():
    j = np.arange(0, QK_ROPE, 2, dtype=np.float32) / np.float32(QK_ROPE)
    inv = (np.float32(1.0) / (np.float32(10000.0) ** j)).astype(np.float32)
    t = np.zeros((1, 128), np.float32)
    t[0, 0:32] = inv
    t[0, 32:64] = inv
    return t


def kernel(**inputs):
    x = np.ascontiguousarray(inputs["x"], dtype=np.float32)
    B, S, _ = x.shape
    L = inputs["w_in"].shape[0]
    key = (L, S)
    if key not in _CACHE:
        _CACHE[key] = build(L=L, S=S)[0]
    nc = _CACHE[key]
    in_maps = []
    for b in range(B):
        m = {"x": x[b], "p": np.ascontiguousarray(inputs["p"][:, b]),
             "positions": np.ascontiguousarray(inputs["positions"][b:b + 1]).astype(np.int32),
             "invf": _invf_table()}
        for n in WNAMES:
            m[n] = np.ascontiguousarray(inputs[n], dtype=np.float32)
        in_maps.append(m)
    res = run_bass_kernel_spmd(nc, in_maps, core_ids=list(range(B)))
    return np.stack([res.results[b]["out"] for b in range(B)], axis=0).astype(np.float32)
```

```python
import math
import numpy as np
import concourse.bass as bass
import concourse.mybir as mybir
from concourse.bass_utils import run_bass_kernel_spmd

F32 = mybir.dt.float32
BF16 = mybir.dt.bfloat16
I32 = mybir.dt.int32
AF = mybir.ActivationFunctionType
ALU = mybir.AluOpType
PG = 512
NSLOT = 8
DTSZ = {F32: 4, BF16: 2, I32: 4}

D_MODEL = 2048
SEQ = 4096
DEPTH = 4
EPS = 1e-6
HEADS = 16
Q_LORA = 512
KV_LORA = 512
QK_NOPE = 128
QK_ROPE = 64
V_DIM = 128
QK_DIM = 192
D_INNER = 4096
SSM_HEADS = 64
SSM_GROUPS = 8
SSM_STATE = 128
CONV_DIM = 6144
D_FF = 8192
PLE_DIM = 256
D_IN_PROJ = 15488
TB = 512
STOP = None
SM_SCALE = QK_DIM ** -0.5
SM_SHIFT = 14.0


class Page:
    __slots__ = ("w", "r")

    def __init__(self):
        self.w = {}
        self.r = {}


class V:
    __slots__ = ("ap", "pages", "excl")

    def __init__(self, ap, pages, excl=False):
        self.ap = ap
        self.pages = pages
        self.excl = excl


class Tile:
    def __init__(self, prog, name, shape, dtype, space="sbuf", off=None):
        self.name, self.shape, self.dtype = name, list(shape), dtype
        nc = prog.nc
        self.esz = DTSZ[dtype]
        self.nbytes = int(np.prod(shape[1:])) * self.esz
        if space == "sbuf":
            assert off is not None and off % 32 == 0
            assert off + self.nbytes <= prog.sb_limit, f"SBUF overflow at {name}: {off + self.nbytes}"
            self.off = off
            self.t = nc.alloc_sbuf_tensor_at(name, list(shape), dtype, offset=off)
            self.pg = prog.sb_pages
        else:
            self.t = nc.alloc_psum_tensor(name, list(shape), dtype)
            self.off = 0
            self.pg = [Page()]
        self.excl = (space != "sbuf")
        self._cache = {}

    def _pages(self, lo, hi):
        if self.excl:
            return self.pg
        a, b = (self.off + lo) // PG, (self.off + hi - 1) // PG
        k = (a, b)
        p = self._cache.get(k)
        if p is None:
            p = self.pg[a:b + 1]
            self._cache[k] = p
        return p

    def all(self):
        return V(self.t[:], self._pages(0, self.nbytes), self.excl)

    def v(self, ap, lo=None, hi=None):
        lo = 0 if lo is None else lo * self.esz
        hi = self.nbytes if hi is None else hi * self.esz
        return V(ap, self._pages(lo, hi), self.excl)

    def c(self, i, a=None, b=None, p0=0, p1=None):
        n = self.shape[2]
        a = 0 if a is None else a
        b = n if b is None else b
        p1 = self.shape[0] if p1 is None else p1
        return V(self.t[p0:p1, i, a:b], self._pages((i * n + a) * self.esz, (i * n + b) * self.esz), self.excl)

    def s(self, a=None, b=None, p0=0, p1=None):
        a = 0 if a is None else a
        b = self.shape[1] if b is None else b
        p1 = self.shape[0] if p1 is None else p1
        return V(self.t[p0:p1, a:b], self._pages(a * self.esz, b * self.esz), self.excl)


class Dram:
    def __init__(self, ap):
        self.ap = ap
        self.pg = [Page()]

    def v(self, ap=None):
        return V(self.ap if ap is None else ap, self.pg)


class Op:
    __slots__ = ("eng", "fn", "is_dma", "tl", "cnt", "waits", "need_inc", "snap", "val")


class Prog:
    def __init__(self, nc, same_engine_sync=True, sb_base=20992, sb_limit=229376):
        self.nc = nc
        self.ops = []
        self.same_engine_sync = same_engine_sync
        self.sb_base = sb_base
        self.sb_limit = sb_limit
        self.sb_pages = [Page() for _ in range(sb_limit // PG + 2)]
        self.known = {e: {} for e in ("tensor", "vector", "scalar", "gpsimd", "sync")}
        self.tl_count = {}
        self.tl_last = {}
        self.dma_rr = {"sync": 0, "gpsimd": 0, "scalar": 0}
        self.epoch = 0

    def op(self, eng, fn, reads=(), writes=(), dma=False):
        o = Op()
        o.eng, o.fn, o.is_dma, o.need_inc = eng, fn, dma, False
        if dma:
            slot = self.dma_rr[eng]
            self.dma_rr[eng] = (slot + 1) % NSLOT
            o.tl = ("dma", eng, slot, self.epoch)
            lastkey = ("dma", eng, slot)
        else:
            o.tl = ("eng", eng, self.epoch)
        xr = [v for v in reads if v.excl]
        if xr:
            reads = [v for v in reads if not v.excl]
            writes = list(writes) + xr
        k = self.known[eng]
        need = {}
        own = ("eng", eng)
        skip_own = (not dma) and (eng == "tensor" or not self.same_engine_sync)

        def consider(d):
            tl = d.tl
            if skip_own and tl[:2] == own:
                return
            if k.get(tl, 0) >= d.cnt:
                return
            cur = need.get(tl)
            if cur is None or cur.cnt < d.cnt:
                need[tl] = d

        if dma:
            prev = self.tl_last.get(lastkey)
            if prev is not None:
                consider(prev)
        for v in reads:
            for p in v.pages:
                for d in p.w.values():
                    consider(d)
        for v in writes:
            for p in v.pages:
                for d in p.w.values():
                    consider(d)
                for d in p.r.values():
                    consider(d)
        wl = []
        for tl, d in need.items():
            if k.get(tl, 0) >= d.cnt:
                continue
            wl.append(d)
            d.need_inc = True
            k[tl] = d.cnt
            for t2, c2 in d.snap.items():
                if k.get(t2, 0) < c2:
                    k[t2] = c2
        o.waits = wl
        c = self.tl_count.get(o.tl, 0) + 1
        self.tl_count[o.tl] = c
        o.cnt = c
        if dma:
            self.tl_last[lastkey] = o
        o.snap = dict(k)
        for v in reads:
            for p in v.pages:
                p.r[o.tl] = o
        for v in writes:
            for p in v.pages:
                if p.r:
                    p.r = {}
                    p.w = {o.tl: o}
                else:
                    p.w[o.tl] = o
        self.ops.append(o)
        return o

    def dma(self, eng, out, in_, **kw):
        oa, ia = out.ap, in_.ap
        return self.op(eng, lambda e: e.dma_start(out=oa, in_=ia, **kw), [in_], [out], dma=True)

    def emit(self, final_waits=()):
        nc = self.nc
        for d in final_waits:
            d.need_inc = True
        tl_val = {}
        for o in self.ops:
            if o.is_dma:
                o.val = tl_val[o.tl] = tl_val.get(o.tl, 0) + 16
            elif o.need_inc:
                o.val = tl_val[o.tl] = tl_val.get(o.tl, 0) + 1
        sems = {tl: nc.alloc_semaphore("s_" + "_".join(str(x) for x in tl)) for tl in tl_val}
        per_eng = {e: [] for e in self.known}
        for o in self.ops:
            per_eng[o.eng].append(o)

        def run(engname, e):
            for o in per_eng[engname]:
                for d in o.waits:
                    e.wait_ge(sems[d.tl], d.val)
                ins = o.fn(e)
                if o.is_dma:
                    ins.then_inc(sems[o.tl], 16)
                elif o.need_inc:
                    ins.then_inc(sems[o.tl], 1)
            if engname == "sync":
                for d in final_waits:
                    e.wait_ge(sems[d.tl], d.val)

        with nc.Block() as block:
            @block.sync
            def _(e):
                run("sync", e)

            @block.scalar
            def _(e):
                run("scalar", e)

            @block.vector
            def _(e):
                run("vector", e)

            @block.gpsimd
            def _(e):
                run("gpsimd", e)

            @block.tensor
            def _(e):
                run("tensor", e)


WNAMES = ["norm_mix_w", "w_in", "q_a_norm_w", "w_uq", "kv_a_norm_w", "w_ukv", "q_norm_w", "k_norm_w",
          "w_o_mla", "conv_w", "conv_b", "dt_bias", "a_log", "d_skip", "ssm_norm_w", "w_o_ssm", "w_out",
          "norm_mlp_w", "w_up", "w_down", "ple_norm_w", "w_ple_gate", "w_ple"]
WSHAPES = {
    "norm_mix_w": [D_MODEL], "w_in": [D_MODEL, D_IN_PROJ], "q_a_norm_w": [Q_LORA], "w_uq": [Q_LORA, HEADS * QK_DIM],
    "kv_a_norm_w": [KV_LORA], "w_ukv": [KV_LORA, HEADS * 256], "q_norm_w": [QK_DIM], "k_norm_w": [QK_DIM],
    "w_o_mla": [HEADS * V_DIM, D_MODEL], "conv_w": [4, CONV_DIM], "conv_b": [CONV_DIM], "dt_bias": [SSM_HEADS],
    "a_log": [SSM_HEADS], "d_skip": [SSM_HEADS], "ssm_norm_w": [D_INNER], "w_o_ssm": [D_INNER, D_MODEL],
    "w_out": [D_MODEL, D_MODEL], "norm_mlp_w": [D_MODEL], "w_up": [D_MODEL, D_FF], "w_down": [D_FF, D_MODEL],
    "ple_norm_w": [D_MODEL], "w_ple_gate": [D_MODEL, D_MODEL], "w_ple": [PLE_DIM, D_MODEL],
}


SAME_ENGINE_SYNC = True


def build(L=DEPTH, S=SEQ, dbg=False):
    NBLK = S // TB
    nc = bass.Bass("TRN2", target_bir_lowering=False)
    P = Prog(nc, same_engine_sync=SAME_ENGINE_SYNC)

    def din(name, shape, dt=F32):
        return Dram(nc.dram_tensor(name, shape, dt, kind="ExternalInput").ap())

    def dscr(name, shape, dt=F32):
        return Dram(nc.dram_tensor(name, shape, dt).ap())

    x_d = din("x", [S, D_MODEL])
    p_d = din("p", [L, S, PLE_DIM])
    pos_d = din("positions", [1, S], I32)
    invf_d = din("invf", [1, 128])
    W = {n: din(n, [L] + WSHAPES[n]) for n in WNAMES}
    out_d = Dram(nc.dram_tensor("out", [S, D_MODEL], F32, kind="ExternalOutput").ap())
    dbg_outs = {}

    xT_d = dscr("xT_s", [D_MODEL, S])
    xbc_d = dscr("xbc_s", [CONV_DIM, 3 + S])
    zs_d = dscr("zs_s", [D_INNER, TB])
    g_d = dscr("g_s", [2 * D_MODEL, TB])
    knT_d = dscr("knT_s", [HEADS, 128, S], BF16)
    krT_d = dscr("krT_s", [HEADS, 64, S], BF16)
    v_d = dscr("v_s", [S, HEADS * V_DIM], BF16)
    oT_d = dscr("oT_s", [HEADS * V_DIM, TB], BF16)
    cos_d = dscr("cos_s", [64, S])
    sin_d = dscr("sin_s", [64, S])

    cur = [P.sb_base]

    def alloc(name, shape, dt, at=None):
        if at is None:
            off = (cur[0] + PG - 1) // PG * PG
            t = Tile(P, name, shape, dt, off=off)
            cur[0] = off + t.nbytes
        else:
            t = Tile(P, name, shape, dt, off=at)
        return t

    class Region:
        def __init__(self, base, size):
            self.base, self.size, self.cur = base, size, base

        def reset(self):
            self.cur = self.base

        def alloc(self, name, shape, dt):
            off = (self.cur + PG - 1) // PG * PG
            t = Tile(P, name, shape, dt, off=off)
            self.cur = off + t.nbytes
            assert self.cur <= self.base + self.size, f"region overflow {name} {self.cur - self.base} > {self.size}"
            return t

    ident_f = alloc("ident_f", [128, 128], F32)
    ident_b = alloc("ident_b", [128, 128], BF16)
    ones_b = alloc("ones_b", [128, 128], BF16)
    ones_f = alloc("ones_f", [128, 128], F32)
    tri = alloc("tri", [128, 128], F32)
    negmask = alloc("negmask", [128, 128], F32)
    sel_last = alloc("sel_last", [128, 128], F32)
    sel_c0 = alloc("sel_c0", [128, 128], F32)
    sel_c1 = alloc("sel_c1", [128, 128], F32)
    rrot_b = alloc("rrot_b", [128, 128], BF16)
    invf = alloc("invf", [128, 1], F32)
    nmw = alloc("nmw", [128, 16], F32)
    nlw = alloc("nlw", [128, 16], F32)
    npw = alloc("npw", [128, 16], F32)
    qaw = alloc("qaw", [128, 4], F32)
    kvaw = alloc("kvaw", [128, 4], F32)
    qnw = alloc("qnw", [128, 2], F32)
    knw = alloc("knw", [128, 2], F32)
    convw = alloc("convw", [128, 4, 48], F32)
    convb = alloc("convb", [128, 48], F32)
    ssmw = alloc("ssmw", [128, 32], F32)
    dtb = alloc("dtb", [128, 1], F32)
    A_bc = alloc("A_bc", [128, 64], F32)
    D_bc = alloc("D_bc", [128, 64], F32)
    X = alloc("X", [128, 16, TB], F32)
    hS = alloc("hS", [128, 8, 512], F32)
    wbuf = [alloc(f"wbuf{i}", [128, 16 * 512], BF16) for i in range(2)]
    cosb = alloc("cosb", [128, TB], F32)
    sinb = alloc("sinb", [128, TB], F32)
    dtraw = alloc("dtraw", [128, TB], F32)
    dtvT = alloc("dtvT", [128, TB], F32)
    gt = alloc("gt", [128, TB], F32)
    rstd = alloc("rstd", [128, TB], F32)
    tmpA = alloc("tmpA", [128, TB], F32)
    tmpB = alloc("tmpB", [128, TB], F32)
    tmpC = alloc("tmpC", [128, TB], F32)
    base1 = (cur[0] + PG - 1) // PG * PG
    R1 = Region(base1, 32 * 1024)
    R2 = Region(base1 + 32 * 1024, P.sb_limit - (base1 + 32 * 1024))

    ps = [Tile(P, f"psb{i}", [128, 512], F32, space="psum") for i in range(8)]
    pmm_rr = [0]

    def next_pmm():
        b = ps[pmm_rr[0] % 4]
        pmm_rr[0] += 1
        return b

    tmp_rr = [0]

    def next_tmp():
        t = (tmpA, tmpB, tmpC)[tmp_rr[0] % 3]
        tmp_rr[0] += 1
        return t

    def mm(out, lhsT, rhs, start=True, stop=True):
        oa, la, ra = out.ap, lhsT.ap, rhs.ap
        return P.op("tensor", lambda e: e.matmul(oa, la, ra, start=start, stop=stop), [lhsT, rhs], [out])

    def tr(out, in_, ident):
        oa, ia, da = out.ap, in_.ap, ident.ap
        return P.op("tensor", lambda e: e.transpose(oa, ia, da), [in_, ident], [out])

    def act(out, in_, func, bias=None, scale=1.0, eng="scalar"):
        oa, ia = out.ap, in_.ap
        rd = [in_]
        kw = {}
        if isinstance(bias, V):
            rd.append(bias)
            kw["bias"] = bias.ap
        elif bias is not None:
            kw["bias"] = float(bias)
        return P.op("scalar", lambda e: e.activation(out=oa, in_=ia, func=func, scale=scale, **kw), rd, [out])

    def vcopy(out, in_, eng="vector"):
        oa, ia = out.ap, in_.ap
        return P.op(eng, lambda e: e.tensor_copy(oa, ia), [in_], [out])

    def tt(out, a, b, op, eng="vector"):
        oa, aa, ba = out.ap, a.ap, b.ap
        return P.op(eng, lambda e: e.tensor_tensor(oa, aa, ba, op), [a, b], [out])

    def ts(out, a, s1, s2, op0, op1=None, eng="vector"):
        oa, aa = out.ap, a.ap
        rd = [a]
        s1a = s1
        if isinstance(s1, V):
            rd.append(s1)
            s1a = s1.ap
        s2a = s2
        if isinstance(s2, V):
            rd.append(s2)
            s2a = s2.ap
        if op1 is None:
            return P.op(eng, lambda e: e.tensor_scalar(oa, aa, s1a, s2a, op0), rd, [out])
        return P.op(eng, lambda e: e.tensor_scalar(oa, aa, s1a, s2a, op0, op1), rd, [out])

    def stt(out, a, sc, b, op0, op1, eng="vector"):
        oa, aa, ba = out.ap, a.ap, b.ap
        rd = [a, b]
        sa = sc
        if isinstance(sc, V):
            rd.append(sc)
            sa = sc.ap
        return P.op(eng, lambda e: e.scalar_tensor_tensor(oa, aa, sa, ba, op0, op1), rd, [out])

    def memset(out, val, eng="vector"):
        oa = out.ap
        return P.op(eng, lambda e: e.memset(oa, val), [], [out])

    def recip(out, in_):
        oa, ia = out.ap, in_.ap
        return P.op("vector", lambda e: e.reciprocal(oa, ia), [in_], [out])

    def dma(out, in_, eng="sync"):
        return P.dma(eng, out, in_)

    def bc_mid(v, tile, n):
        ap = v.ap
        return V(ap.unsqueeze(1).to_broadcast([ap.shape[0], n, ap.shape[1]]), v.pages)

    def bc_last(v, n):
        ap = v.ap
        return V(ap.unsqueeze(2).to_broadcast([ap.shape[0], ap.shape[1], n]), v.pages)

    def setup_consts():
        io = R2.alloc("c_io", [128, 128], F32)
        ip = R2.alloc("c_ip", [128, 1], F32)
        jh = R2.alloc("c_jh", [128, 128], F32)
        ph = R2.alloc("c_ph", [128, 1], F32)
        t1 = R2.alloc("c_t1", [128, 128], F32)
        t2 = R2.alloc("c_t2", [128, 128], F32)
        P.op("gpsimd", lambda e: e.iota(io.t[:], pattern=[[1, 128]], base=0, channel_multiplier=0,
                                        allow_small_or_imprecise_dtypes=True), [], [io.all()])
        P.op("gpsimd", lambda e: e.iota(ip.t[:], pattern=[[0, 1]], base=0, channel_multiplier=1,
                                        allow_small_or_imprecise_dtypes=True), [], [ip.all()])
        ts(ident_f.all(), io.all(), ip.all(), None, ALU.is_equal)
        vcopy(ident_b.all(), ident_f.all())
        memset(ones_b.all(), 1.0)
        memset(ones_f.all(), 1.0)
        ts(jh.all(), io.all(), 64.0, None, ALU.is_ge)
        ts(ph.all(), ip.all(), 64.0, None, ALU.is_ge)
        ts(t1.all(), io.all(), ip.all(), None, ALU.is_ge)
        ts(t2.all(), jh.all(), ph.all(), None, ALU.is_equal)
        tt(tri.all(), t1.all(), t2.all(), ALU.mult)
        ts(negmask.all(), tri.all(), -1.0, 30000.0, ALU.add, ALU.mult)
        ts(t1.all(), jh.all(), 64.0, 63.0, ALU.mult, ALU.add)
        ts(sel_last.all(), t1.all(), ip.all(), None, ALU.is_equal)
        memset(t1.all(), 63.0)
        ts(sel_c0.all(), t1.all(), ip.all(), None, ALU.is_equal)
        memset(t1.all(), 127.0)
        ts(sel_c1.all(), t1.all(), ip.all(), None, ALU.is_equal)
        ts(t1.all(), io.all(), -32.0, None, ALU.add)
        ts(t1.all(), t1.all(), ip.all(), None, ALU.is_equal)
        ts(t2.all(), io.all(), 32.0, None, ALU.add)
        ts(t2.all(), t2.all(), ip.all(), None, ALU.is_equal)
        tt(t1.all(), t1.all(), t2.all(), ALU.subtract)
        vcopy(rrot_b.all(), t1.all())
        memset(dtraw.all(), 0.0)
        memset(hS.all(), 0.0)
        st = R2.alloc("c_st", [128, 128], F32)
        memset(st.all(), 0.0)
        dma(st.s(0, 128, 0, 1), invf_d.v())
        tr(ps[6].s(0, 128), st.all(), ident_f.all())
        vcopy(invf.all(), ps[6].s(0, 1))
        memset(tmpA.all(), 0.0)
        for j in range(CONV_DIM // 128):
            dma(xbc_d.v(xbc_d.ap[j * 128:(j + 1) * 128, 0:3]), tmpA.s(0, 3))
        R2.reset()
        TWO_PI = 2.0 * math.pi
        C1 = 6.28125
        C2 = TWO_PI - C1
        MAGIC = 12582912.0
        for c0 in range(0, S, 2048):
            n = min(2048, S - c0)
            pi_ = R2.alloc("r_pi", [64, 2048], I32)
            ang = R2.alloc("r_ang", [64, 2048], F32)
            nn = R2.alloc("r_n", [64, 2048], F32)
            rr = R2.alloc("r_r", [64, 2048], F32)
            dma(pi_.s(0, n), pos_d.v(pos_d.ap[0, c0:c0 + n].partition_broadcast(64)))
            vcopy(ang.s(0, n), pi_.s(0, n))
            ts(ang.s(0, n), ang.s(0, n), invf.s(0, 1, 0, 64), None, ALU.mult)
            ts(nn.s(0, n), ang.s(0, n), 1.0 / TWO_PI, MAGIC, ALU.mult, ALU.add)
            ts(nn.s(0, n), nn.s(0, n), -MAGIC, None, ALU.add)
            stt(rr.s(0, n), nn.s(0, n), -C1, ang.s(0, n), ALU.mult, ALU.add)
            stt(rr.s(0, n), nn.s(0, n), -C2, rr.s(0, n), ALU.mult, ALU.add)
            ts(rr.s(0, n), rr.s(0, n), 3.1415925, -3.1415925, ALU.min, ALU.max)
            act(ang.s(0, n), rr.s(0, n), AF.Sin)
            dma(sin_d.v(sin_d.ap[:, c0:c0 + n]), ang.s(0, n))
            ts(nn.s(0, n), rr.s(0, n), -1.0, None, ALU.mult)
            tt(nn.s(0, n), nn.s(0, n), rr.s(0, n), ALU.max)
            act(ang.s(0, n), nn.s(0, n), AF.Sin, bias=math.pi / 2, scale=-1.0)
            dma(cos_d.v(cos_d.ap[:, c0:c0 + n]), ang.s(0, n))
            R2.reset()

    def load_cols(dst_v, src_ap_rows, nrows, ncols=128):
        st = R2.alloc("lc_st", [128, 128], F32)
        memset(st.all(), 0.0)
        dma(st.s(0, ncols, 0, nrows), Dram(src_ap_rows).v())
        tr(ps[6].s(0, 128), st.all(), ident_f.all())
        vcopy(dst_v, ps[6].s(0, nrows))
        R2.cur -= 0

    def layer_consts(l):
        R2.reset()
        for (t, nm, n) in ((nmw, "norm_mix_w", 16), (nlw, "norm_mlp_w", 16), (npw, "ple_norm_w", 16),
                           (qaw, "q_a_norm_w", 4), (kvaw, "kv_a_norm_w", 4), (ssmw, "ssm_norm_w", 32),
                           (convb, "conv_b", 48)):
            load_cols(t.all(), W[nm].ap[l].rearrange("(r c) -> r c", c=128), n)
            R2.reset()
        for tap in range(4):
            load_cols(convw.c(tap), W["conv_w"].ap[l, tap].rearrange("(r c) -> r c", c=128), 48)
            R2.reset()
        for (t, nm) in ((qnw, "q_norm_w"), (knw, "k_norm_w")):
            load_cols(t.s(0, 1), W[nm].ap[l, 0:128].rearrange("(r c) -> r c", c=128), 1)
            R2.reset()
            load_cols(t.s(1, 2), W[nm].ap[l, 128:192].rearrange("(r c) -> r c", c=64), 1, ncols=64)
            R2.reset()
        load_cols(dtb.all(), W["dt_bias"].ap[l].rearrange("(r c) -> r c", c=64), 1, ncols=64)
        R2.reset()
        dma(A_bc.all(), W["a_log"].v(W["a_log"].ap[l].partition_broadcast(128)))
        act(A_bc.all(), A_bc.all(), AF.Exp)
        ts(A_bc.all(), A_bc.all(), -1.0, None, ALU.mult)
        dma(D_bc.all(), W["d_skip"].v(W["d_skip"].ap[l].partition_broadcast(128)))

    wb_rr = [0]

    def dense(Wd, l, row0, KC, in_chunk, segs, consumer, gcols=None):
        if gcols is None:
            gcols = min(512, 8192 // KC)
        groups = []
        for sg in segs:
            c0, n, tag = sg
            if groups and groups[-1][0] + groups[-1][1] == c0 and (groups[-1][1] + n) <= gcols:
                groups[-1][1] += n
                groups[-1][2].append(sg)
            else:
                groups.append([c0, n, [sg]])
        for (g0, gw, sgs) in groups:
            wb = wbuf[wb_rr[0] % 2]
            wb_rr[0] += 1
            src = Wd.ap[l, row0:row0 + KC * 128, g0:g0 + gw].rearrange("(kc p) f -> p kc f", p=128)
            dst_ap = wb.t[:, 0:KC * gw].rearrange("p (kc f) -> p kc f", f=gw)
            dstv = wb.v(dst_ap, 0, KC * gw)
            P.dma("gpsimd", dstv, Wd.v(src))
            for (c0, n, tag) in sgs:
                pb = next_pmm()
                o = c0 - g0
                for kc in range(KC):
                    lw = wb.v(wb.t[:, kc * gw + o: kc * gw + o + n], kc * gw + o, kc * gw + o + n)
                    mm(pb.s(0, TB, 0, n), lw, in_chunk(kc), start=(kc == 0), stop=(kc == KC - 1))
                consumer(tag, pb)

    def rmsnorm_big(src, wcols, dst, nch, D):
        sqt = R2sq[0]
        for kc in range(nch):
            act(sqt.c(kc), src.c(kc), AF.Square)
        for kc in range(nch):
            mm(ps[7].all(), ones_b.all(), sqt.c(kc), start=(kc == 0), stop=(kc == nch - 1))
        act(rstd.all(), ps[7].all(), AF.Sqrt, bias=EPS, scale=1.0 / D)
        recip(rstd.all(), rstd.all())
        for kc in range(nch):
            stt(dst.c(kc), src.c(kc), wcols.s(kc, kc + 1), rstd.all(), ALU.mult, ALU.mult)

    R2sq = [None]
    fin_ops = []

    def rope(dst_bf, src_f32, tmp_b, tmp_f, pbank=None):
        pbank = ps[6] if pbank is None else pbank
        vcopy(tmp_b, src_f32)
        mm(pbank.s(0, TB, 0, 64), rrot_b.s(0, 64, 0, 64), tmp_b)
        tt(tmp_f, pbank.s(0, TB, 0, 64), sinb.s(0, TB, 0, 64), ALU.mult)
        tt(src_f32, src_f32, cosb.s(0, TB, 0, 64), ALU.mult)
        tt(dst_bf, src_f32, tmp_f, ALU.add)

    def block(l, tb):
        t0 = tb * TB
        last = (l == L - 1)
        R1.reset()
        R2.reset()
        if l == 0:
            stg = R2.alloc("xs", [128, D_MODEL], F32)
            for j in range(4):
                dma(stg.all(), x_d.v(x_d.ap[t0 + j * 128: t0 + (j + 1) * 128, :]))
                for q in range(4):
                    pb = next_pmm()
                    for i in range(4):
                        kc = q * 4 + i
                        tr(pb.s(i * 128, (i + 1) * 128), stg.s(kc * 128, (kc + 1) * 128), ident_f.all())
                    dst = X.v(X.t[:, q * 4:(q + 1) * 4, j * 128:(j + 1) * 128], q * 4 * TB, (q + 1) * 4 * TB)
                    src = pb.v(pb.t[:, :].rearrange("p (c t) -> p c t", t=128))
                    vcopy(dst, src, eng="scalar" if False else "vector")
            R2.reset()
        else:
            dma(X.all(), xT_d.v(xT_d.ap[:, t0:t0 + TB].rearrange("(kc p) t -> p kc t", p=128)))
        dma(cosb.s(0, TB, 0, 64), cos_d.v(cos_d.ap[:, t0:t0 + TB]))
        dma(sinb.s(0, TB, 0, 64), sin_d.v(sin_d.ap[:, t0:t0 + TB]))

        if STOP == 'load':
            return
        sqt = R1.alloc("sq", [128, 16, TB], BF16)
        hT = R1.alloc("hT", [128, 16, TB], BF16)
        R2sq[0] = sqt
        cq = R2.alloc("cq", [128, 4, TB], F32)
        ckv = R2.alloc("ckv", [128, 4, TB], F32)
        kr = R2.alloc("kr", [128, TB], F32)
        rmsnorm_big(X, nmw, hT, 16, D_MODEL)
        segs = []
        for j in range(4):
            segs.append((j * 128, 128, ("cq", j)))
        for j in range(4):
            segs.append((512 + j * 128, 128, ("ckv", j)))
        segs.append((1024, 64, ("kr", 0)))
        for j in range(32):
            segs.append((1088 + j * 128, 128, ("z", j)))
        for j in range(48):
            segs.append((5184 + j * 128, 128, ("xbc", j)))
        segs.append((11328, 64, ("dt", 0)))
        for j in range(32):
            segs.append((11392 + j * 128, 128, ("g", j)))

        def cons_in(tag, pb):
            kind, j = tag
            if kind == "cq":
                act(cq.c(j), pb.all(), AF.Copy)
            elif kind == "ckv":
                act(ckv.c(j), pb.all(), AF.Copy)
            elif kind == "kr":
                act(kr.s(0, TB, 0, 64), pb.s(0, TB, 0, 64), AF.Copy)
            elif kind == "dt":
                vcopy(dtraw.s(0, TB, 0, 64), pb.s(0, TB, 0, 64))
            elif kind == "z":
                t = next_tmp()
                act(t.all(), pb.all(), AF.Silu)
                dma(zs_d.v(zs_d.ap[j * 128:(j + 1) * 128, :]), t.all())
            elif kind == "xbc":
                t = next_tmp()
                vcopy(t.all(), pb.all())
                dma(xbc_d.v(xbc_d.ap[j * 128:(j + 1) * 128, 3 + t0:3 + t0 + TB]), t.all())
            elif kind == "g":
                t = next_tmp()
                act(t.all(), pb.all(), AF.Sigmoid)
                dma(g_d.v(g_d.ap[j * 128:(j + 1) * 128, :]), t.all())

        dense(W["w_in"], l, 0, 16, lambda kc: hT.c(kc), segs, cons_in)

        if STOP == 'A':
            return
        R1.reset()
        QnT = R1.alloc("QnT", [128, 16, TB], BF16)
        QrT = R1.alloc("QrT", [128, 16, TB], BF16)
        cqn = R2.alloc("cqn", [128, 4, TB], BF16)
        ckvn = R2.alloc("ckvn", [128, 4, TB], BF16)
        sq4 = R2.alloc("sq4", [128, 4, TB], BF16)
        R2sq[0] = sq4
        hf = [R2.alloc(f"hf{i}", [128, TB], F32) for i in range(2)]
        hsets = []
        for i, (pa, pb_) in enumerate(((ps[6], ps[7]), (ps[4], ps[5]))):
            hsets.append(dict(
                hr=R2.alloc(f"hr{i}", [128, TB], F32), vf=R2.alloc(f"vf{i}", [128, TB], F32),
                tb16=R2.alloc(f"tb16{i}", [128, TB], BF16), tf32=R2.alloc(f"tf32{i}", [128, TB], F32),
                kob=R2.alloc(f"kob{i}", [128, TB], BF16), krb=R2.alloc(f"krb{i}", [128, TB], BF16),
                vtm=R2.alloc(f"vtm{i}", [128, 4, 128], BF16), rstd=R2.alloc(f"rstdh{i}", [128, TB], F32), pA=pa, pB=pb_))
        kr2 = R2.alloc("kr2", [128, TB], BF16)

        rmsnorm_big(cq, qaw, cqn, 4, Q_LORA)
        hf_rr = [0]
        cur_hf = [None]

        def head_stats(nope_v, rope_sq_v, rope_src_v, hs):
            sq = next_tmp()
            pB, rs = hs["pB"], hs["rstd"]
            sqb = V(sq.t[:, :].bitcast(BF16)[:, 0:TB], sq._pages(0, TB * 2))
            act(sqb, nope_v, AF.Square)
            mm(pB.all(), ones_b.all(), sqb, start=True, stop=False)
            if rope_sq_v is None:
                sq2 = V(sq.t[0:64, :].bitcast(BF16)[:, TB:2 * TB], sq._pages(TB * 2, TB * 4))
                act(sq2, rope_src_v, AF.Square)
                rope_sq_v = sq2
            mm(pB.all(), ones_b.s(0, 128, 0, 64), rope_sq_v, start=False, stop=True)
            act(rs.all(), pB.all(), AF.Sqrt, bias=EPS, scale=1.0 / QK_DIM)
            recip(rs.all(), rs.all())

        def cons_q(tag, pb):
            kind, h = tag
            hs = hsets[h % 2]
            hr, rs = hs["hr"], hs["rstd"]
            if kind == "qn":
                cur_hf[0] = hf[hf_rr[0] % 2]
                hf_rr[0] += 1
                act(cur_hf[0].all(), pb.all(), AF.Copy)
            else:
                act(hr.s(0, TB, 0, 64), pb.s(0, TB, 0, 64), AF.Copy)
                head_stats(cur_hf[0].all(), None, hr.s(0, TB, 0, 64), hs)
                stt(QnT.c(h), cur_hf[0].all(), qnw.s(0, 1), rs.all(), ALU.mult, ALU.mult)
                stt(hr.s(0, TB, 0, 64), hr.s(0, TB, 0, 64), qnw.s(1, 2, 0, 64), rs.s(0, TB, 0, 64), ALU.mult, ALU.mult)
                rope(QrT.c(h, None, None, 0, 64), hr.s(0, TB, 0, 64), hs["tb16"].s(0, TB, 0, 64), hs["tf32"].s(0, TB, 0, 64), hs["pA"])

        segs = []
        for h in range(HEADS):
            segs.append((h * 192, 128, ("qn", h)))
            segs.append((h * 192 + 128, 64, ("qr", h)))
        dense(W["w_uq"], l, 0, 4, lambda kc: cqn.c(kc), segs, cons_q)

        rmsnorm_big(ckv, kvaw, ckvn, 4, KV_LORA)
        act(kr2.s(0, TB, 0, 64), kr.s(0, TB, 0, 64), AF.Square)

        def cons_kv(tag, pb):
            kind, h = tag
            hs = hsets[h % 2]
            hr, rs, kob, krb, vf, vtm = hs["hr"], hs["rstd"], hs["kob"], hs["krb"], hs["vf"], hs["vtm"]
            if kind == "kn":
                cur_hf[0] = hf[hf_rr[0] % 2]
                hf_rr[0] += 1
                act(cur_hf[0].all(), pb.all(), AF.Copy)
                head_stats(cur_hf[0].all(), kr2.s(0, TB, 0, 64), None, hs)
                stt(kob.all(), cur_hf[0].all(), knw.s(0, 1), rs.all(), ALU.mult, ALU.mult)
                dma(knT_d.v(knT_d.ap[h, :, t0:t0 + TB]), kob.all())
                stt(hr.s(0, TB, 0, 64), kr.s(0, TB, 0, 64), knw.s(1, 2, 0, 64), rs.s(0, TB, 0, 64), ALU.mult, ALU.mult)
                rope(krb.s(0, TB, 0, 64), hr.s(0, TB, 0, 64), hs["tb16"].s(0, TB, 0, 64), hs["tf32"].s(0, TB, 0, 64), hs["pA"])
                dma(krT_d.v(krT_d.ap[h, :, t0:t0 + TB]), krb.s(0, TB, 0, 64))
            else:
                act(vf.all(), pb.all(), AF.Copy)
                pt_ = hs["pA"]
                for j in range(4):
                    tr(pt_.s(j * 128, (j + 1) * 128), vf.s(j * 128, (j + 1) * 128), ident_f.all())
                vcopy(vtm.all(), pt_.v(pt_.t[:, :].rearrange("p (j v) -> p j v", v=128)))
                dst = v_d.ap[t0:t0 + TB, h * 128:(h + 1) * 128].rearrange("(j p) v -> p j v", p=128)
                dma(v_d.v(dst), vtm.all())

        segs = []
        for h in range(HEADS):
            segs.append((h * 256, 128, ("kn", h)))
            segs.append((h * 256 + 128, 128, ("v", h)))
        dense(W["w_ukv"], l, 0, 4, lambda kc: ckvn.c(kc), segs, cons_kv)

        if STOP == 'A2':
            return
        R2.reset()
        nk = (tb + 1) * TB
        ntile = nk // 128
        RWa = Region(wbuf[0].off, 2 * wbuf[0].nbytes)
        kvs = []
        for reg, sfx in ((R2, "0"), (RWa, "1")):
            kvs.append((reg.alloc("Kn" + sfx, [128, S], BF16), reg.alloc("Kr" + sfx, [128, S], BF16),
                        reg.alloc("Vh" + sfx, [128, S // 128, 128], BF16)))
        pts = [R2.alloc(f"pt{i}", [128, TB], BF16) for i in range(3)]
        oTt = R2.alloc("oT", [128, 16, TB], BF16)
        rden = R2.alloc("rden", [128, TB], F32)
        O_ps, D_ps = ps[4], ps[5]
        for h in range(HEADS):
            Kn, Kr, Vh = kvs[h % 2]
            dma(Kn.s(0, nk), knT_d.v(knT_d.ap[h, :, 0:nk]))
            dma(Kr.s(0, nk, 0, 64), krT_d.v(krT_d.ap[h, :, 0:nk]))
            dma(Vh.v(Vh.t[:, 0:ntile, :], 0, ntile * 128),
                v_d.v(v_d.ap[0:nk, h * 128:(h + 1) * 128].rearrange("(kt p) v -> p kt v", p=128)))
            for kt in range(ntile):
                j = kt - 4 * tb
                c0 = 128 * j if j >= 0 else 0
                sp = next_pmm()
                mm(sp.s(c0, TB), Kn.s(kt * 128, (kt + 1) * 128), QnT.c(h, c0, TB), start=True, stop=False)
                mm(sp.s(c0, TB), Kr.s(kt * 128, (kt + 1) * 128, 0, 64), QrT.c(h, c0, TB, 0, 64), start=False, stop=True)
                pt = pts[kt % 3]
                act(pt.s(c0, TB), sp.s(c0, TB), AF.Exp, bias=-SM_SHIFT, scale=SM_SCALE)
                if j >= 0:
                    memset(pt.s(c0, c0 + 64, 64, 128), 0.0)
                mm(O_ps.s(c0, TB), Vh.c(kt), pt.s(c0, TB), start=(kt == 0), stop=(kt == ntile - 1))
                mm(D_ps.s(c0, TB), ones_b.all(), pt.s(c0, TB), start=(kt == 0), stop=(kt == ntile - 1))
            recip(rden.all(), D_ps.all())
            tt(oTt.c(h), O_ps.all(), rden.all(), ALU.mult)
        dma(oT_d.v(oT_d.ap.rearrange("(h p) t -> p h t", p=128)), oTt.all())

        if STOP == 'B':
            return
        R1.reset()
        R2.reset()
        ybT = R1.alloc("ybT", [128, 32, TB], BF16)
        class _Sm:
            def __init__(self, t):
                self.t = t

            def __getitem__(self, j):
                t = self.t

                class _J:
                    def all(self_):
                        return t.c(j)

                    def s(self_, a, b):
                        return t.c(j, a, b)
                return _J()
        sm = {nm: _Sm(R2.alloc(nm, [128, 4, 64], F32)) for nm in ("dt_tm", "acum", "wdec", "ea", "cd0", "cd1")}
        win = R2.alloc("win", [128, TB + 3], F32)
        acc = R2.alloc("acc", [128, TB], F32)
        RW = Region(wbuf[0].off, 2 * wbuf[0].nbytes)

        def ssd_set(reg, sfx, banks, reg_y):
            d = {}
            d["xg"] = reg.alloc("xg" + sfx, [128, 4, TB], F32)
            d["Bgf"] = reg.alloc("Bgf" + sfx, [128, TB], F32)
            d["BgT"] = reg.alloc("BgT" + sfx, [128, TB], BF16)
            d["CgT"] = reg.alloc("CgT" + sfx, [128, TB], BF16)
            d["xtm"] = reg.alloc("xtm" + sfx, [128, TB], BF16)
            d["skipt"] = reg.alloc("skipt" + sfx, [128, TB], F32)
            d["Blo"] = reg.alloc("Blo" + sfx, [128, 128], BF16)
            d["Bhi"] = reg.alloc("Bhi" + sfx, [128, 128], BF16)
            d["Clo"] = reg.alloc("Clo" + sfx, [128, 128], BF16)
            d["Chi"] = reg.alloc("Chi" + sfx, [128, 128], BF16)
            d["cbs"] = reg.alloc("cbs" + sfx, [128, 128], F32)
            d["rdg"] = reg.alloc("rdg" + sfx, [128, 8, 128], F32)
            d["Dm"] = d["rdg"]
            d["mT"] = reg.alloc("mT" + sfx, [128, 8, 128], BF16)
            d["xw"] = reg.alloc("xw" + sfx, [128, TB], BF16)
            d["hb0"] = reg.alloc("hb0" + sfx, [128, TB], BF16)
            d["hb1"] = reg.alloc("hb1" + sfx, [128, TB], BF16)
            d["ysb"] = reg.alloc("ysb" + sfx, [128, TB], F32)
            d["yTg"] = reg_y.alloc("yTg" + sfx, [128, 4, TB], F32)
            d["sqg"] = Tile(P, "sqg" + sfx, [128, 4, TB], BF16, off=d["rdg"].off)
            d["zt"] = Tile(P, "zt" + sfx, [128, TB], F32, off=d["skipt"].off)
            d["rstd"] = Tile(P, "rstd" + sfx, [128, TB], F32, off=d["ysb"].off)
            d["banks"] = banks
            return d

        sets = [ssd_set(R2, "_a", [ps[0], ps[1], ps[2], ps[3]], R2), ssd_set(RW, "_b", [ps[4], ps[5], ps[6], ps[7]], R2)]

        act(dtvT.all(), dtraw.all(), AF.Exp, bias=dtb.all())
        act(dtvT.all(), dtvT.all(), AF.Ln, bias=1.0)
        if STOP == 'D0a':
            return
        for j in range(4):
            tr(ps[6].s(0, 128), dtvT.s(j * 128, (j + 1) * 128), ident_f.all())
            vcopy(sm["dt_tm"][j].all(), ps[6].s(0, 64))
            if STOP == 'D0b':
                return
            a_t = next_tmp()
            tt(a_t.s(0, 64), sm["dt_tm"][j].all(), A_bc.all(), ALU.mult)
            mm(ps[7].s(0, 64), tri.all(), a_t.s(0, 64))
            if STOP == 'D0c':
                return
            vcopy(sm["acum"][j].all(), ps[7].s(0, 64))
            act(sm["ea"][j].all(), ps[7].s(0, 64), AF.Exp)
            if STOP == 'D0d':
                return
            mm(ps[6].s(0, 64), sel_last.all(), sm["acum"][j].all())
            tt(sm["wdec"][j].all(), ps[6].s(0, 64), sm["acum"][j].all(), ALU.subtract)
            act(sm["wdec"][j].all(), sm["wdec"][j].all(), AF.Exp)
            tt(sm["wdec"][j].all(), sm["wdec"][j].all(), sm["dt_tm"][j].all(), ALU.mult)
            mm(ps[7].s(0, 64), sel_c0.all(), sm["acum"][j].all())
            act(sm["cd0"][j].all(), ps[7].s(0, 64), AF.Exp)
            mm(ps[6].s(0, 64), sel_c1.all(), sm["acum"][j].all())
            act(sm["cd1"][j].all(), ps[6].s(0, 64), AF.Exp)

        if STOP == 'D0':
            return
        def conv_chunk(ci, out_v):
            dma(win.all(), xbc_d.v(xbc_d.ap[ci * 128:(ci + 1) * 128, t0:t0 + TB + 3]))
            ts(acc.all(), win.s(0, TB), convw.c(0, ci, ci + 1), None, ALU.mult)
            for tap in range(1, 4):
                stt(acc.all(), win.s(tap, tap + TB), convw.c(tap, ci, ci + 1), acc.all(), ALU.mult, ALU.add)
            act(out_v, acc.all(), AF.Silu, bias=convb.s(ci, ci + 1))

        def ssd_group(g, d):
            g8 = g * 8
            b0, b1, b2, b3 = d["banks"]
            xg, Bgf, BgT, CgT, xtm, skipt = d["xg"], d["Bgf"], d["BgT"], d["CgT"], d["xtm"], d["skipt"]
            Blo, Bhi, Clo, Chi, cbs, rdg, Dm, mT = d["Blo"], d["Bhi"], d["Clo"], d["Chi"], d["cbs"], d["rdg"], d["Dm"], d["mT"]
            xw, hb0, hb1, ysb, yTg, sqg, zt, rstd_g = d["xw"], d["hb0"], d["hb1"], d["ysb"], d["yTg"], d["sqg"], d["zt"], d["rstd"]
            for c in range(4):
                conv_chunk(g * 4 + c, xg.c(c))
                yield
            conv_chunk(32 + g, Bgf.all())
            vcopy(BgT.all(), Bgf.all())
            conv_chunk(40 + g, CgT.all())
            yield
            hs_g = hS.c(g)
            for j in range(4):
                a_cum, dtt, wd, eav, c0v, c1v = (sm[n][j] for n in ("acum", "dt_tm", "wdec", "ea", "cd0", "cd1"))
                for c in range(4):
                    tr(b0.s(c * 128, (c + 1) * 128), xg.c(c, j * 128, (j + 1) * 128), ident_f.all())
                tr(b1.s(0, 128), Bgf.s(j * 128, (j + 1) * 128), ident_f.all())
                mm(b2.s(0, 128), BgT.s(j * 128, (j + 1) * 128), CgT.s(j * 128, (j + 1) * 128))
                yield
                vcopy(xtm.all(), b0.all())
                x3 = b0.v(b0.t[:, :].rearrange("p (r q) -> p r q", q=64))
                tt(skipt.v(skipt.t[:, :].rearrange("p (r q) -> p r q", q=64)), x3,
                   bc_last(D_bc.s(g8, g8 + 8), 64), ALU.mult)
                tt(xw.v(xw.t[:, :].rearrange("p (r q) -> p r q", q=64)), x3,
                   bc_last(wd.s(g8, g8 + 8), 64), ALU.mult)
                memset(Blo.all(), 0.0, eng="gpsimd")
                memset(Bhi.all(), 0.0, eng="gpsimd")
                act(Blo.s(0, 128, 0, 64), b1.s(0, 128, 0, 64), AF.Copy)
                act(Bhi.s(0, 128, 64, 128), b1.s(0, 128, 64, 128), AF.Copy)
                memset(Clo.all(), 0.0, eng="gpsimd")
                memset(Chi.all(), 0.0, eng="gpsimd")
                vcopy(Clo.s(0, 64), CgT.s(j * 128, j * 128 + 64), eng="gpsimd")
                vcopy(Chi.s(64, 128), CgT.s(j * 128 + 64, (j + 1) * 128), eng="gpsimd")
                vcopy(cbs.all(), b2.s(0, 128))
                tt(rdg.all(), bc_mid(ident_f.all(), ident_f, 8), bc_last(a_cum.s(g8, g8 + 8), 128), ALU.mult)
                yield
                mm(b3.all(), ones_f.all(), rdg.v(rdg.t[:, 0:4, :], 0, 512))
                mm(b1.all(), ones_f.all(), rdg.v(rdg.t[:, 4:8, :], 512, 1024))
                yield
                for q, rb in ((0, b3), (1, b1)):
                    tt(Dm.v(Dm.t[:, q * 4:(q + 1) * 4, :], q * 512, (q + 1) * 512),
                       rb.v(rb.t[:, :].rearrange("p (h t) -> p h t", t=128)),
                       bc_last(a_cum.s(g8 + q * 4, g8 + q * 4 + 4), 128), ALU.subtract)
                tt(Dm.all(), Dm.all(), bc_mid(negmask.all(), negmask, 8), ALU.add)
                yield
                act(Dm.all(), Dm.all(), AF.Exp)
                yield
                tt(Dm.all(), Dm.all(), bc_last(dtt.s(g8, g8 + 8), 128), ALU.mult)
                tt(mT.all(), Dm.all(), bc_mid(cbs.all(), cbs, 8), ALU.mult)
                yield
                for r in range(8):
                    mm(b0.s(r * 64, (r + 1) * 64), mT.c(r), xtm.s(r * 64, (r + 1) * 64))
                mm(b2.all(), Blo.all(), xw.all())
                mm(b3.all(), Bhi.all(), xw.all())
                yield
                act(hb0.all(), hs_g, AF.Copy)
                h3 = V(hs_g.ap.rearrange("p (r q) -> p r q", q=64), hs_g.pages)
                tt(h3, h3, bc_last(c0v.s(g8, g8 + 8), 64), ALU.mult)
                tt(hs_g, hs_g, b2.all(), ALU.add)
                act(hb1.all(), hs_g, AF.Copy)
                tt(h3, h3, bc_last(c1v.s(g8, g8 + 8), 64), ALU.mult)
                tt(hs_g, hs_g, b3.all(), ALU.add)
                yield
                mm(b1.all(), Clo.all(), hb0.all(), start=True, stop=False)
                mm(b1.all(), Chi.all(), hb1.all(), start=False, stop=True)
                yield
                y3 = ysb.v(ysb.t[:, :].rearrange("p (r q) -> p r q", q=64))
                tt(y3, b1.v(b1.t[:, :].rearrange("p (r q) -> p r q", q=64)), bc_last(eav.s(g8, g8 + 8), 64), ALU.mult)
                tt(ysb.all(), ysb.all(), b0.all(), ALU.add)
                tt(ysb.all(), ysb.all(), skipt.all(), ALU.add)
                yield
                for c in range(4):
                    tr(b2.s(c * 128, (c + 1) * 128), ysb.s(c * 128, (c + 1) * 128), ident_f.all())
                yield
                act(yTg.v(yTg.t[:, :, j * 128:(j + 1) * 128]), b2.v(b2.t[:, :].rearrange("p (c t) -> p c t", t=128)), AF.Copy)
                yield
            for c in range(4):
                ci = g * 4 + c
                dma(zt.all(), zs_d.v(zs_d.ap[ci * 128:(ci + 1) * 128, :]))
                tt(yTg.c(c), yTg.c(c), zt.all(), ALU.mult)
                act(sqg.c(c), yTg.c(c), AF.Square)
                yield
            for c in range(4):
                mm(b3.all(), ones_b.all(), sqg.c(c), start=(c == 0), stop=(c == 3))
            yield
            act(rstd_g.all(), b3.all(), AF.Sqrt, bias=EPS, scale=1.0 / 512)
            recip(rstd_g.all(), rstd_g.all())
            yield
            for c in range(4):
                ci = g * 4 + c
                stt(ybT.c(ci), yTg.c(c), ssmw.s(ci, ci + 1), rstd_g.all(), ALU.mult, ALU.mult)
            yield

        for gp in range(0, SSM_GROUPS, 2):
            gens = [ssd_group(gp, sets[0]), ssd_group(gp + 1, sets[1])]
            while gens:
                for gn in list(gens):
                    try:
                        next(gn)
                    except StopIteration:
                        gens.remove(gn)

        if STOP == 'D':
            return
        R2.reset()
        oTt = R2.alloc("oT2", [128, 16, TB], BF16)
        mga = R2.alloc("mga", [128, 16, TB], BF16)
        mgb = R2.alloc("mgb", [128, 16, TB], BF16)
        dma(oTt.all(), oT_d.v(oT_d.ap.rearrange("(h p) t -> p h t", p=128)))

        def cons_ya(tag, pb):
            c = tag
            dma(gt.all(), g_d.v(g_d.ap[c * 128:(c + 1) * 128, :]))
            tt(mga.c(c), pb.all(), gt.all(), ALU.mult)

        dense(W["w_o_mla"], l, 0, 16, lambda kc: oTt.c(kc), [(c * 128, 128, c) for c in range(16)], cons_ya)

        def cons_yb(tag, pb):
            c = tag
            dma(gt.all(), g_d.v(g_d.ap[(16 + c) * 128:(17 + c) * 128, :]))
            t = next_tmp()
            tt(t.all(), pb.all(), gt.all(), ALU.mult)
            tt(mgb.c(c), t.all(), mga.c(c), ALU.add)

        dense(W["w_o_ssm"], l, 0, 32, lambda kc: ybT.c(kc), [(c * 128, 128, c) for c in range(16)], cons_yb)

        def cons_res(tag, pb):
            tt(X.c(tag), X.c(tag), pb.all(), ALU.add)

        dense(W["w_out"], l, 0, 16, lambda kc: mgb.c(kc), [(c * 128, 128, c) for c in range(16)], cons_res)

        if STOP == 'E':
            return
        R1.reset()
        R2.reset()
        uT = R1.alloc("uT", [128, 32, TB], BF16)
        sqt = R2.alloc("sq", [128, 16, TB], BF16)
        hT = R2.alloc("hT", [128, 16, TB], BF16)
        R2sq[0] = sqt
        rmsnorm_big(X, nlw, hT, 16, D_MODEL)
        for half in range(2):
            def cons_up(tag, pb):
                t = next_tmp()
                act(t.all(), pb.all(), AF.Relu)
                tt(uT.c(tag), t.all(), t.all(), ALU.mult)

            dense(W["w_up"], l, 0, 16, lambda kc: hT.c(kc),
                  [(half * 4096 + c * 128, 128, c) for c in range(32)], cons_up)
            for kq in range(2):
                dense(W["w_down"], l, half * 4096 + kq * 2048, 16, lambda kc, kq=kq: uT.c(kq * 16 + kc),
                      [(c * 128, 128, c) for c in range(16)], cons_res)

        if STOP == 'F':
            return
        R1.reset()
        eT = R1.alloc("eT", [128, 16, TB], F32)
        pst = R2.alloc("pst", [128, PLE_DIM], F32)
        pTb = R2.alloc("pTb", [128, 2, TB], BF16)
        rmsnorm_big(X, npw, hT, 16, D_MODEL)
        for j in range(4):
            dma(pst.all(), p_d.v(p_d.ap[l, t0 + j * 128:t0 + (j + 1) * 128, :]))
            for c in range(2):
                tr(ps[6].s(c * 128, (c + 1) * 128), pst.s(c * 128, (c + 1) * 128), ident_f.all())
            vcopy(pTb.v(pTb.t[:, :, j * 128:(j + 1) * 128]), ps[6].v(ps[6].t[:, 0:256].rearrange("p (c t) -> p c t", t=128)))

        def cons_e(tag, pb):
            act(eT.c(tag), pb.all(), AF.Copy)

        dense(W["w_ple"], l, 0, 2, lambda kc: pTb.c(kc), [(c * 128, 128, c) for c in range(16)], cons_e)

        def cons_pg(tag, pb):
            t = next_tmp()
            act(t.all(), pb.all(), AF.Sigmoid)
            tt(t.all(), t.all(), eT.c(tag), ALU.mult)
            tt(X.c(tag), X.c(tag), t.all(), ALU.add)

        dense(W["w_ple_gate"], l, 0, 16, lambda kc: hT.c(kc), [(c * 128, 128, c) for c in range(16)], cons_pg)

        if STOP == 'G':
            return
        if not last:
            dma(xT_d.v(xT_d.ap[:, t0:t0 + TB].rearrange("(kc p) t -> p kc t", p=128)), X.all())
        else:
            R2.reset()
            ost = R2.alloc("ost", [128, D_MODEL], F32)
            for j in range(4):
                for q in range(4):
                    pb = next_pmm()
                    for i in range(4):
                        kc = q * 4 + i
                        tr(pb.s(i * 128, (i + 1) * 128), X.c(kc, j * 128, (j + 1) * 128), ident_f.all())
                    vcopy(ost.s(q * 512, (q + 1) * 512), pb.all())
                fin_ops.append(dma(out_d.v(out_d.ap[t0 + j * 128:t0 + (j + 1) * 128, :]), ost.all()))

    setup_consts()
    for l in range(L):
        if STOP == 'setup':
            break
        P.epoch = l
        layer_consts(l)
        if STOP == 'lconsts':
            break
        memset(hS.all(), 0.0)
        for tb in range(NBLK):
            block(l, tb)
    P.emit(final_waits=fin_ops)
    return nc, len(P.ops)


_CACHE = {}


def _invf_table():
    j = np.arange(0, QK_ROPE, 2, dtype=np.float32) / np.float32(QK_ROPE)
    inv = (np.float32(1.0) / (np.float32(10000.0) ** j)).astype(np.float32)
    t = np.zeros((1, 128), np.float32)
    t[0, 0:32] = inv
    t[0, 32:64] = inv
    return t


def kernel(**inputs):
    x = np.ascontiguousarray(inputs["x"], dtype=np.float32)
    B, S, _ = x.shape
    L = inputs["w_in"].shape[0]
    key = (L, S)
    if key not in _CACHE:
        _CACHE[key] = build(L=L, S=S)[0]
    nc = _CACHE[key]
    in_maps = []
    for b in range(B):
        m = {"x": x[b], "p": np.ascontiguousarray(inputs["p"][:, b]),
             "positions": np.ascontiguousarray(inputs["positions"][b:b + 1]).astype(np.int32),
             "invf": _invf_table()}
        for n in WNAMES:
            m[n] = np.ascontiguousarray(inputs[n], dtype=np.float32)
        in_maps.append(m)
    res = run_bass_kernel_spmd(nc, in_maps, core_ids=list(range(B)))
    return np.stack([res.results[b]["out"] for b in range(B)], axis=0).astype(np.float32)
```

```python
import math
import numpy as np
import concourse.bass as bass
import concourse.mybir as mybir
from concourse.bass_utils import run_bass_kernel_spmd

F32 = mybir.dt.float32
BF16 = mybir.dt.bfloat16
I32 = mybir.dt.int32
AF = mybir.ActivationFunctionType
ALU = mybir.AluOpType
PG = 512
NSLOT = 8
DTSZ = {F32: 4, BF16: 2, I32: 4}

D_MODEL = 2048
SEQ = 4096
DEPTH = 4
EPS = 1e-6
HEADS = 16
Q_LORA = 512
KV_LORA = 512
QK_NOPE = 128
QK_ROPE = 64
V_DIM = 128
QK_DIM = 192
D_INNER = 4096
SSM_HEADS = 64
SSM_GROUPS = 8
SSM_STATE = 128
CONV_DIM = 6144
D_FF = 8192
PLE_DIM = 256
D_IN_PROJ = 15488
TB = 512
STOP = None
SM_SCALE = QK_DIM ** -0.5
SM_SHIFT = 14.0


class Page:
    __slots__ = ("w", "r")

    def __init__(self):
        self.w = {}
        self.r = {}


class V:
    __slots__ = ("ap", "pages", "excl")

    def __init__(self, ap, pages, excl=False):
        self.ap = ap
        self.pages = pages
        self.excl = excl


class Tile:
    def __init__(self, prog, name, shape, dtype, space="sbuf", off=None):
        self.name, self.shape, self.dtype = name, list(shape), dtype
        nc = prog.nc
        self.esz = DTSZ[dtype]
        self.nbytes = int(np.prod(shape[1:])) * self.esz
        if space == "sbuf":
            assert off is not None and off % 32 == 0
            assert off + self.nbytes <= prog.sb_limit, f"SBUF overflow at {name}: {off + self.nbytes}"
            self.off = off
            self.t = nc.alloc_sbuf_tensor_at(name, list(shape), dtype, offset=off)
            self.pg = prog.sb_pages
        else:
            self.t = nc.alloc_psum_tensor(name, list(shape), dtype)
            self.off = 0
            self.pg = [Page()]
        self.excl = (space != "sbuf")
        self._cache = {}

    def _pages(self, lo, hi):
        if self.excl:
            return self.pg
        a, b = (self.off + lo) // PG, (self.off + hi - 1) // PG
        k = (a, b)
        p = self._cache.get(k)
        if p is None:
            p = self.pg[a:b + 1]
            self._cache[k] = p
        return p

    def all(self):
        return V(self.t[:], self._pages(0, self.nbytes), self.excl)

    def v(self, ap, lo=None, hi=None):
        lo = 0 if lo is None else lo * self.esz
        hi = self.nbytes if hi is None else hi * self.esz
        return V(ap, self._pages(lo, hi), self.excl)

    def c(self, i, a=None, b=None, p0=0, p1=None):
        n = self.shape[2]
        a = 0 if a is None else a
        b = n if b is None else b
        p1 = self.shape[0] if p1 is None else p1
        return V(self.t[p0:p1, i, a:b], self._pages((i * n + a) * self.esz, (i * n + b) * self.esz), self.excl)

    def s(self, a=None, b=None, p0=0, p1=None):
        a = 0 if a is None else a
        b = self.shape[1] if b is None else b
        p1 = self.shape[0] if p1 is None else p1
        return V(self.t[p0:p1, a:b], self._pages(a * self.esz, b * self.esz), self.excl)


class Dram:
    def __init__(self, ap):
        self.ap = ap
        self.pg = [Page()]

    def v(self, ap=None):
        return V(self.ap if ap is None else ap, self.pg)


class Op:
    __slots__ = ("eng", "fn", "is_dma", "tl", "cnt", "waits", "need_inc", "snap", "val")


class Prog:
    def __init__(self, nc, same_engine_sync=True, sb_base=20992, sb_limit=229376):
        self.nc = nc
        self.ops = []
        self.same_engine_sync = same_engine_sync
        self.sb_base = sb_base
        self.sb_limit = sb_limit
        self.sb_pages = [Page() for _ in range(sb_limit // PG + 2)]
        self.known = {e: {} for e in ("tensor", "vector", "scalar", "gpsimd", "sync")}
        self.tl_count = {}
        self.tl_last = {}
        self.dma_rr = {"sync": 0, "gpsimd": 0, "scalar": 0}
        self.epoch = 0

    def op(self, eng, fn, reads=(), writes=(), dma=False):
        o = Op()
        o.eng, o.fn, o.is_dma, o.need_inc = eng, fn, dma, False
        if dma:
            slot = self.dma_rr[eng]
            self.dma_rr[eng] = (slot + 1) % NSLOT
            o.tl = ("dma", eng, slot, self.epoch)
            lastkey = ("dma", eng, slot)
        else:
            o.tl = ("eng", eng, self.epoch)
        xr = [v for v in reads if v.excl]
        if xr:
            reads = [v for v in reads if not v.excl]
            writes = list(writes) + xr
        k = self.known[eng]
        need = {}
        own = ("eng", eng)
        skip_own = (not dma) and (eng == "tensor" or not self.same_engine_sync)

        def consider(d):
            tl = d.tl
            if skip_own and tl[:2] == own:
                return
            if k.get(tl, 0) >= d.cnt:
                return
            cur = need.get(tl)
            if cur is None or cur.cnt < d.cnt:
                need[tl] = d

        if dma:
            prev = self.tl_last.get(lastkey)
            if prev is not None:
                consider(prev)
        for v in reads:
            for p in v.pages:
                for d in p.w.values():
                    consider(d)
        for v in writes:
            for p in v.pages:
                for d in p.w.values():
                    consider(d)
                for d in p.r.values():
                    consider(d)
        wl = []
        for tl, d in need.items():
            if k.get(tl, 0) >= d.cnt:
                continue
            wl.append(d)
            d.need_inc = True
            k[tl] = d.cnt
            for t2, c2 in d.snap.items():
                if k.get(t2, 0) < c2:
                    k[t2] = c2
        o.waits = wl
        c = self.tl_count.get(o.tl, 0) + 1
        self.tl_count[o.tl] = c
        o.cnt = c
        if dma:
            self.tl_last[lastkey] = o
        o.snap = dict(k)
        for v in reads:
            for p in v.pages:
                p.r[o.tl] = o
        for v in writes:
            for p in v.pages:
                if p.r:
                    p.r = {}
                    p.w = {o.tl: o}
                else:
                    p.w[o.tl] = o
        self.ops.append(o)
        return o

    def dma(self, eng, out, in_, **kw):
        oa, ia = out.ap, in_.ap
        return self.op(eng, lambda e: e.dma_start(out=oa, in_=ia, **kw), [in_], [out], dma=True)

    def emit(self, final_waits=()):
        nc = self.nc
        for d in final_waits:
            d.need_inc = True
        tl_val = {}
        for o in self.ops:
            if o.is_dma:
                o.val = tl_val[o.tl] = tl_val.get(o.tl, 0) + 16
            elif o.need_inc:
                o.val = tl_val[o.tl] = tl_val.get(o.tl, 0) + 1
        sems = {tl: nc.alloc_semaphore("s_" + "_".join(str(x) for x in tl)) for tl in tl_val}
        per_eng = {e: [] for e in self.known}
        for o in self.ops:
            per_eng[o.eng].append(o)

        def run(engname, e):
            for o in per_eng[engname]:
                for d in o.waits:
                    e.wait_ge(sems[d.tl], d.val)
                ins = o.fn(e)
                if o.is_dma:
                    ins.then_inc(sems[o.tl], 16)
                elif o.need_inc:
                    ins.then_inc(sems[o.tl], 1)
            if engname == "sync":
                for d in final_waits:
                    e.wait_ge(sems[d.tl], d.val)

        with nc.Block() as block:
            @block.sync
            def _(e):
                run("sync", e)

            @block.scalar
            def _(e):
                run("scalar", e)

            @block.vector
            def _(e):
                run("vector", e)

            @block.gpsimd
            def _(e):
                run("gpsimd", e)

            @block.tensor
            def _(e):
                run("tensor", e)


WNAMES = ["norm_mix_w", "w_in", "q_a_norm_w", "w_uq", "kv_a_norm_w", "w_ukv", "q_norm_w", "k_norm_w",
          "w_o_mla", "conv_w", "conv_b", "dt_bias", "a_log", "d_skip", "ssm_norm_w", "w_o_ssm", "w_out",
          "norm_mlp_w", "w_up", "w_down", "ple_norm_w", "w_ple_gate", "w_ple"]
WSHAPES = {
    "norm_mix_w": [D_MODEL], "w_in": [D_MODEL, D_IN_PROJ], "q_a_norm_w": [Q_LORA], "w_uq": [Q_LORA, HEADS * QK_DIM],
    "kv_a_norm_w": [KV_LORA], "w_ukv": [KV_LORA, HEADS * 256], "q_norm_w": [QK_DIM], "k_norm_w": [QK_DIM],
    "w_o_mla": [HEADS * V_DIM, D_MODEL], "conv_w": [4, CONV_DIM], "conv_b": [CONV_DIM], "dt_bias": [SSM_HEADS],
    "a_log": [SSM_HEADS], "d_skip": [SSM_HEADS], "ssm_norm_w": [D_INNER], "w_o_ssm": [D_INNER, D_MODEL],
    "w_out": [D_MODEL, D_MODEL], "norm_mlp_w": [D_MODEL], "w_up": [D_MODEL, D_FF], "w_down": [D_FF, D_MODEL],
    "ple_norm_w": [D_MODEL], "w_ple_gate": [D_MODEL, D_MODEL], "w_ple": [PLE_DIM, D_MODEL],
}


SAME_ENGINE_SYNC = True


def build(L=DEPTH, S=SEQ, dbg=False):
    NBLK = S // TB
    nc = bass.Bass("TRN2", target_bir_lowering=False)
    P = Prog(nc, same_engine_sync=SAME_ENGINE_SYNC)

    def din(name, shape, dt=F32):
        return Dram(nc.dram_tensor(name, shape, dt, kind="ExternalInput").ap())

    def dscr(name, shape, dt=F32):
        return Dram(nc.dram_tensor(name, shape, dt).ap())

    x_d = din("x", [S, D_MODEL])
    p_d = din("p", [L, S, PLE_DIM])
    pos_d = din("positions", [1, S], I32)
    invf_d = din("invf", [1, 128])
    W = {n: din(n, [L] + WSHAPES[n]) for n in WNAMES}
    out_d = Dram(nc.dram_tensor("out", [S, D_MODEL], F32, kind="ExternalOutput").ap())
    dbg_outs = {}

    xT_d = dscr("xT_s", [D_MODEL, S])
    xbc_d = dscr("xbc_s", [CONV_DIM, 3 + S])
    zs_d = dscr("zs_s", [D_INNER, TB])
    g_d = dscr("g_s", [2 * D_MODEL, TB])
    knT_d = dscr("knT_s", [HEADS, 128, S], BF16)
    krT_d = dscr("krT_s", [HEADS, 64, S], BF16)
    v_d = dscr("v_s", [S, HEADS * V_DIM], BF16)
    oT_d = dscr("oT_s", [HEADS * V_DIM, TB], BF16)
    cos_d = dscr("cos_s", [64, S])
    sin_d = dscr("sin_s", [64, S])

    cur = [P.sb_base]

    def alloc(name, shape, dt, at=None):
        if at is None:
            off = (cur[0] + PG - 1) // PG * PG
            t = Tile(P, name, shape, dt, off=off)
            cur[0] = off + t.nbytes
        else:
            t = Tile(P, name, shape, dt, off=at)
        return t

    class Region:
        def __init__(self, base, size):
            self.base, self.size, self.cur = base, size, base

        def reset(self):
            self.cur = self.base

        def alloc(self, name, shape, dt):
            off = (self.cur + PG - 1) // PG * PG
            t = Tile(P, name, shape, dt, off=off)
            self.cur = off + t.nbytes
            assert self.cur <= self.base + self.size, f"region overflow {name} {self.cur - self.base} > {self.size}"
            return t

    ident_f = alloc("ident_f", [128, 128], F32)
    ident_b = alloc("ident_b", [128, 128], BF16)
    ones_b = alloc("ones_b", [128, 128], BF16)
    ones_f = alloc("ones_f", [128, 128], F32)
    tri = alloc("tri", [128, 128], F32)
    negmask = alloc("negmask", [128, 128], F32)
    sel_last = alloc("sel_last", [128, 128], F32)
    sel_c0 = alloc("sel_c0", [128, 128], F32)
    sel_c1 = alloc("sel_c1", [128, 128], F32)
    rrot_b = alloc("rrot_b", [128, 128], BF16)
    invf = alloc("invf", [128, 1], F32)
    nmw = alloc("nmw", [128, 16], F32)
    nlw = alloc("nlw", [128, 16], F32)
    npw = alloc("npw", [128, 16], F32)
    qaw = alloc("qaw", [128, 4], F32)
    kvaw = alloc("kvaw", [128, 4], F32)
    qnw = alloc("qnw", [128, 2], F32)
    knw = alloc("knw", [128, 2], F32)
    convw = alloc("convw", [128, 4, 48], F32)
    convb = alloc("convb", [128, 48], F32)
    ssmw = alloc("ssmw", [128, 32], F32)
    dtb = alloc("dtb", [128, 1], F32)
    A_bc = alloc("A_bc", [128, 64], F32)
    D_bc = alloc("D_bc", [128, 64], F32)
    X = alloc("X", [128, 16, TB], F32)
    hS = alloc("hS", [128, 8, 512], F32)
    wbuf = [alloc(f"wbuf{i}", [128, 16 * 512], BF16) for i in range(2)]
    cosb = alloc("cosb", [128, TB], F32)
    sinb = alloc("sinb", [128, TB], F32)
    dtraw = alloc("dtraw", [128, TB], F32)
    dtvT = alloc("dtvT", [128, TB], F32)
    gt = alloc("gt", [128, TB], F32)
    rstd = alloc("rstd", [128, TB], F32)
    tmpA = alloc("tmpA", [128, TB], F32)
    tmpB = alloc("tmpB", [128, TB], F32)
    tmpC = alloc("tmpC", [128, TB], F32)
    base1 = (cur[0] + PG - 1) // PG * PG
    R1 = Region(base1, 32 * 1024)
    R2 = Region(base1 + 32 * 1024, P.sb_limit - (base1 + 32 * 1024))

    ps = [Tile(P, f"psb{i}", [128, 512], F32, space="psum") for i in range(8)]
    pmm_rr = [0]

    pmm_n = [4]

    def next_pmm():
        b = ps[pmm_rr[0] % pmm_n[0]]
        pmm_rr[0] += 1
        return b

    tmp_rr = [0]

    def next_tmp():
        t = (tmpA, tmpB, tmpC)[tmp_rr[0] % 3]
        tmp_rr[0] += 1
        return t

    def mm(out, lhsT, rhs, start=True, stop=True):
        oa, la, ra = out.ap, lhsT.ap, rhs.ap
        return P.op("tensor", lambda e: e.matmul(oa, la, ra, start=start, stop=stop), [lhsT, rhs], [out])

    def tr(out, in_, ident):
        oa, ia, da = out.ap, in_.ap, ident.ap
        return P.op("tensor", lambda e: e.transpose(oa, ia, da), [in_, ident], [out])

    def act(out, in_, func, bias=None, scale=1.0, eng="scalar"):
        oa, ia = out.ap, in_.ap
        rd = [in_]
        kw = {}
        if isinstance(bias, V):
            rd.append(bias)
            kw["bias"] = bias.ap
        elif bias is not None:
            kw["bias"] = float(bias)
        return P.op("scalar", lambda e: e.activation(out=oa, in_=ia, func=func, scale=scale, **kw), rd, [out])

    def vcopy(out, in_, eng="vector"):
        oa, ia = out.ap, in_.ap
        return P.op(eng, lambda e: e.tensor_copy(oa, ia), [in_], [out])

    def tt(out, a, b, op, eng="vector"):
        oa, aa, ba = out.ap, a.ap, b.ap
        return P.op(eng, lambda e: e.tensor_tensor(oa, aa, ba, op), [a, b], [out])

    def ts(out, a, s1, s2, op0, op1=None, eng="vector"):
        oa, aa = out.ap, a.ap
        rd = [a]
        s1a = s1
        if isinstance(s1, V):
            rd.append(s1)
            s1a = s1.ap
        s2a = s2
        if isinstance(s2, V):
            rd.append(s2)
            s2a = s2.ap
        if op1 is None:
            return P.op(eng, lambda e: e.tensor_scalar(oa, aa, s1a, s2a, op0), rd, [out])
        return P.op(eng, lambda e: e.tensor_scalar(oa, aa, s1a, s2a, op0, op1), rd, [out])

    def stt(out, a, sc, b, op0, op1, eng="vector"):
        oa, aa, ba = out.ap, a.ap, b.ap
        rd = [a, b]
        sa = sc
        if isinstance(sc, V):
            rd.append(sc)
            sa = sc.ap
        return P.op(eng, lambda e: e.scalar_tensor_tensor(oa, aa, sa, ba, op0, op1), rd, [out])

    def memset(out, val, eng="vector"):
        oa = out.ap
        return P.op(eng, lambda e: e.memset(oa, val), [], [out])

    def recip(out, in_):
        oa, ia = out.ap, in_.ap
        return P.op("vector", lambda e: e.reciprocal(oa, ia), [in_], [out])

    def dma(out, in_, eng="sync"):
        return P.dma(eng, out, in_)

    def bc_mid(v, tile, n):
        ap = v.ap
        return V(ap.unsqueeze(1).to_broadcast([ap.shape[0], n, ap.shape[1]]), v.pages)

    def bc_last(v, n):
        ap = v.ap
        return V(ap.unsqueeze(2).to_broadcast([ap.shape[0], ap.shape[1], n]), v.pages)

    def setup_consts():
        io = R2.alloc("c_io", [128, 128], F32)
        ip = R2.alloc("c_ip", [128, 1], F32)
        jh = R2.alloc("c_jh", [128, 128], F32)
        ph = R2.alloc("c_ph", [128, 1], F32)
        t1 = R2.alloc("c_t1", [128, 128], F32)
        t2 = R2.alloc("c_t2", [128, 128], F32)
        P.op("gpsimd", lambda e: e.iota(io.t[:], pattern=[[1, 128]], base=0, channel_multiplier=0,
                                        allow_small_or_imprecise_dtypes=True), [], [io.all()])
        P.op("gpsimd", lambda e: e.iota(ip.t[:], pattern=[[0, 1]], base=0, channel_multiplier=1,
                                        allow_small_or_imprecise_dtypes=True), [], [ip.all()])
        ts(ident_f.all(), io.all(), ip.all(), None, ALU.is_equal)
        vcopy(ident_b.all(), ident_f.all())
        memset(ones_b.all(), 1.0)
        memset(ones_f.all(), 1.0)
        ts(jh.all(), io.all(), 64.0, None, ALU.is_ge)
        ts(ph.all(), ip.all(), 64.0, None, ALU.is_ge)
        ts(t1.all(), io.all(), ip.all(), None, ALU.is_ge)
        ts(t2.all(), jh.all(), ph.all(), None, ALU.is_equal)
        tt(tri.all(), t1.all(), t2.all(), ALU.mult)
        ts(negmask.all(), tri.all(), -1.0, 30000.0, ALU.add, ALU.mult)
        ts(t1.all(), jh.all(), 64.0, 63.0, ALU.mult, ALU.add)
        ts(sel_last.all(), t1.all(), ip.all(), None, ALU.is_equal)
        memset(t1.all(), 63.0)
        ts(sel_c0.all(), t1.all(), ip.all(), None, ALU.is_equal)
        memset(t1.all(), 127.0)
        ts(sel_c1.all(), t1.all(), ip.all(), None, ALU.is_equal)
        ts(t1.all(), io.all(), -32.0, None, ALU.add)
        ts(t1.all(), t1.all(), ip.all(), None, ALU.is_equal)
        ts(t2.all(), io.all(), 32.0, None, ALU.add)
        ts(t2.all(), t2.all(), ip.all(), None, ALU.is_equal)
        tt(t1.all(), t1.all(), t2.all(), ALU.subtract)
        vcopy(rrot_b.all(), t1.all())
        memset(dtraw.all(), 0.0)
        memset(hS.all(), 0.0)
        st = R2.alloc("c_st", [128, 128], F32)
        memset(st.all(), 0.0)
        dma(st.s(0, 128, 0, 1), invf_d.v())
        tr(ps[6].s(0, 128), st.all(), ident_f.all())
        vcopy(invf.all(), ps[6].s(0, 1))
        memset(tmpA.all(), 0.0)
        for j in range(CONV_DIM // 128):
            dma(xbc_d.v(xbc_d.ap[j * 128:(j + 1) * 128, 0:3]), tmpA.s(0, 3))
        R2.reset()
        TWO_PI = 2.0 * math.pi
        C1 = 6.28125
        C2 = TWO_PI - C1
        MAGIC = 12582912.0
        for c0 in range(0, S, 2048):
            n = min(2048, S - c0)
            pi_ = R2.alloc("r_pi", [64, 2048], I32)
            ang = R2.alloc("r_ang", [64, 2048], F32)
            nn = R2.alloc("r_n", [64, 2048], F32)
            rr = R2.alloc("r_r", [64, 2048], F32)
            dma(pi_.s(0, n), pos_d.v(pos_d.ap[0, c0:c0 + n].partition_broadcast(64)))
            vcopy(ang.s(0, n), pi_.s(0, n))
            ts(ang.s(0, n), ang.s(0, n), invf.s(0, 1, 0, 64), None, ALU.mult)
            ts(nn.s(0, n), ang.s(0, n), 1.0 / TWO_PI, MAGIC, ALU.mult, ALU.add)
            ts(nn.s(0, n), nn.s(0, n), -MAGIC, None, ALU.add)
            stt(rr.s(0, n), nn.s(0, n), -C1, ang.s(0, n), ALU.mult, ALU.add)
            stt(rr.s(0, n), nn.s(0, n), -C2, rr.s(0, n), ALU.mult, ALU.add)
            ts(rr.s(0, n), rr.s(0, n), 3.1415925, -3.1415925, ALU.min, ALU.max)
            act(ang.s(0, n), rr.s(0, n), AF.Sin)
            dma(sin_d.v(sin_d.ap[:, c0:c0 + n]), ang.s(0, n))
            ts(nn.s(0, n), rr.s(0, n), -1.0, None, ALU.mult)
            tt(nn.s(0, n), nn.s(0, n), rr.s(0, n), ALU.max)
            act(ang.s(0, n), nn.s(0, n), AF.Sin, bias=math.pi / 2, scale=-1.0)
            dma(cos_d.v(cos_d.ap[:, c0:c0 + n]), ang.s(0, n))
            R2.reset()

    def load_cols(dst_v, src_ap_rows, nrows, ncols=128):
        st = R2.alloc("lc_st", [128, 128], F32)
        memset(st.all(), 0.0)
        dma(st.s(0, ncols, 0, nrows), Dram(src_ap_rows).v())
        tr(ps[6].s(0, 128), st.all(), ident_f.all())
        vcopy(dst_v, ps[6].s(0, nrows))
        R2.cur -= 0

    def layer_consts(l):
        R2.reset()
        for (t, nm, n) in ((nmw, "norm_mix_w", 16), (nlw, "norm_mlp_w", 16), (npw, "ple_norm_w", 16),
                           (qaw, "q_a_norm_w", 4), (kvaw, "kv_a_norm_w", 4), (ssmw, "ssm_norm_w", 32),
                           (convb, "conv_b", 48)):
            load_cols(t.all(), W[nm].ap[l].rearrange("(r c) -> r c", c=128), n)
            R2.reset()
        for tap in range(4):
            load_cols(convw.c(tap), W["conv_w"].ap[l, tap].rearrange("(r c) -> r c", c=128), 48)
            R2.reset()
        for (t, nm) in ((qnw, "q_norm_w"), (knw, "k_norm_w")):
            load_cols(t.s(0, 1), W[nm].ap[l, 0:128].rearrange("(r c) -> r c", c=128), 1)
            R2.reset()
            load_cols(t.s(1, 2), W[nm].ap[l, 128:192].rearrange("(r c) -> r c", c=64), 1, ncols=64)
            R2.reset()
        load_cols(dtb.all(), W["dt_bias"].ap[l].rearrange("(r c) -> r c", c=64), 1, ncols=64)
        R2.reset()
        dma(A_bc.all(), W["a_log"].v(W["a_log"].ap[l].partition_broadcast(128)))
        act(A_bc.all(), A_bc.all(), AF.Exp)
        ts(A_bc.all(), A_bc.all(), -1.0, None, ALU.mult)
        dma(D_bc.all(), W["d_skip"].v(W["d_skip"].ap[l].partition_broadcast(128)))

    wb_rr = [0]

    def dense(Wd, l, row0, KC, in_chunk, segs, consumer, gcols=None):
        if gcols is None:
            gcols = min(512, 8192 // KC)
        groups = []
        for sg in segs:
            c0, n, tag = sg
            if groups and groups[-1][0] + groups[-1][1] == c0 and (groups[-1][1] + n) <= gcols:
                groups[-1][1] += n
                groups[-1][2].append(sg)
            else:
                groups.append([c0, n, [sg]])
        for (g0, gw, sgs) in groups:
            wb = wbuf[wb_rr[0] % 2]
            wb_rr[0] += 1
            src = Wd.ap[l, row0:row0 + KC * 128, g0:g0 + gw].rearrange("(kc p) f -> p kc f", p=128)
            dst_ap = wb.t[:, 0:KC * gw].rearrange("p (kc f) -> p kc f", f=gw)
            dstv = wb.v(dst_ap, 0, KC * gw)
            P.dma("gpsimd", dstv, Wd.v(src))
            for (c0, n, tag) in sgs:
                pb = next_pmm()
                o = c0 - g0
                for kc in range(KC):
                    lw = wb.v(wb.t[:, kc * gw + o: kc * gw + o + n], kc * gw + o, kc * gw + o + n)
                    mm(pb.s(0, TB, 0, n), lw, in_chunk(kc), start=(kc == 0), stop=(kc == KC - 1))
                consumer(tag, pb)

    def rmsnorm_big(src, wcols, dst, nch, D):
        sqt = R2sq[0]
        for kc in range(nch):
            act(sqt.c(kc), src.c(kc), AF.Square)
        for kc in range(nch):
            mm(ps[7].all(), ones_b.all(), sqt.c(kc), start=(kc == 0), stop=(kc == nch - 1))
        act(rstd.all(), ps[7].all(), AF.Sqrt, bias=EPS, scale=1.0 / D)
        recip(rstd.all(), rstd.all())
        for kc in range(nch):
            stt(dst.c(kc), src.c(kc), wcols.s(kc, kc + 1), rstd.all(), ALU.mult, ALU.mult)

    R2sq = [None]
    fin_ops = []

    def rope(dst_bf, src_f32, tmp_b, tmp_f, pbank=None):
        pbank = ps[6] if pbank is None else pbank
        vcopy(tmp_b, src_f32)
        mm(pbank.s(0, TB, 0, 64), rrot_b.s(0, 64, 0, 64), tmp_b)
        tt(tmp_f, pbank.s(0, TB, 0, 64), sinb.s(0, TB, 0, 64), ALU.mult)
        tt(src_f32, src_f32, cosb.s(0, TB, 0, 64), ALU.mult)
        tt(dst_bf, src_f32, tmp_f, ALU.add)

    def block(l, tb):
        t0 = tb * TB
        last = (l == L - 1)
        R1.reset()
        R2.reset()
        if l == 0:
            stg = R2.alloc("xs", [128, D_MODEL], F32)
            for j in range(4):
                dma(stg.all(), x_d.v(x_d.ap[t0 + j * 128: t0 + (j + 1) * 128, :]))
                for q in range(4):
                    pb = next_pmm()
                    for i in range(4):
                        kc = q * 4 + i
                        tr(pb.s(i * 128, (i + 1) * 128), stg.s(kc * 128, (kc + 1) * 128), ident_f.all())
                    dst = X.v(X.t[:, q * 4:(q + 1) * 4, j * 128:(j + 1) * 128], q * 4 * TB, (q + 1) * 4 * TB)
                    src = pb.v(pb.t[:, :].rearrange("p (c t) -> p c t", t=128))
                    vcopy(dst, src, eng="scalar" if False else "vector")
            R2.reset()
        else:
            dma(X.all(), xT_d.v(xT_d.ap[:, t0:t0 + TB].rearrange("(kc p) t -> p kc t", p=128)))
        dma(cosb.s(0, TB, 0, 64), cos_d.v(cos_d.ap[:, t0:t0 + TB]))
        dma(sinb.s(0, TB, 0, 64), sin_d.v(sin_d.ap[:, t0:t0 + TB]))

        if STOP == 'load':
            return
        sqt = R1.alloc("sq", [128, 16, TB], BF16)
        hT = R1.alloc("hT", [128, 16, TB], BF16)
        R2sq[0] = sqt
        cq = R2.alloc("cq", [128, 4, TB], F32)
        ckv = R2.alloc("ckv", [128, 4, TB], F32)
        kr = R2.alloc("kr", [128, TB], F32)
        rmsnorm_big(X, nmw, hT, 16, D_MODEL)
        segs = []
        for j in range(4):
            segs.append((j * 128, 128, ("cq", j)))
        for j in range(4):
            segs.append((512 + j * 128, 128, ("ckv", j)))
        segs.append((1024, 64, ("kr", 0)))
        for j in range(32):
            segs.append((1088 + j * 128, 128, ("z", j)))
        for j in range(48):
            segs.append((5184 + j * 128, 128, ("xbc", j)))
        segs.append((11328, 64, ("dt", 0)))
        for j in range(32):
            segs.append((11392 + j * 128, 128, ("g", j)))

        def cons_in(tag, pb):
            kind, j = tag
            if kind == "cq":
                act(cq.c(j), pb.all(), AF.Copy)
            elif kind == "ckv":
                act(ckv.c(j), pb.all(), AF.Copy)
            elif kind == "kr":
                act(kr.s(0, TB, 0, 64), pb.s(0, TB, 0, 64), AF.Copy)
            elif kind == "dt":
                vcopy(dtraw.s(0, TB, 0, 64), pb.s(0, TB, 0, 64))
            elif kind == "z":
                t = next_tmp()
                act(t.all(), pb.all(), AF.Silu)
                dma(zs_d.v(zs_d.ap[j * 128:(j + 1) * 128, :]), t.all())
            elif kind == "xbc":
                t = next_tmp()
                vcopy(t.all(), pb.all())
                dma(xbc_d.v(xbc_d.ap[j * 128:(j + 1) * 128, 3 + t0:3 + t0 + TB]), t.all())
            elif kind == "g":
                t = next_tmp()
                act(t.all(), pb.all(), AF.Sigmoid)
                dma(g_d.v(g_d.ap[j * 128:(j + 1) * 128, :]), t.all())

        pmm_n[0] = 6
        dense(W["w_in"], l, 0, 16, lambda kc: hT.c(kc), segs, cons_in)
        pmm_n[0] = 4

        if STOP == 'A':
            return
        R1.reset()
        QnT = R1.alloc("QnT", [128, 16, TB], BF16)
        QrT = R1.alloc("QrT", [128, 16, TB], BF16)
        cqn = R2.alloc("cqn", [128, 4, TB], BF16)
        ckvn = R2.alloc("ckvn", [128, 4, TB], BF16)
        sq4 = R2.alloc("sq4", [128, 4, TB], BF16)
        R2sq[0] = sq4
        hf = [R2.alloc(f"hf{i}", [128, TB], F32) for i in range(2)]
        hsets = []
        for i, (pa, pb_) in enumerate(((ps[6], ps[7]), (ps[4], ps[5]))):
            hsets.append(dict(
                hr=R2.alloc(f"hr{i}", [128, TB], F32), vf=R2.alloc(f"vf{i}", [128, TB], F32),
                tb16=R2.alloc(f"tb16{i}", [128, TB], BF16), tf32=R2.alloc(f"tf32{i}", [128, TB], F32),
                kob=R2.alloc(f"kob{i}", [128, TB], BF16), krb=R2.alloc(f"krb{i}", [128, TB], BF16),
                vtm=R2.alloc(f"vtm{i}", [128, 4, 128], BF16), rstd=R2.alloc(f"rstdh{i}", [128, TB], F32), pA=pa, pB=pb_))
        kr2 = R2.alloc("kr2", [128, TB], BF16)

        rmsnorm_big(cq, qaw, cqn, 4, Q_LORA)
        hf_rr = [0]
        cur_hf = [None]

        def head_stats(nope_v, rope_sq_v, rope_src_v, hs):
            sq = next_tmp()
            pB, rs = hs["pB"], hs["rstd"]
            sqb = V(sq.t[:, :].bitcast(BF16)[:, 0:TB], sq._pages(0, TB * 2))
            act(sqb, nope_v, AF.Square)
            mm(pB.all(), ones_b.all(), sqb, start=True, stop=False)
            if rope_sq_v is None:
                sq2 = V(sq.t[0:64, :].bitcast(BF16)[:, TB:2 * TB], sq._pages(TB * 2, TB * 4))
                act(sq2, rope_src_v, AF.Square)
                rope_sq_v = sq2
            mm(pB.all(), ones_b.s(0, 128, 0, 64), rope_sq_v, start=False, stop=True)
            act(rs.all(), pB.all(), AF.Sqrt, bias=EPS, scale=1.0 / QK_DIM)
            recip(rs.all(), rs.all())

        def cons_q(tag, pb):
            kind, h = tag
            hs = hsets[h % 2]
            hr, rs = hs["hr"], hs["rstd"]
            if kind == "qn":
                cur_hf[0] = hf[hf_rr[0] % 2]
                hf_rr[0] += 1
                act(cur_hf[0].all(), pb.all(), AF.Copy)
            else:
                act(hr.s(0, TB, 0, 64), pb.s(0, TB, 0, 64), AF.Copy)
                head_stats(cur_hf[0].all(), None, hr.s(0, TB, 0, 64), hs)
                stt(QnT.c(h), cur_hf[0].all(), qnw.s(0, 1), rs.all(), ALU.mult, ALU.mult)
                stt(hr.s(0, TB, 0, 64), hr.s(0, TB, 0, 64), qnw.s(1, 2, 0, 64), rs.s(0, TB, 0, 64), ALU.mult, ALU.mult)
                rope(QrT.c(h, None, None, 0, 64), hr.s(0, TB, 0, 64), hs["tb16"].s(0, TB, 0, 64), hs["tf32"].s(0, TB, 0, 64), hs["pA"])

        segs = []
        for h in range(HEADS):
            segs.append((h * 192, 128, ("qn", h)))
            segs.append((h * 192 + 128, 64, ("qr", h)))
        dense(W["w_uq"], l, 0, 4, lambda kc: cqn.c(kc), segs, cons_q)

        rmsnorm_big(ckv, kvaw, ckvn, 4, KV_LORA)
        act(kr2.s(0, TB, 0, 64), kr.s(0, TB, 0, 64), AF.Square)

        def cons_kv(tag, pb):
            kind, h = tag
            hs = hsets[h % 2]
            hr, rs, kob, krb, vf, vtm = hs["hr"], hs["rstd"], hs["kob"], hs["krb"], hs["vf"], hs["vtm"]
            if kind == "kn":
                cur_hf[0] = hf[hf_rr[0] % 2]
                hf_rr[0] += 1
                act(cur_hf[0].all(), pb.all(), AF.Copy)
                head_stats(cur_hf[0].all(), kr2.s(0, TB, 0, 64), None, hs)
                stt(kob.all(), cur_hf[0].all(), knw.s(0, 1), rs.all(), ALU.mult, ALU.mult)
                dma(knT_d.v(knT_d.ap[h, :, t0:t0 + TB]), kob.all())
                stt(hr.s(0, TB, 0, 64), kr.s(0, TB, 0, 64), knw.s(1, 2, 0, 64), rs.s(0, TB, 0, 64), ALU.mult, ALU.mult)
                rope(krb.s(0, TB, 0, 64), hr.s(0, TB, 0, 64), hs["tb16"].s(0, TB, 0, 64), hs["tf32"].s(0, TB, 0, 64), hs["pA"])
                dma(krT_d.v(krT_d.ap[h, :, t0:t0 + TB]), krb.s(0, TB, 0, 64))
            else:
                act(vf.all(), pb.all(), AF.Copy)
                pt_ = hs["pA"]
                for j in range(4):
                    tr(pt_.s(j * 128, (j + 1) * 128), vf.s(j * 128, (j + 1) * 128), ident_f.all())
                vcopy(vtm.all(), pt_.v(pt_.t[:, :].rearrange("p (j v) -> p j v", v=128)))
                dst = v_d.ap[t0:t0 + TB, h * 128:(h + 1) * 128].rearrange("(j p) v -> p j v", p=128)
                dma(v_d.v(dst), vtm.all())

        segs = []
        for h in range(HEADS):
            segs.append((h * 256, 128, ("kn", h)))
            segs.append((h * 256 + 128, 128, ("v", h)))
        dense(W["w_ukv"], l, 0, 4, lambda kc: ckvn.c(kc), segs, cons_kv)

        if STOP == 'A2':
            return
        R2.reset()
        nk = (tb + 1) * TB
        ntile = nk // 128
        RWa = Region(wbuf[0].off, 2 * wbuf[0].nbytes)
        kvs = []
        for reg, sfx in ((R2, "0"), (RWa, "1")):
            kvs.append((reg.alloc("Kn" + sfx, [128, S], BF16), reg.alloc("Kr" + sfx, [128, S], BF16),
                        reg.alloc("Vh" + sfx, [128, S // 128, 128], BF16)))
        pts = [R2.alloc(f"pt{i}", [128, TB], BF16) for i in range(3)]
        oTt = R2.alloc("oT", [128, 16, TB], BF16)
        rden = R2.alloc("rden", [128, TB], F32)
        O_ps, D_ps = ps[4], ps[5]
        for h in range(HEADS):
            Kn, Kr, Vh = kvs[h % 2]
            dma(Kn.s(0, nk), knT_d.v(knT_d.ap[h, :, 0:nk]))
            dma(Kr.s(0, nk, 0, 64), krT_d.v(krT_d.ap[h, :, 0:nk]))
            dma(Vh.v(Vh.t[:, 0:ntile, :], 0, ntile * 128),
                v_d.v(v_d.ap[0:nk, h * 128:(h + 1) * 128].rearrange("(kt p) v -> p kt v", p=128)))
            def qk(kt):
                j = kt - 4 * tb
                c0 = 128 * j if j >= 0 else 0
                sp = next_pmm()
                mm(sp.s(c0, TB), Kn.s(kt * 128, (kt + 1) * 128), QnT.c(h, c0, TB), start=True, stop=False)
                mm(sp.s(c0, TB), Kr.s(kt * 128, (kt + 1) * 128, 0, 64), QrT.c(h, c0, TB, 0, 64), start=False, stop=True)
                pt = pts[kt % 3]
                act(pt.s(c0, TB), sp.s(c0, TB), AF.Exp, bias=-SM_SHIFT, scale=SM_SCALE)
                if j >= 0:
                    memset(pt.s(c0, c0 + 64, 64, 128), 0.0)
                return pt, c0

            nxt = qk(0)
            for kt in range(ntile):
                pt, c0 = nxt
                if kt + 1 < ntile:
                    nxt = qk(kt + 1)
                mm(O_ps.s(c0, TB), Vh.c(kt), pt.s(c0, TB), start=(kt == 0), stop=(kt == ntile - 1))
                mm(D_ps.s(c0, TB), ones_b.all(), pt.s(c0, TB), start=(kt == 0), stop=(kt == ntile - 1))
            recip(rden.all(), D_ps.all())
            tt(oTt.c(h), O_ps.all(), rden.all(), ALU.mult)
        dma(oT_d.v(oT_d.ap.rearrange("(h p) t -> p h t", p=128)), oTt.all())

        if STOP == 'B':
            return
        R1.reset()
        R2.reset()
        ybT = R1.alloc("ybT", [128, 32, TB], BF16)
        class _Sm:
            def __init__(self, t):
                self.t = t

            def __getitem__(self, j):
                t = self.t

                class _J:
                    def all(self_):
                        return t.c(j)

                    def s(self_, a, b):
                        return t.c(j, a, b)
                return _J()
        sm = {nm: _Sm(R2.alloc(nm, [128, 4, 64], F32)) for nm in ("dt_tm", "acum", "wdec", "ea", "cd0", "cd1", "acm2")}
        win = R2.alloc("win", [128, TB + 3], F32)
        acc = R2.alloc("acc", [128, TB], F32)
        RW = Region(wbuf[0].off, 2 * wbuf[0].nbytes)

        def ssd_set(reg, sfx, banks, reg_y):
            d = {}
            d["xg"] = reg.alloc("xg" + sfx, [128, 4, TB], F32)
            d["Bgf"] = reg.alloc("Bgf" + sfx, [128, TB], F32)
            d["BgT"] = reg.alloc("BgT" + sfx, [128, TB], BF16)
            d["CgT"] = reg.alloc("CgT" + sfx, [128, TB], BF16)
            d["xtm"] = reg.alloc("xtm" + sfx, [128, TB], BF16)
            d["skipt"] = reg.alloc("skipt" + sfx, [128, TB], F32)
            d["Blo"] = reg.alloc("Blo" + sfx, [128, 128], BF16)
            d["Bhi"] = reg.alloc("Bhi" + sfx, [128, 128], BF16)
            d["Clo"] = reg.alloc("Clo" + sfx, [128, 128], BF16)
            d["Chi"] = reg.alloc("Chi" + sfx, [128, 128], BF16)
            d["cbs"] = reg.alloc("cbs" + sfx, [128, 128], F32)
            d["rdg"] = reg.alloc("rdg" + sfx, [128, 8, 128], F32)
            d["Dm"] = d["rdg"]
            d["mT"] = reg.alloc("mT" + sfx, [128, 8, 128], BF16)
            d["xw"] = reg.alloc("xw" + sfx, [128, TB], BF16)
            d["hb0"] = reg.alloc("hb0" + sfx, [128, TB], BF16)
            d["hb1"] = reg.alloc("hb1" + sfx, [128, TB], BF16)
            d["ysb"] = reg.alloc("ysb" + sfx, [128, TB], F32)
            d["yTg"] = reg_y.alloc("yTg" + sfx, [128, 4, TB], F32)
            d["sqg"] = Tile(P, "sqg" + sfx, [128, 4, TB], BF16, off=d["rdg"].off)
            d["zt"] = Tile(P, "zt" + sfx, [128, TB], F32, off=d["skipt"].off)
            d["rstd"] = Tile(P, "rstd" + sfx, [128, TB], F32, off=d["ysb"].off)
            d["banks"] = banks
            return d

        sets = [ssd_set(R2, "_a", [ps[0], ps[1], ps[2], ps[3]], R2), ssd_set(RW, "_b", [ps[4], ps[5], ps[6], ps[7]], R2)]

        act(dtvT.all(), dtraw.all(), AF.Exp, bias=dtb.all())
        act(dtvT.all(), dtvT.all(), AF.Ln, bias=1.0)
        if STOP == 'D0a':
            return
        for j in range(4):
            tr(ps[6].s(0, 128), dtvT.s(j * 128, (j + 1) * 128), ident_f.all())
            vcopy(sm["dt_tm"][j].all(), ps[6].s(0, 64))
            if STOP == 'D0b':
                return
            a_t = next_tmp()
            tt(a_t.s(0, 64), sm["dt_tm"][j].all(), A_bc.all(), ALU.mult)
            mm(ps[7].s(0, 64), tri.all(), a_t.s(0, 64))
            if STOP == 'D0c':
                return
            vcopy(sm["acum"][j].all(), ps[7].s(0, 64))
            act(sm["acm2"][j].all(), sm["dt_tm"][j].all(), AF.Ln)
            tt(sm["acm2"][j].all(), sm["acum"][j].all(), sm["acm2"][j].all(), ALU.subtract)
            act(sm["ea"][j].all(), ps[7].s(0, 64), AF.Exp)
            if STOP == 'D0d':
                return
            mm(ps[6].s(0, 64), sel_last.all(), sm["acum"][j].all())
            tt(sm["wdec"][j].all(), ps[6].s(0, 64), sm["acum"][j].all(), ALU.subtract)
            act(sm["wdec"][j].all(), sm["wdec"][j].all(), AF.Exp)
            tt(sm["wdec"][j].all(), sm["wdec"][j].all(), sm["dt_tm"][j].all(), ALU.mult)
            mm(ps[7].s(0, 64), sel_c0.all(), sm["acum"][j].all())
            act(sm["cd0"][j].all(), ps[7].s(0, 64), AF.Exp)
            mm(ps[6].s(0, 64), sel_c1.all(), sm["acum"][j].all())
            act(sm["cd1"][j].all(), ps[6].s(0, 64), AF.Exp)

        if STOP == 'D0':
            return
        def conv_chunk(ci, out_v):
            dma(win.all(), xbc_d.v(xbc_d.ap[ci * 128:(ci + 1) * 128, t0:t0 + TB + 3]))
            ts(acc.all(), win.s(0, TB), convw.c(0, ci, ci + 1), None, ALU.mult)
            for tap in range(1, 4):
                stt(acc.all(), win.s(tap, tap + TB), convw.c(tap, ci, ci + 1), acc.all(), ALU.mult, ALU.add)
            act(out_v, acc.all(), AF.Silu, bias=convb.s(ci, ci + 1))

        def ssd_group(g, d):
            g8 = g * 8
            b0, b1, b2, b3 = d["banks"]
            xg, Bgf, BgT, CgT, xtm, skipt = d["xg"], d["Bgf"], d["BgT"], d["CgT"], d["xtm"], d["skipt"]
            Blo, Bhi, Clo, Chi, cbs, rdg, Dm, mT = d["Blo"], d["Bhi"], d["Clo"], d["Chi"], d["cbs"], d["rdg"], d["Dm"], d["mT"]
            xw, hb0, hb1, ysb, yTg, sqg, zt, rstd_g = d["xw"], d["hb0"], d["hb1"], d["ysb"], d["yTg"], d["sqg"], d["zt"], d["rstd"]
            for c in range(4):
                conv_chunk(g * 4 + c, xg.c(c))
                yield
            conv_chunk(32 + g, Bgf.all())
            vcopy(BgT.all(), Bgf.all())
            conv_chunk(40 + g, CgT.all())
            yield
            hs_g = hS.c(g)
            for j in range(4):
                a_cum, dtt, wd, eav, c0v, c1v, a_c2 = (sm[n][j] for n in ("acum", "dt_tm", "wdec", "ea", "cd0", "cd1", "acm2"))
                for c in range(4):
                    tr(b0.s(c * 128, (c + 1) * 128), xg.c(c, j * 128, (j + 1) * 128), ident_f.all())
                tr(b1.s(0, 128), Bgf.s(j * 128, (j + 1) * 128), ident_f.all())
                mm(b2.s(0, 128), BgT.s(j * 128, (j + 1) * 128), CgT.s(j * 128, (j + 1) * 128))
                yield
                vcopy(xtm.all(), b0.all())
                x3 = b0.v(b0.t[:, :].rearrange("p (r q) -> p r q", q=64))
                tt(skipt.v(skipt.t[:, :].rearrange("p (r q) -> p r q", q=64)), x3,
                   bc_last(D_bc.s(g8, g8 + 8), 64), ALU.mult)
                tt(xw.v(xw.t[:, :].rearrange("p (r q) -> p r q", q=64)), x3,
                   bc_last(wd.s(g8, g8 + 8), 64), ALU.mult)
                memset(Blo.all(), 0.0, eng="gpsimd")
                memset(Bhi.all(), 0.0, eng="gpsimd")
                act(Blo.s(0, 128, 0, 64), b1.s(0, 128, 0, 64), AF.Copy)
                act(Bhi.s(0, 128, 64, 128), b1.s(0, 128, 64, 128), AF.Copy)
                memset(Clo.all(), 0.0, eng="gpsimd")
                memset(Chi.all(), 0.0, eng="gpsimd")
                vcopy(Clo.s(0, 64), CgT.s(j * 128, j * 128 + 64), eng="gpsimd")
                vcopy(Chi.s(64, 128), CgT.s(j * 128 + 64, (j + 1) * 128), eng="gpsimd")
                vcopy(cbs.all(), b2.s(0, 128))
                tt(rdg.all(), bc_mid(ident_f.all(), ident_f, 8), bc_last(a_cum.s(g8, g8 + 8), 128), ALU.mult)
                yield
                mm(b3.all(), ones_f.all(), rdg.v(rdg.t[:, 0:4, :], 0, 512))
                mm(b1.all(), ones_f.all(), rdg.v(rdg.t[:, 4:8, :], 512, 1024))
                yield
                for q, rb in ((0, b3), (1, b1)):
                    tt(Dm.v(Dm.t[:, q * 4:(q + 1) * 4, :], q * 512, (q + 1) * 512),
                       rb.v(rb.t[:, :].rearrange("p (h t) -> p h t", t=128)),
                       bc_last(a_c2.s(g8 + q * 4, g8 + q * 4 + 4), 128), ALU.subtract)
                tt(Dm.all(), Dm.all(), bc_mid(negmask.all(), negmask, 8), ALU.add)
                yield
                act(Dm.all(), Dm.all(), AF.Exp)
                yield
                tt(mT.all(), Dm.all(), bc_mid(cbs.all(), cbs, 8), ALU.mult)
                yield
                for r in range(8):
                    mm(b0.s(r * 64, (r + 1) * 64), mT.c(r), xtm.s(r * 64, (r + 1) * 64))
                mm(b2.all(), Blo.all(), xw.all())
                mm(b3.all(), Bhi.all(), xw.all())
                yield
                act(hb0.all(), hs_g, AF.Copy)
                h3 = V(hs_g.ap.rearrange("p (r q) -> p r q", q=64), hs_g.pages)
                tt(h3, h3, bc_last(c0v.s(g8, g8 + 8), 64), ALU.mult)
                tt(hs_g, hs_g, b2.all(), ALU.add)
                act(hb1.all(), hs_g, AF.Copy)
                tt(h3, h3, bc_last(c1v.s(g8, g8 + 8), 64), ALU.mult)
                tt(hs_g, hs_g, b3.all(), ALU.add)
                yield
                mm(b1.all(), Clo.all(), hb0.all(), start=True, stop=False)
                mm(b1.all(), Chi.all(), hb1.all(), start=False, stop=True)
                yield
                y3 = ysb.v(ysb.t[:, :].rearrange("p (r q) -> p r q", q=64))
                tt(y3, b1.v(b1.t[:, :].rearrange("p (r q) -> p r q", q=64)), bc_last(eav.s(g8, g8 + 8), 64), ALU.mult)
                tt(ysb.all(), ysb.all(), b0.all(), ALU.add)
                tt(ysb.all(), ysb.all(), skipt.all(), ALU.add)
                yield
                for c in range(4):
                    tr(b2.s(c * 128, (c + 1) * 128), ysb.s(c * 128, (c + 1) * 128), ident_f.all())
                yield
                act(yTg.v(yTg.t[:, :, j * 128:(j + 1) * 128]), b2.v(b2.t[:, :].rearrange("p (c t) -> p c t", t=128)), AF.Copy)
                yield
            for c in range(4):
                ci = g * 4 + c
                dma(zt.all(), zs_d.v(zs_d.ap[ci * 128:(ci + 1) * 128, :]))
                tt(yTg.c(c), yTg.c(c), zt.all(), ALU.mult)
                act(sqg.c(c), yTg.c(c), AF.Square)
                yield
            for c in range(4):
                mm(b3.all(), ones_b.all(), sqg.c(c), start=(c == 0), stop=(c == 3))
            yield
            act(rstd_g.all(), b3.all(), AF.Sqrt, bias=EPS, scale=1.0 / 512)
            recip(rstd_g.all(), rstd_g.all())
            yield
            for c in range(4):
                ci = g * 4 + c
                stt(ybT.c(ci), yTg.c(c), ssmw.s(ci, ci + 1), rstd_g.all(), ALU.mult, ALU.mult)
            yield

        for gp in range(0, SSM_GROUPS, 2):
            gens = [ssd_group(gp, sets[0]), ssd_group(gp + 1, sets[1])]
            while gens:
                for gn in list(gens):
                    try:
                        next(gn)
                    except StopIteration:
                        gens.remove(gn)

        if STOP == 'D':
            return
        R2.reset()
        pmm_n[0] = 6
        oTt = R2.alloc("oT2", [128, 16, TB], BF16)
        mga = R2.alloc("mga", [128, 16, TB], BF16)
        mgb = R2.alloc("mgb", [128, 16, TB], BF16)
        dma(oTt.all(), oT_d.v(oT_d.ap.rearrange("(h p) t -> p h t", p=128)))

        def cons_ya(tag, pb):
            c = tag
            dma(gt.all(), g_d.v(g_d.ap[c * 128:(c + 1) * 128, :]))
            tt(mga.c(c), pb.all(), gt.all(), ALU.mult)

        dense(W["w_o_mla"], l, 0, 16, lambda kc: oTt.c(kc), [(c * 128, 128, c) for c in range(16)], cons_ya)

        def cons_yb(tag, pb):
            c = tag
            dma(gt.all(), g_d.v(g_d.ap[(16 + c) * 128:(17 + c) * 128, :]))
            t = next_tmp()
            tt(t.all(), pb.all(), gt.all(), ALU.mult)
            tt(mgb.c(c), t.all(), mga.c(c), ALU.add)

        dense(W["w_o_ssm"], l, 0, 32, lambda kc: ybT.c(kc), [(c * 128, 128, c) for c in range(16)], cons_yb)

        def cons_res(tag, pb):
            tt(X.c(tag), X.c(tag), pb.all(), ALU.add)

        dense(W["w_out"], l, 0, 16, lambda kc: mgb.c(kc), [(c * 128, 128, c) for c in range(16)], cons_res)

        if STOP == 'E':
            return
        R1.reset()
        R2.reset()
        uT = R1.alloc("uT", [128, 32, TB], BF16)
        sqt = R2.alloc("sq", [128, 16, TB], BF16)
        hT = R2.alloc("hT", [128, 16, TB], BF16)
        R2sq[0] = sqt
        rmsnorm_big(X, nlw, hT, 16, D_MODEL)
        for half in range(2):
            def cons_up(tag, pb):
                t = next_tmp()
                act(t.all(), pb.all(), AF.Relu)
                tt(uT.c(tag), t.all(), t.all(), ALU.mult)

            dense(W["w_up"], l, 0, 16, lambda kc: hT.c(kc),
                  [(half * 4096 + c * 128, 128, c) for c in range(32)], cons_up)
            for kq in range(2):
                dense(W["w_down"], l, half * 4096 + kq * 2048, 16, lambda kc, kq=kq: uT.c(kq * 16 + kc),
                      [(c * 128, 128, c) for c in range(16)], cons_res)

        if STOP == 'F':
            return
        R1.reset()
        eT = R1.alloc("eT", [128, 16, TB], F32)
        pst = R2.alloc("pst", [128, PLE_DIM], F32)
        pTb = R2.alloc("pTb", [128, 2, TB], BF16)
        rmsnorm_big(X, npw, hT, 16, D_MODEL)
        for j in range(4):
            dma(pst.all(), p_d.v(p_d.ap[l, t0 + j * 128:t0 + (j + 1) * 128, :]))
            for c in range(2):
                tr(ps[6].s(c * 128, (c + 1) * 128), pst.s(c * 128, (c + 1) * 128), ident_f.all())
            vcopy(pTb.v(pTb.t[:, :, j * 128:(j + 1) * 128]), ps[6].v(ps[6].t[:, 0:256].rearrange("p (c t) -> p c t", t=128)))

        def cons_e(tag, pb):
            act(eT.c(tag), pb.all(), AF.Copy)

        dense(W["w_ple"], l, 0, 2, lambda kc: pTb.c(kc), [(c * 128, 128, c) for c in range(16)], cons_e)

        def cons_pg(tag, pb):
            t = next_tmp()
            act(t.all(), pb.all(), AF.Sigmoid)
            tt(t.all(), t.all(), eT.c(tag), ALU.mult)
            tt(X.c(tag), X.c(tag), t.all(), ALU.add)

        dense(W["w_ple_gate"], l, 0, 16, lambda kc: hT.c(kc), [(c * 128, 128, c) for c in range(16)], cons_pg)

        if STOP == 'G':
            return
        pmm_n[0] = 4
        if not last:
            dma(xT_d.v(xT_d.ap[:, t0:t0 + TB].rearrange("(kc p) t -> p kc t", p=128)), X.all())
        else:
            R2.reset()
            ost = R2.alloc("ost", [128, D_MODEL], F32)
            for j in range(4):
                for q in range(4):
                    pb = next_pmm()
                    for i in range(4):
                        kc = q * 4 + i
                        tr(pb.s(i * 128, (i + 1) * 128), X.c(kc, j * 128, (j + 1) * 128), ident_f.all())
                    vcopy(ost.s(q * 512, (q + 1) * 512), pb.all())
                fin_ops.append(dma(out_d.v(out_d.ap[t0 + j * 128:t0 + (j + 1) * 128, :]), ost.all()))

    setup_consts()
    for l in range(L):
        if STOP == 'setup':
            break
        P.epoch = l
        layer_consts(l)
        if STOP == 'lconsts':
            break
        memset(hS.all(), 0.0)
        for tb in range(NBLK):
            block(l, tb)
    P.emit(final_waits=fin_ops)
    return nc, len(P.ops)


_CACHE = {}


def _invf_table():
    j = np.arange(0, QK_ROPE, 2, dtype=np.float32) / np.float32(QK_ROPE)
    inv = (np.float32(1.0) / (np.float32(10000.0) ** j)).astype(np.float32)
    t = np.zeros((1, 128), np.float32)
    t[0, 0:32] = inv
    t[0, 32:64] = inv
    return t


def kernel(**inputs):
    x = np.ascontiguousarray(inputs["x"], dtype=np.float32)
    B, S, _ = x.shape
    L = inputs["w_in"].shape[0]
    key = (L, S)
    if key not in _CACHE:
        _CACHE[key] = build(L=L, S=S)[0]
    nc = _CACHE[key]
    in_maps = []
    for b in range(B):
        m = {"x": x[b], "p": np.ascontiguousarray(inputs["p"][:, b]),
             "positions": np.ascontiguousarray(inputs["positions"][b:b + 1]).astype(np.int32),
             "invf": _invf_table()}
        for n in WNAMES:
            m[n] = np.ascontiguousarray(inputs[n], dtype=np.float32)
        in_maps.append(m)
    res = run_bass_kernel_spmd(nc, in_maps, core_ids=list(range(B)))
    return np.stack([res.results[b]["out"] for b in range(B)], axis=0).astype(np.float32)
```

```python
import math
import numpy as np
import concourse.bass as bass
import concourse.mybir as mybir
from concourse.bass_utils import run_bass_kernel_spmd

F32 = mybir.dt.float32
BF16 = mybir.dt.bfloat16
I32 = mybir.dt.int32
AF = mybir.ActivationFunctionType
ALU = mybir.AluOpType
PG = 512
NSLOT = 8
DTSZ = {F32: 4, BF16: 2, I32: 4}

D_MODEL = 2048
SEQ = 4096
DEPTH = 4
EPS = 1e-6
HEADS = 16
Q_LORA = 512
KV_LORA = 512
QK_NOPE = 128
QK_ROPE = 64
V_DIM = 128
QK_DIM = 192
D_INNER = 4096
SSM_HEADS = 64
SSM_GROUPS = 8
SSM_STATE = 128
CONV_DIM = 6144
D_FF = 8192
PLE_DIM = 256
D_IN_PROJ = 15488
TB = 512
STOP = None
SM_SCALE = QK_DIM ** -0.5
SM_SHIFT = 14.0


class Page:
    __slots__ = ("w", "r")

    def __init__(self):
        self.w = {}
        self.r = {}


class V:
    __slots__ = ("ap", "pages", "excl")

    def __init__(self, ap, pages, excl=False):
        self.ap = ap
        self.pages = pages
        self.excl = excl


class Tile:
    def __init__(self, prog, name, shape, dtype, space="sbuf", off=None):
        self.name, self.shape, self.dtype = name, list(shape), dtype
        nc = prog.nc
        self.esz = DTSZ[dtype]
        self.nbytes = int(np.prod(shape[1:])) * self.esz
        if space == "sbuf":
            assert off is not None and off % 32 == 0
            assert off + self.nbytes <= prog.sb_limit, f"SBUF overflow at {name}: {off + self.nbytes}"
            self.off = off
            self.t = nc.alloc_sbuf_tensor_at(name, list(shape), dtype, offset=off)
            self.pg = prog.sb_pages
        else:
            self.t = nc.alloc_psum_tensor(name, list(shape), dtype)
            self.off = 0
            self.pg = [Page()]
        self.excl = (space != "sbuf")
        self._cache = {}

    def _pages(self, lo, hi):
        if self.excl:
            return self.pg
        a, b = (self.off + lo) // PG, (self.off + hi - 1) // PG
        k = (a, b)
        p = self._cache.get(k)
        if p is None:
            p = self.pg[a:b + 1]
            self._cache[k] = p
        return p

    def all(self):
        return V(self.t[:], self._pages(0, self.nbytes), self.excl)

    def v(self, ap, lo=None, hi=None):
        lo = 0 if lo is None else lo * self.esz
        hi = self.nbytes if hi is None else hi * self.esz
        return V(ap, self._pages(lo, hi), self.excl)

    def c(self, i, a=None, b=None, p0=0, p1=None):
        n = self.shape[2]
        a = 0 if a is None else a
        b = n if b is None else b
        p1 = self.shape[0] if p1 is None else p1
        return V(self.t[p0:p1, i, a:b], self._pages((i * n + a) * self.esz, (i * n + b) * self.esz), self.excl)

    def s(self, a=None, b=None, p0=0, p1=None):
        a = 0 if a is None else a
        b = self.shape[1] if b is None else b
        p1 = self.shape[0] if p1 is None else p1
        return V(self.t[p0:p1, a:b], self._pages(a * self.esz, b * self.esz), self.excl)


class Dram:
    def __init__(self, ap):
        self.ap = ap
        self.pg = [Page()]

    def v(self, ap=None):
        return V(self.ap if ap is None else ap, self.pg)


class Op:
    __slots__ = ("eng", "fn", "is_dma", "tl", "cnt", "waits", "need_inc", "snap", "val")


class Prog:
    def __init__(self, nc, same_engine_sync=True, sb_base=20992, sb_limit=229376):
        self.nc = nc
        self.ops = []
        self.same_engine_sync = same_engine_sync
        self.sb_base = sb_base
        self.sb_limit = sb_limit
        self.sb_pages = [Page() for _ in range(sb_limit // PG + 2)]
        self.known = {e: {} for e in ("tensor", "vector", "scalar", "gpsimd", "sync")}
        self.tl_count = {}
        self.tl_last = {}
        self.dma_rr = {"sync": 0, "gpsimd": 0, "scalar": 0}
        self.epoch = 0

    def op(self, eng, fn, reads=(), writes=(), dma=False):
        o = Op()
        o.eng, o.fn, o.is_dma, o.need_inc = eng, fn, dma, False
        if dma:
            slot = self.dma_rr[eng]
            self.dma_rr[eng] = (slot + 1) % NSLOT
            o.tl = ("dma", eng, slot, self.epoch)
            lastkey = ("dma", eng, slot)
        else:
            o.tl = ("eng", eng, self.epoch)
        xr = [v for v in reads if v.excl]
        if xr:
            reads = [v for v in reads if not v.excl]
            writes = list(writes) + xr
        k = self.known[eng]
        need = {}
        own = ("eng", eng)
        skip_own = (not dma) and (eng == "tensor" or not self.same_engine_sync)

        def consider(d):
            tl = d.tl
            if skip_own and tl[:2] == own:
                return
            if k.get(tl, 0) >= d.cnt:
                return
            cur = need.get(tl)
            if cur is None or cur.cnt < d.cnt:
                need[tl] = d

        if dma:
            prev = self.tl_last.get(lastkey)
            if prev is not None:
                consider(prev)
        for v in reads:
            for p in v.pages:
                for d in p.w.values():
                    consider(d)
        for v in writes:
            for p in v.pages:
                for d in p.w.values():
                    consider(d)
                for d in p.r.values():
                    consider(d)
        wl = []
        for tl, d in need.items():
            if k.get(tl, 0) >= d.cnt:
                continue
            wl.append(d)
            d.need_inc = True
            k[tl] = d.cnt
            for t2, c2 in d.snap.items():
                if k.get(t2, 0) < c2:
                    k[t2] = c2
        o.waits = wl
        c = self.tl_count.get(o.tl, 0) + 1
        self.tl_count[o.tl] = c
        o.cnt = c
        if dma:
            self.tl_last[lastkey] = o
        o.snap = dict(k)
        for v in reads:
            for p in v.pages:
                p.r[o.tl] = o
        for v in writes:
            for p in v.pages:
                if p.r:
                    p.r = {}
                    p.w = {o.tl: o}
                else:
                    p.w[o.tl] = o
        self.ops.append(o)
        return o

    def dma(self, eng, out, in_, **kw):
        oa, ia = out.ap, in_.ap
        return self.op(eng, lambda e: e.dma_start(out=oa, in_=ia, **kw), [in_], [out], dma=True)

    def emit(self, final_waits=()):
        nc = self.nc
        for d in final_waits:
            d.need_inc = True
        tl_val = {}
        for o in self.ops:
            if o.is_dma:
                o.val = tl_val[o.tl] = tl_val.get(o.tl, 0) + 16
            elif o.need_inc:
                o.val = tl_val[o.tl] = tl_val.get(o.tl, 0) + 1
        sems = {tl: nc.alloc_semaphore("s_" + "_".join(str(x) for x in tl)) for tl in tl_val}
        per_eng = {e: [] for e in self.known}
        for o in self.ops:
            per_eng[o.eng].append(o)

        def run(engname, e):
            for o in per_eng[engname]:
                for d in o.waits:
                    e.wait_ge(sems[d.tl], d.val)
                ins = o.fn(e)
                if o.is_dma:
                    ins.then_inc(sems[o.tl], 16)
                elif o.need_inc:
                    ins.then_inc(sems[o.tl], 1)
            if engname == "sync":
                for d in final_waits:
                    e.wait_ge(sems[d.tl], d.val)

        with nc.Block() as block:
            @block.sync
            def _(e):
                run("sync", e)

            @block.scalar
            def _(e):
                run("scalar", e)

            @block.vector
            def _(e):
                run("vector", e)

            @block.gpsimd
            def _(e):
                run("gpsimd", e)

            @block.tensor
            def _(e):
                run("tensor", e)


WNAMES = ["norm_mix_w", "w_in", "q_a_norm_w", "w_uq", "kv_a_norm_w", "w_ukv", "q_norm_w", "k_norm_w",
          "w_o_mla", "conv_w", "conv_b", "dt_bias", "a_log", "d_skip", "ssm_norm_w", "w_o_ssm", "w_out",
          "norm_mlp_w", "w_up", "w_down", "ple_norm_w", "w_ple_gate", "w_ple"]
WSHAPES = {
    "norm_mix_w": [D_MODEL], "w_in": [D_MODEL, D_IN_PROJ], "q_a_norm_w": [Q_LORA], "w_uq": [Q_LORA, HEADS * QK_DIM],
    "kv_a_norm_w": [KV_LORA], "w_ukv": [KV_LORA, HEADS * 256], "q_norm_w": [QK_DIM], "k_norm_w": [QK_DIM],
    "w_o_mla": [HEADS * V_DIM, D_MODEL], "conv_w": [4, CONV_DIM], "conv_b": [CONV_DIM], "dt_bias": [SSM_HEADS],
    "a_log": [SSM_HEADS], "d_skip": [SSM_HEADS], "ssm_norm_w": [D_INNER], "w_o_ssm": [D_INNER, D_MODEL],
    "w_out": [D_MODEL, D_MODEL], "norm_mlp_w": [D_MODEL], "w_up": [D_MODEL, D_FF], "w_down": [D_FF, D_MODEL],
    "ple_norm_w": [D_MODEL], "w_ple_gate": [D_MODEL, D_MODEL], "w_ple": [PLE_DIM, D_MODEL],
}


SAME_ENGINE_SYNC = True


def build(L=DEPTH, S=SEQ, dbg=False):
    NBLK = S // TB
    nc = bass.Bass("TRN2", target_bir_lowering=False)
    P = Prog(nc, same_engine_sync=SAME_ENGINE_SYNC)

    def din(name, shape, dt=F32):
        return Dram(nc.dram_tensor(name, shape, dt, kind="ExternalInput").ap())

    def dscr(name, shape, dt=F32):
        return Dram(nc.dram_tensor(name, shape, dt).ap())

    x_d = din("x", [S, D_MODEL])
    p_d = din("p", [L, S, PLE_DIM])
    pos_d = din("positions", [1, S], I32)
    invf_d = din("invf", [1, 128])
    W = {n: din(n, [L] + WSHAPES[n]) for n in WNAMES}
    out_d = Dram(nc.dram_tensor("out", [S, D_MODEL], F32, kind="ExternalOutput").ap())
    dbg_outs = {}

    xT_d = dscr("xT_s", [D_MODEL, S])
    xbc_d = dscr("xbc_s", [CONV_DIM, 3 + S])
    zs_d = dscr("zs_s", [D_INNER, TB])
    g_d = dscr("g_s", [2 * D_MODEL, TB])
    knT_d = dscr("knT_s", [HEADS, 128, S], BF16)
    krT_d = dscr("krT_s", [HEADS, 64, S], BF16)
    v_d = dscr("v_s", [S, HEADS * V_DIM], BF16)
    oT_d = dscr("oT_s", [HEADS * V_DIM, TB], BF16)
    cos_d = dscr("cos_s", [64, S])
    sin_d = dscr("sin_s", [64, S])

    cur = [P.sb_base]

    def alloc(name, shape, dt, at=None):
        if at is None:
            off = (cur[0] + PG - 1) // PG * PG
            t = Tile(P, name, shape, dt, off=off)
            cur[0] = off + t.nbytes
        else:
            t = Tile(P, name, shape, dt, off=at)
        return t

    class Region:
        def __init__(self, base, size):
            self.base, self.size, self.cur = base, size, base

        def reset(self):
            self.cur = self.base

        def alloc(self, name, shape, dt):
            off = (self.cur + PG - 1) // PG * PG
            t = Tile(P, name, shape, dt, off=off)
            self.cur = off + t.nbytes
            assert self.cur <= self.base + self.size, f"region overflow {name} {self.cur - self.base} > {self.size}"
            return t

    ident_f = alloc("ident_f", [128, 128], F32)
    ident_b = alloc("ident_b", [128, 128], BF16)
    ones_b = alloc("ones_b", [128, 128], BF16)
    ones_f = alloc("ones_f", [128, 128], F32)
    tri = alloc("tri", [128, 128], F32)
    negmask = alloc("negmask", [128, 128], F32)
    sel_last = alloc("sel_last", [128, 128], F32)
    sel_c0 = alloc("sel_c0", [128, 128], F32)
    sel_c1 = alloc("sel_c1", [128, 128], F32)
    rrot_b = alloc("rrot_b", [128, 128], BF16)
    invf = alloc("invf", [128, 1], F32)
    nmw = alloc("nmw", [128, 16], F32)
    nlw = alloc("nlw", [128, 16], F32)
    npw = alloc("npw", [128, 16], F32)
    qaw = alloc("qaw", [128, 4], F32)
    kvaw = alloc("kvaw", [128, 4], F32)
    qnw = alloc("qnw", [128, 2], F32)
    knw = alloc("knw", [128, 2], F32)
    convw = alloc("convw", [128, 4, 48], F32)
    convb = alloc("convb", [128, 48], F32)
    ssmw = alloc("ssmw", [128, 32], F32)
    dtb = alloc("dtb", [128, 1], F32)
    A_bc = alloc("A_bc", [128, 64], F32)
    D_bc = alloc("D_bc", [128, 64], F32)
    X = alloc("X", [128, 16, TB], F32)
    hS = alloc("hS", [128, 8, 512], F32)
    wbuf = [alloc(f"wbuf{i}", [128, 16 * 512], BF16) for i in range(2)]
    cosb = alloc("cosb", [128, TB], F32)
    sinb = alloc("sinb", [128, TB], F32)
    dtraw = alloc("dtraw", [128, TB], F32)
    dtvT = alloc("dtvT", [128, TB], F32)
    gt = alloc("gt", [128, TB], F32)
    rstd = alloc("rstd", [128, TB], F32)
    tmpA = alloc("tmpA", [128, TB], F32)
    tmpB = alloc("tmpB", [128, TB], F32)
    tmpC = alloc("tmpC", [128, TB], F32)
    base1 = (cur[0] + PG - 1) // PG * PG
    R1 = Region(base1, 32 * 1024)
    R2 = Region(base1 + 32 * 1024, P.sb_limit - (base1 + 32 * 1024))

    ps = [Tile(P, f"psb{i}", [128, 512], F32, space="psum") for i in range(8)]
    pmm_rr = [0]

    pmm_n = [4]

    def next_pmm():
        b = ps[pmm_rr[0] % pmm_n[0]]
        pmm_rr[0] += 1
        return b

    tmp_rr = [0]

    def next_tmp():
        t = (tmpA, tmpB, tmpC)[tmp_rr[0] % 3]
        tmp_rr[0] += 1
        return t

    def mm(out, lhsT, rhs, start=True, stop=True):
        oa, la, ra = out.ap, lhsT.ap, rhs.ap
        return P.op("tensor", lambda e: e.matmul(oa, la, ra, start=start, stop=stop), [lhsT, rhs], [out])

    def tr(out, in_, ident):
        oa, ia, da = out.ap, in_.ap, ident.ap
        return P.op("tensor", lambda e: e.transpose(oa, ia, da), [in_, ident], [out])

    def act(out, in_, func, bias=None, scale=1.0, eng="scalar"):
        oa, ia = out.ap, in_.ap
        rd = [in_]
        kw = {}
        if isinstance(bias, V):
            rd.append(bias)
            kw["bias"] = bias.ap
        elif bias is not None:
            kw["bias"] = float(bias)
        return P.op("scalar", lambda e: e.activation(out=oa, in_=ia, func=func, scale=scale, **kw), rd, [out])

    def vcopy(out, in_, eng="vector"):
        oa, ia = out.ap, in_.ap
        return P.op(eng, lambda e: e.tensor_copy(oa, ia), [in_], [out])

    def tt(out, a, b, op, eng="vector"):
        oa, aa, ba = out.ap, a.ap, b.ap
        return P.op(eng, lambda e: e.tensor_tensor(oa, aa, ba, op), [a, b], [out])

    def ts(out, a, s1, s2, op0, op1=None, eng="vector"):
        oa, aa = out.ap, a.ap
        rd = [a]
        s1a = s1
        if isinstance(s1, V):
            rd.append(s1)
            s1a = s1.ap
        s2a = s2
        if isinstance(s2, V):
            rd.append(s2)
            s2a = s2.ap
        if op1 is None:
            return P.op(eng, lambda e: e.tensor_scalar(oa, aa, s1a, s2a, op0), rd, [out])
        return P.op(eng, lambda e: e.tensor_scalar(oa, aa, s1a, s2a, op0, op1), rd, [out])

    def stt(out, a, sc, b, op0, op1, eng="vector"):
        oa, aa, ba = out.ap, a.ap, b.ap
        rd = [a, b]
        sa = sc
        if isinstance(sc, V):
            rd.append(sc)
            sa = sc.ap
        return P.op(eng, lambda e: e.scalar_tensor_tensor(oa, aa, sa, ba, op0, op1), rd, [out])

    def memset(out, val, eng="vector"):
        oa = out.ap
        return P.op(eng, lambda e: e.memset(oa, val), [], [out])

    def recip(out, in_):
        oa, ia = out.ap, in_.ap
        return P.op("vector", lambda e: e.reciprocal(oa, ia), [in_], [out])

    def dma(out, in_, eng="sync"):
        return P.dma(eng, out, in_)

    def bc_mid(v, tile, n):
        ap = v.ap
        return V(ap.unsqueeze(1).to_broadcast([ap.shape[0], n, ap.shape[1]]), v.pages)

    def bc_last(v, n):
        ap = v.ap
        return V(ap.unsqueeze(2).to_broadcast([ap.shape[0], ap.shape[1], n]), v.pages)

    def setup_consts():
        io = R2.alloc("c_io", [128, 128], F32)
        ip = R2.alloc("c_ip", [128, 1], F32)
        jh = R2.alloc("c_jh", [128, 128], F32)
        ph = R2.alloc("c_ph", [128, 1], F32)
        t1 = R2.alloc("c_t1", [128, 128], F32)
        t2 = R2.alloc("c_t2", [128, 128], F32)
        P.op("gpsimd", lambda e: e.iota(io.t[:], pattern=[[1, 128]], base=0, channel_multiplier=0,
                                        allow_small_or_imprecise_dtypes=True), [], [io.all()])
        P.op("gpsimd", lambda e: e.iota(ip.t[:], pattern=[[0, 1]], base=0, channel_multiplier=1,
                                        allow_small_or_imprecise_dtypes=True), [], [ip.all()])
        ts(ident_f.all(), io.all(), ip.all(), None, ALU.is_equal)
        vcopy(ident_b.all(), ident_f.all())
        memset(ones_b.all(), 1.0)
        memset(ones_f.all(), 1.0)
        ts(jh.all(), io.all(), 64.0, None, ALU.is_ge)
        ts(ph.all(), ip.all(), 64.0, None, ALU.is_ge)
        ts(t1.all(), io.all(), ip.all(), None, ALU.is_ge)
        ts(t2.all(), jh.all(), ph.all(), None, ALU.is_equal)
        tt(tri.all(), t1.all(), t2.all(), ALU.mult)
        ts(negmask.all(), tri.all(), -1.0, 30000.0, ALU.add, ALU.mult)
        ts(t1.all(), jh.all(), 64.0, 63.0, ALU.mult, ALU.add)
        ts(sel_last.all(), t1.all(), ip.all(), None, ALU.is_equal)
        memset(t1.all(), 63.0)
        ts(sel_c0.all(), t1.all(), ip.all(), None, ALU.is_equal)
        memset(t1.all(), 127.0)
        ts(sel_c1.all(), t1.all(), ip.all(), None, ALU.is_equal)
        ts(t1.all(), io.all(), -32.0, None, ALU.add)
        ts(t1.all(), t1.all(), ip.all(), None, ALU.is_equal)
        ts(t2.all(), io.all(), 32.0, None, ALU.add)
        ts(t2.all(), t2.all(), ip.all(), None, ALU.is_equal)
        tt(t1.all(), t1.all(), t2.all(), ALU.subtract)
        vcopy(rrot_b.all(), t1.all())
        memset(dtraw.all(), 0.0)
        memset(hS.all(), 0.0)
        st = R2.alloc("c_st", [128, 128], F32)
        memset(st.all(), 0.0)
        dma(st.s(0, 128, 0, 1), invf_d.v())
        tr(ps[6].s(0, 128), st.all(), ident_f.all())
        vcopy(invf.all(), ps[6].s(0, 1))
        memset(tmpA.all(), 0.0)
        for j in range(CONV_DIM // 128):
            dma(xbc_d.v(xbc_d.ap[j * 128:(j + 1) * 128, 0:3]), tmpA.s(0, 3))
        R2.reset()
        TWO_PI = 2.0 * math.pi
        C1 = 6.28125
        C2 = TWO_PI - C1
        MAGIC = 12582912.0
        for c0 in range(0, S, 2048):
            n = min(2048, S - c0)
            pi_ = R2.alloc("r_pi", [64, 2048], I32)
            ang = R2.alloc("r_ang", [64, 2048], F32)
            nn = R2.alloc("r_n", [64, 2048], F32)
            rr = R2.alloc("r_r", [64, 2048], F32)
            dma(pi_.s(0, n), pos_d.v(pos_d.ap[0, c0:c0 + n].partition_broadcast(64)))
            vcopy(ang.s(0, n), pi_.s(0, n))
            ts(ang.s(0, n), ang.s(0, n), invf.s(0, 1, 0, 64), None, ALU.mult)
            ts(nn.s(0, n), ang.s(0, n), 1.0 / TWO_PI, MAGIC, ALU.mult, ALU.add)
            ts(nn.s(0, n), nn.s(0, n), -MAGIC, None, ALU.add)
            stt(rr.s(0, n), nn.s(0, n), -C1, ang.s(0, n), ALU.mult, ALU.add)
            stt(rr.s(0, n), nn.s(0, n), -C2, rr.s(0, n), ALU.mult, ALU.add)
            ts(rr.s(0, n), rr.s(0, n), 3.1415925, -3.1415925, ALU.min, ALU.max)
            act(ang.s(0, n), rr.s(0, n), AF.Sin)
            dma(sin_d.v(sin_d.ap[:, c0:c0 + n]), ang.s(0, n))
            ts(nn.s(0, n), rr.s(0, n), -1.0, None, ALU.mult)
            tt(nn.s(0, n), nn.s(0, n), rr.s(0, n), ALU.max)
            act(ang.s(0, n), nn.s(0, n), AF.Sin, bias=math.pi / 2, scale=-1.0)
            dma(cos_d.v(cos_d.ap[:, c0:c0 + n]), ang.s(0, n))
            R2.reset()

    def load_cols(dst_v, src_ap_rows, nrows, ncols=128):
        st = R2.alloc("lc_st", [128, 128], F32)
        memset(st.all(), 0.0)
        dma(st.s(0, ncols, 0, nrows), Dram(src_ap_rows).v())
        tr(ps[6].s(0, 128), st.all(), ident_f.all())
        vcopy(dst_v, ps[6].s(0, nrows))
        R2.cur -= 0

    def layer_consts(l):
        R2.reset()
        for (t, nm, n) in ((nmw, "norm_mix_w", 16), (nlw, "norm_mlp_w", 16), (npw, "ple_norm_w", 16),
                           (qaw, "q_a_norm_w", 4), (kvaw, "kv_a_norm_w", 4), (ssmw, "ssm_norm_w", 32),
                           (convb, "conv_b", 48)):
            load_cols(t.all(), W[nm].ap[l].rearrange("(r c) -> r c", c=128), n)
            R2.reset()
        for tap in range(4):
            load_cols(convw.c(tap), W["conv_w"].ap[l, tap].rearrange("(r c) -> r c", c=128), 48)
            R2.reset()
        for (t, nm) in ((qnw, "q_norm_w"), (knw, "k_norm_w")):
            load_cols(t.s(0, 1), W[nm].ap[l, 0:128].rearrange("(r c) -> r c", c=128), 1)
            R2.reset()
            load_cols(t.s(1, 2), W[nm].ap[l, 128:192].rearrange("(r c) -> r c", c=64), 1, ncols=64)
            R2.reset()
        load_cols(dtb.all(), W["dt_bias"].ap[l].rearrange("(r c) -> r c", c=64), 1, ncols=64)
        R2.reset()
        dma(A_bc.all(), W["a_log"].v(W["a_log"].ap[l].partition_broadcast(128)))
        act(A_bc.all(), A_bc.all(), AF.Exp)
        ts(A_bc.all(), A_bc.all(), -1.0, None, ALU.mult)
        dma(D_bc.all(), W["d_skip"].v(W["d_skip"].ap[l].partition_broadcast(128)))

    wb_rr = [0]

    def dense(Wd, l, row0, KC, in_chunk, segs, consumer, gcols=None):
        if gcols is None:
            gcols = min(512, 8192 // KC)
        groups = []
        for sg in segs:
            c0, n, tag = sg
            if groups and groups[-1][0] + groups[-1][1] == c0 and (groups[-1][1] + n) <= gcols:
                groups[-1][1] += n
                groups[-1][2].append(sg)
            else:
                groups.append([c0, n, [sg]])
        pending = []
        for (g0, gw, sgs) in groups:
            wb = wbuf[wb_rr[0] % 2]
            wb_rr[0] += 1
            src = Wd.ap[l, row0:row0 + KC * 128, g0:g0 + gw].rearrange("(kc p) f -> p kc f", p=128)
            dst_ap = wb.t[:, 0:KC * gw].rearrange("p (kc f) -> p kc f", f=gw)
            dstv = wb.v(dst_ap, 0, KC * gw)
            P.dma("gpsimd", dstv, Wd.v(src))
            for (c0, n, tag) in sgs:
                pb = next_pmm()
                o = c0 - g0
                for kc in range(KC):
                    lw = wb.v(wb.t[:, kc * gw + o: kc * gw + o + n], kc * gw + o, kc * gw + o + n)
                    mm(pb.s(0, TB, 0, n), lw, in_chunk(kc), start=(kc == 0), stop=(kc == KC - 1))
                tail = consumer(tag, pb)
                if tail is not None:
                    if pending:
                        pending.pop(0)()
                    pending.append(tail)
        while pending:
            pending.pop(0)()

    def rmsnorm_big(src, wcols, dst, nch, D):
        sqt = R2sq[0]
        for kc in range(nch):
            act(sqt.c(kc), src.c(kc), AF.Square)
        for kc in range(nch):
            mm(ps[7].all(), ones_b.all(), sqt.c(kc), start=(kc == 0), stop=(kc == nch - 1))
        act(rstd.all(), ps[7].all(), AF.Sqrt, bias=EPS, scale=1.0 / D)
        recip(rstd.all(), rstd.all())
        for kc in range(nch):
            stt(dst.c(kc), src.c(kc), wcols.s(kc, kc + 1), rstd.all(), ALU.mult, ALU.mult)

    R2sq = [None]
    fin_ops = []

    def rope(dst_bf, src_f32, tmp_b, tmp_f, pbank=None):
        pbank = ps[6] if pbank is None else pbank
        vcopy(tmp_b, src_f32)
        mm(pbank.s(0, TB, 0, 64), rrot_b.s(0, 64, 0, 64), tmp_b)
        tt(tmp_f, pbank.s(0, TB, 0, 64), sinb.s(0, TB, 0, 64), ALU.mult)
        tt(src_f32, src_f32, cosb.s(0, TB, 0, 64), ALU.mult)
        tt(dst_bf, src_f32, tmp_f, ALU.add)

    def block(l, tb):
        t0 = tb * TB
        last = (l == L - 1)
        R1.reset()
        R2.reset()
        if l == 0:
            stg = R2.alloc("xs", [128, D_MODEL], F32)
            for j in range(4):
                dma(stg.all(), x_d.v(x_d.ap[t0 + j * 128: t0 + (j + 1) * 128, :]))
                for q in range(4):
                    pb = next_pmm()
                    for i in range(4):
                        kc = q * 4 + i
                        tr(pb.s(i * 128, (i + 1) * 128), stg.s(kc * 128, (kc + 1) * 128), ident_f.all())
                    dst = X.v(X.t[:, q * 4:(q + 1) * 4, j * 128:(j + 1) * 128], q * 4 * TB, (q + 1) * 4 * TB)
                    src = pb.v(pb.t[:, :].rearrange("p (c t) -> p c t", t=128))
                    vcopy(dst, src, eng="scalar" if False else "vector")
            R2.reset()
        else:
            dma(X.all(), xT_d.v(xT_d.ap[:, t0:t0 + TB].rearrange("(kc p) t -> p kc t", p=128)))
        dma(cosb.s(0, TB, 0, 64), cos_d.v(cos_d.ap[:, t0:t0 + TB]))
        dma(sinb.s(0, TB, 0, 64), sin_d.v(sin_d.ap[:, t0:t0 + TB]))

        if STOP == 'load':
            return
        sqt = R1.alloc("sq", [128, 16, TB], BF16)
        hT = R1.alloc("hT", [128, 16, TB], BF16)
        R2sq[0] = sqt
        cq = R2.alloc("cq", [128, 4, TB], F32)
        ckv = R2.alloc("ckv", [128, 4, TB], F32)
        kr = R2.alloc("kr", [128, TB], F32)
        rmsnorm_big(X, nmw, hT, 16, D_MODEL)
        segs = []
        for j in range(4):
            segs.append((j * 128, 128, ("cq", j)))
        for j in range(4):
            segs.append((512 + j * 128, 128, ("ckv", j)))
        segs.append((1024, 64, ("kr", 0)))
        for j in range(32):
            segs.append((1088 + j * 128, 128, ("z", j)))
        for j in range(48):
            segs.append((5184 + j * 128, 128, ("xbc", j)))
        segs.append((11328, 64, ("dt", 0)))
        for j in range(32):
            segs.append((11392 + j * 128, 128, ("g", j)))

        def cons_in(tag, pb):
            kind, j = tag
            if kind == "cq":
                act(cq.c(j), pb.all(), AF.Copy)
            elif kind == "ckv":
                act(ckv.c(j), pb.all(), AF.Copy)
            elif kind == "kr":
                act(kr.s(0, TB, 0, 64), pb.s(0, TB, 0, 64), AF.Copy)
            elif kind == "dt":
                vcopy(dtraw.s(0, TB, 0, 64), pb.s(0, TB, 0, 64))
            elif kind == "z":
                t = next_tmp()
                act(t.all(), pb.all(), AF.Silu)
                dma(zs_d.v(zs_d.ap[j * 128:(j + 1) * 128, :]), t.all())
            elif kind == "xbc":
                t = next_tmp()
                vcopy(t.all(), pb.all())
                dma(xbc_d.v(xbc_d.ap[j * 128:(j + 1) * 128, 3 + t0:3 + t0 + TB]), t.all())
            elif kind == "g":
                t = next_tmp()
                act(t.all(), pb.all(), AF.Sigmoid)
                dma(g_d.v(g_d.ap[j * 128:(j + 1) * 128, :]), t.all())

        pmm_n[0] = 6
        dense(W["w_in"], l, 0, 16, lambda kc: hT.c(kc), segs, cons_in)
        pmm_n[0] = 4

        if STOP == 'A':
            return
        R1.reset()
        QnT = R1.alloc("QnT", [128, 16, TB], BF16)
        QrT = R1.alloc("QrT", [128, 16, TB], BF16)
        cqn = R2.alloc("cqn", [128, 4, TB], BF16)
        ckvn = R2.alloc("ckvn", [128, 4, TB], BF16)
        sq4 = R2.alloc("sq4", [128, 4, TB], BF16)
        R2sq[0] = sq4
        hf = [R2.alloc(f"hf{i}", [128, TB], F32) for i in range(2)]
        hsets = []
        for i, (pa, pb_) in enumerate(((ps[6], ps[7]), (ps[4], ps[5]))):
            hsets.append(dict(
                hr=R2.alloc(f"hr{i}", [128, TB], F32), vf=R2.alloc(f"vf{i}", [128, TB], F32),
                tb16=R2.alloc(f"tb16{i}", [128, TB], BF16), tf32=R2.alloc(f"tf32{i}", [128, TB], F32),
                kob=R2.alloc(f"kob{i}", [128, TB], BF16), krb=R2.alloc(f"krb{i}", [128, TB], BF16),
                vtm=R2.alloc(f"vtm{i}", [128, 4, 128], BF16), rstd=R2.alloc(f"rstdh{i}", [128, TB], F32), pA=pa, pB=pb_))
        kr2 = R2.alloc("kr2", [128, TB], BF16)

        rmsnorm_big(cq, qaw, cqn, 4, Q_LORA)
        hf_rr = [0]
        cur_hf = [None]

        def head_stats(nope_v, rope_sq_v, rope_src_v, hs):
            sq = next_tmp()
            pB, rs = hs["pB"], hs["rstd"]
            sqb = V(sq.t[:, :].bitcast(BF16)[:, 0:TB], sq._pages(0, TB * 2))
            act(sqb, nope_v, AF.Square)
            mm(pB.all(), ones_b.all(), sqb, start=True, stop=False)
            if rope_sq_v is None:
                sq2 = V(sq.t[0:64, :].bitcast(BF16)[:, TB:2 * TB], sq._pages(TB * 2, TB * 4))
                act(sq2, rope_src_v, AF.Square)
                rope_sq_v = sq2
            mm(pB.all(), ones_b.s(0, 128, 0, 64), rope_sq_v, start=False, stop=True)
            act(rs.all(), pB.all(), AF.Sqrt, bias=EPS, scale=1.0 / QK_DIM)
            recip(rs.all(), rs.all())

        def cons_q(tag, pb):
            kind, h = tag
            hs = hsets[h % 2]
            hr, rs = hs["hr"], hs["rstd"]
            if kind == "qn":
                cur_hf[0] = hf[hf_rr[0] % 2]
                hf_rr[0] += 1
                act(cur_hf[0].all(), pb.all(), AF.Copy)
            else:
                act(hr.s(0, TB, 0, 64), pb.s(0, TB, 0, 64), AF.Copy)
                hfc = cur_hf[0]

                def tail():
                    head_stats(hfc.all(), None, hr.s(0, TB, 0, 64), hs)
                    stt(QnT.c(h), hfc.all(), qnw.s(0, 1), rs.all(), ALU.mult, ALU.mult)
                    stt(hr.s(0, TB, 0, 64), hr.s(0, TB, 0, 64), qnw.s(1, 2, 0, 64), rs.s(0, TB, 0, 64), ALU.mult, ALU.mult)
                    rope(QrT.c(h, None, None, 0, 64), hr.s(0, TB, 0, 64), hs["tb16"].s(0, TB, 0, 64), hs["tf32"].s(0, TB, 0, 64), hs["pA"])
                return tail

        segs = []
        for h in range(HEADS):
            segs.append((h * 192, 128, ("qn", h)))
            segs.append((h * 192 + 128, 64, ("qr", h)))
        dense(W["w_uq"], l, 0, 4, lambda kc: cqn.c(kc), segs, cons_q)

        rmsnorm_big(ckv, kvaw, ckvn, 4, KV_LORA)
        act(kr2.s(0, TB, 0, 64), kr.s(0, TB, 0, 64), AF.Square)

        def cons_kv(tag, pb):
            kind, h = tag
            hs = hsets[h % 2]
            hr, rs, kob, krb, vf, vtm = hs["hr"], hs["rstd"], hs["kob"], hs["krb"], hs["vf"], hs["vtm"]
            if kind == "kn":
                cur_hf[0] = hf[hf_rr[0] % 2]
                hf_rr[0] += 1
                act(cur_hf[0].all(), pb.all(), AF.Copy)
                return None
            act(vf.all(), pb.all(), AF.Copy)
            hfc = cur_hf[0]

            def tail():
                head_stats(hfc.all(), kr2.s(0, TB, 0, 64), None, hs)
                stt(kob.all(), hfc.all(), knw.s(0, 1), rs.all(), ALU.mult, ALU.mult)
                dma(knT_d.v(knT_d.ap[h, :, t0:t0 + TB]), kob.all())
                stt(hr.s(0, TB, 0, 64), kr.s(0, TB, 0, 64), knw.s(1, 2, 0, 64), rs.s(0, TB, 0, 64), ALU.mult, ALU.mult)
                rope(krb.s(0, TB, 0, 64), hr.s(0, TB, 0, 64), hs["tb16"].s(0, TB, 0, 64), hs["tf32"].s(0, TB, 0, 64), hs["pA"])
                dma(krT_d.v(krT_d.ap[h, :, t0:t0 + TB]), krb.s(0, TB, 0, 64))
                pt_ = hs["pA"]
                for j in range(4):
                    tr(pt_.s(j * 128, (j + 1) * 128), vf.s(j * 128, (j + 1) * 128), ident_f.all())
                vcopy(vtm.all(), pt_.v(pt_.t[:, :].rearrange("p (j v) -> p j v", v=128)))
                dst = v_d.ap[t0:t0 + TB, h * 128:(h + 1) * 128].rearrange("(j p) v -> p j v", p=128)
                dma(v_d.v(dst), vtm.all())
            return tail

        segs = []
        for h in range(HEADS):
            segs.append((h * 256, 128, ("kn", h)))
            segs.append((h * 256 + 128, 128, ("v", h)))
        dense(W["w_ukv"], l, 0, 4, lambda kc: ckvn.c(kc), segs, cons_kv)

        if STOP == 'A2':
            return
        R2.reset()
        nk = (tb + 1) * TB
        ntile = nk // 128
        RWa = Region(wbuf[0].off, 2 * wbuf[0].nbytes)
        kvs = []
        for reg, sfx in ((R2, "0"), (RWa, "1")):
            kvs.append((reg.alloc("Kn" + sfx, [128, S], BF16), reg.alloc("Kr" + sfx, [128, S], BF16),
                        reg.alloc("Vh" + sfx, [128, S // 128, 128], BF16)))
        pts = [R2.alloc(f"pt{i}", [128, TB], BF16) for i in range(3)]
        oTt = R2.alloc("oT", [128, 16, TB], BF16)
        rden = R2.alloc("rden", [128, TB], F32)
        for h in range(HEADS):
            Kn, Kr, Vh = kvs[h % 2]
            O_ps, D_ps = (ps[4], ps[5]) if h % 2 == 0 else (ps[6], ps[7])
            dma(Kn.s(0, nk), knT_d.v(knT_d.ap[h, :, 0:nk]))
            dma(Kr.s(0, nk, 0, 64), krT_d.v(krT_d.ap[h, :, 0:nk]))
            dma(Vh.v(Vh.t[:, 0:ntile, :], 0, ntile * 128),
                v_d.v(v_d.ap[0:nk, h * 128:(h + 1) * 128].rearrange("(kt p) v -> p kt v", p=128)))
            def qk(kt):
                j = kt - 4 * tb
                c0 = 128 * j if j >= 0 else 0
                sp = next_pmm()
                mm(sp.s(c0, TB), Kn.s(kt * 128, (kt + 1) * 128), QnT.c(h, c0, TB), start=True, stop=False)
                mm(sp.s(c0, TB), Kr.s(kt * 128, (kt + 1) * 128, 0, 64), QrT.c(h, c0, TB, 0, 64), start=False, stop=True)
                pt = pts[kt % 3]
                act(pt.s(c0, TB), sp.s(c0, TB), AF.Exp, bias=-SM_SHIFT, scale=SM_SCALE)
                if j >= 0:
                    memset(pt.s(c0, c0 + 64, 64, 128), 0.0)
                return pt, c0

            nxt = qk(0)
            for kt in range(ntile):
                pt, c0 = nxt
                if kt + 1 < ntile:
                    nxt = qk(kt + 1)
                mm(O_ps.s(c0, TB), Vh.c(kt), pt.s(c0, TB), start=(kt == 0), stop=(kt == ntile - 1))
                mm(D_ps.s(c0, TB), ones_b.all(), pt.s(c0, TB), start=(kt == 0), stop=(kt == ntile - 1))
            recip(rden.all(), D_ps.all())
            tt(oTt.c(h), O_ps.all(), rden.all(), ALU.mult)
        dma(oT_d.v(oT_d.ap.rearrange("(h p) t -> p h t", p=128)), oTt.all())

        if STOP == 'B':
            return
        R1.reset()
        R2.reset()
        ybT = R1.alloc("ybT", [128, 32, TB], BF16)
        class _Sm:
            def __init__(self, t):
                self.t = t

            def __getitem__(self, j):
                t = self.t

                class _J:
                    def all(self_):
                        return t.c(j)

                    def s(self_, a, b):
                        return t.c(j, a, b)
                return _J()
        sm = {nm: _Sm(R2.alloc(nm, [128, 4, 64], F32)) for nm in ("dt_tm", "acum", "wdec", "ea", "cd0", "cd1", "acm2")}
        win = R2.alloc("win", [128, TB + 3], F32)
        acc = R2.alloc("acc", [128, TB], F32)
        RW = Region(wbuf[0].off, 2 * wbuf[0].nbytes)

        def ssd_set(reg, sfx, banks, reg_y):
            d = {}
            d["xg"] = reg.alloc("xg" + sfx, [128, 4, TB], F32)
            d["Bgf"] = reg.alloc("Bgf" + sfx, [128, TB], F32)
            d["BgT"] = reg.alloc("BgT" + sfx, [128, TB], BF16)
            d["CgT"] = reg.alloc("CgT" + sfx, [128, TB], BF16)
            d["xtm"] = reg.alloc("xtm" + sfx, [128, TB], BF16)
            d["skipt"] = reg.alloc("skipt" + sfx, [128, TB], F32)
            d["Blo"] = reg.alloc("Blo" + sfx, [128, 128], BF16)
            d["Bhi"] = reg.alloc("Bhi" + sfx, [128, 128], BF16)
            d["Clo"] = reg.alloc("Clo" + sfx, [128, 128], BF16)
            d["Chi"] = reg.alloc("Chi" + sfx, [128, 128], BF16)
            d["cbs"] = reg.alloc("cbs" + sfx, [128, 128], F32)
            d["rdg"] = reg.alloc("rdg" + sfx, [128, 8, 128], F32)
            d["Dm"] = d["rdg"]
            d["mT"] = reg.alloc("mT" + sfx, [128, 8, 128], BF16)
            d["xw"] = reg.alloc("xw" + sfx, [128, TB], BF16)
            d["hb0"] = reg.alloc("hb0" + sfx, [128, TB], BF16)
            d["hb1"] = reg.alloc("hb1" + sfx, [128, TB], BF16)
            d["ysb"] = reg.alloc("ysb" + sfx, [128, TB], F32)
            d["yTg"] = reg_y.alloc("yTg" + sfx, [128, 4, TB], F32)
            d["sqg"] = Tile(P, "sqg" + sfx, [128, 4, TB], BF16, off=d["rdg"].off)
            d["zt"] = Tile(P, "zt" + sfx, [128, TB], F32, off=d["skipt"].off)
            d["rstd"] = Tile(P, "rstd" + sfx, [128, TB], F32, off=d["ysb"].off)
            d["banks"] = banks
            return d

        sets = [ssd_set(R2, "_a", [ps[0], ps[1], ps[2], ps[3]], R2), ssd_set(RW, "_b", [ps[4], ps[5], ps[6], ps[7]], R2)]

        act(dtvT.all(), dtraw.all(), AF.Exp, bias=dtb.all())
        act(dtvT.all(), dtvT.all(), AF.Ln, bias=1.0)
        if STOP == 'D0a':
            return
        for j in range(4):
            tr(ps[6].s(0, 128), dtvT.s(j * 128, (j + 1) * 128), ident_f.all())
            vcopy(sm["dt_tm"][j].all(), ps[6].s(0, 64))
            if STOP == 'D0b':
                return
            a_t = next_tmp()
            tt(a_t.s(0, 64), sm["dt_tm"][j].all(), A_bc.all(), ALU.mult)
            mm(ps[7].s(0, 64), tri.all(), a_t.s(0, 64))
            if STOP == 'D0c':
                return
            vcopy(sm["acum"][j].all(), ps[7].s(0, 64))
            act(sm["acm2"][j].all(), sm["dt_tm"][j].all(), AF.Ln)
            tt(sm["acm2"][j].all(), sm["acum"][j].all(), sm["acm2"][j].all(), ALU.subtract)
            act(sm["ea"][j].all(), ps[7].s(0, 64), AF.Exp)
            if STOP == 'D0d':
                return
            mm(ps[6].s(0, 64), sel_last.all(), sm["acum"][j].all())
            tt(sm["wdec"][j].all(), ps[6].s(0, 64), sm["acum"][j].all(), ALU.subtract)
            act(sm["wdec"][j].all(), sm["wdec"][j].all(), AF.Exp)
            tt(sm["wdec"][j].all(), sm["wdec"][j].all(), sm["dt_tm"][j].all(), ALU.mult)
            mm(ps[7].s(0, 64), sel_c0.all(), sm["acum"][j].all())
            act(sm["cd0"][j].all(), ps[7].s(0, 64), AF.Exp)
            mm(ps[6].s(0, 64), sel_c1.all(), sm["acum"][j].all())
            act(sm["cd1"][j].all(), ps[6].s(0, 64), AF.Exp)

        if STOP == 'D0':
            return
        def conv_chunk(ci, out_v):
            dma(win.all(), xbc_d.v(xbc_d.ap[ci * 128:(ci + 1) * 128, t0:t0 + TB + 3]))
            ts(acc.all(), win.s(0, TB), convw.c(0, ci, ci + 1), None, ALU.mult)
            for tap in range(1, 4):
                stt(acc.all(), win.s(tap, tap + TB), convw.c(tap, ci, ci + 1), acc.all(), ALU.mult, ALU.add)
            act(out_v, acc.all(), AF.Silu, bias=convb.s(ci, ci + 1))

        def ssd_group(g, d):
            g8 = g * 8
            b0, b1, b2, b3 = d["banks"]
            xg, Bgf, BgT, CgT, xtm, skipt = d["xg"], d["Bgf"], d["BgT"], d["CgT"], d["xtm"], d["skipt"]
            Blo, Bhi, Clo, Chi, cbs, rdg, Dm, mT = d["Blo"], d["Bhi"], d["Clo"], d["Chi"], d["cbs"], d["rdg"], d["Dm"], d["mT"]
            xw, hb0, hb1, ysb, yTg, sqg, zt, rstd_g = d["xw"], d["hb0"], d["hb1"], d["ysb"], d["yTg"], d["sqg"], d["zt"], d["rstd"]
            for c in range(4):
                conv_chunk(g * 4 + c, xg.c(c))
                yield
            conv_chunk(32 + g, Bgf.all())
            vcopy(BgT.all(), Bgf.all())
            conv_chunk(40 + g, CgT.all())
            yield
            hs_g = hS.c(g)
            for j in range(4):
                a_cum, dtt, wd, eav, c0v, c1v, a_c2 = (sm[n][j] for n in ("acum", "dt_tm", "wdec", "ea", "cd0", "cd1", "acm2"))
                for c in range(4):
                    tr(b0.s(c * 128, (c + 1) * 128), xg.c(c, j * 128, (j + 1) * 128), ident_f.all())
                tr(b1.s(0, 128), Bgf.s(j * 128, (j + 1) * 128), ident_f.all())
                mm(b2.s(0, 128), BgT.s(j * 128, (j + 1) * 128), CgT.s(j * 128, (j + 1) * 128))
                yield
                act(xtm.all(), b0.all(), AF.Copy)
                x3 = b0.v(b0.t[:, :].rearrange("p (r q) -> p r q", q=64))
                tt(skipt.v(skipt.t[:, :].rearrange("p (r q) -> p r q", q=64)), x3,
                   bc_last(D_bc.s(g8, g8 + 8), 64), ALU.mult)
                tt(xw.v(xw.t[:, :].rearrange("p (r q) -> p r q", q=64)), x3,
                   bc_last(wd.s(g8, g8 + 8), 64), ALU.mult)
                memset(Blo.all(), 0.0, eng="gpsimd")
                memset(Bhi.all(), 0.0, eng="gpsimd")
                act(Blo.s(0, 128, 0, 64), b1.s(0, 128, 0, 64), AF.Copy)
                act(Bhi.s(0, 128, 64, 128), b1.s(0, 128, 64, 128), AF.Copy)
                memset(Clo.all(), 0.0, eng="gpsimd")
                memset(Chi.all(), 0.0, eng="gpsimd")
                vcopy(Clo.s(0, 64), CgT.s(j * 128, j * 128 + 64), eng="gpsimd")
                vcopy(Chi.s(64, 128), CgT.s(j * 128 + 64, (j + 1) * 128), eng="gpsimd")
                act(cbs.all(), b2.s(0, 128), AF.Copy)
                tt(rdg.all(), bc_mid(ident_f.all(), ident_f, 8), bc_last(a_cum.s(g8, g8 + 8), 128), ALU.mult)
                yield
                mm(b3.all(), ones_f.all(), rdg.v(rdg.t[:, 0:4, :], 0, 512))
                mm(b1.all(), ones_f.all(), rdg.v(rdg.t[:, 4:8, :], 512, 1024))
                yield
                for q, rb in ((0, b3), (1, b1)):
                    tt(Dm.v(Dm.t[:, q * 4:(q + 1) * 4, :], q * 512, (q + 1) * 512),
                       rb.v(rb.t[:, :].rearrange("p (h t) -> p h t", t=128)),
                       bc_last(a_c2.s(g8 + q * 4, g8 + q * 4 + 4), 128), ALU.subtract)
                tt(Dm.all(), Dm.all(), bc_mid(negmask.all(), negmask, 8), ALU.add)
                yield
                act(Dm.all(), Dm.all(), AF.Exp)
                yield
                tt(mT.all(), Dm.all(), bc_mid(cbs.all(), cbs, 8), ALU.mult)
                yield
                for r in range(8):
                    mm(b0.s(r * 64, (r + 1) * 64), mT.c(r), xtm.s(r * 64, (r + 1) * 64))
                mm(b2.all(), Blo.all(), xw.all())
                mm(b3.all(), Bhi.all(), xw.all())
                yield
                act(hb0.all(), hs_g, AF.Copy)
                h3 = V(hs_g.ap.rearrange("p (r q) -> p r q", q=64), hs_g.pages)
                tt(h3, h3, bc_last(c0v.s(g8, g8 + 8), 64), ALU.mult)
                tt(hs_g, hs_g, b2.all(), ALU.add)
                act(hb1.all(), hs_g, AF.Copy)
                tt(h3, h3, bc_last(c1v.s(g8, g8 + 8), 64), ALU.mult)
                tt(hs_g, hs_g, b3.all(), ALU.add)
                yield
                mm(b1.all(), Clo.all(), hb0.all(), start=True, stop=False)
                mm(b1.all(), Chi.all(), hb1.all(), start=False, stop=True)
                yield
                y3 = ysb.v(ysb.t[:, :].rearrange("p (r q) -> p r q", q=64))
                tt(y3, b1.v(b1.t[:, :].rearrange("p (r q) -> p r q", q=64)), bc_last(eav.s(g8, g8 + 8), 64), ALU.mult)
                tt(ysb.all(), ysb.all(), b0.all(), ALU.add)
                tt(ysb.all(), ysb.all(), skipt.all(), ALU.add, eng="gpsimd")
                yield
                for c in range(4):
                    tr(b2.s(c * 128, (c + 1) * 128), ysb.s(c * 128, (c + 1) * 128), ident_f.all())
                yield
                act(yTg.v(yTg.t[:, :, j * 128:(j + 1) * 128]), b2.v(b2.t[:, :].rearrange("p (c t) -> p c t", t=128)), AF.Copy)
                yield
            for c in range(4):
                ci = g * 4 + c
                dma(zt.all(), zs_d.v(zs_d.ap[ci * 128:(ci + 1) * 128, :]))
                tt(yTg.c(c), yTg.c(c), zt.all(), ALU.mult)
                act(sqg.c(c), yTg.c(c), AF.Square)
                yield
            for c in range(4):
                mm(b3.all(), ones_b.all(), sqg.c(c), start=(c == 0), stop=(c == 3))
            yield
            act(rstd_g.all(), b3.all(), AF.Sqrt, bias=EPS, scale=1.0 / 512)
            recip(rstd_g.all(), rstd_g.all())
            yield
            for c in range(4):
                ci = g * 4 + c
                stt(ybT.c(ci), yTg.c(c), ssmw.s(ci, ci + 1), rstd_g.all(), ALU.mult, ALU.mult)
            yield

        pendg = list(range(SSM_GROUPS))
        active = {slot: ssd_group(pendg.pop(0), sets[slot]) for slot in (0, 1)}
        while active:
            for slot in list(active):
                try:
                    next(active[slot])
                except StopIteration:
                    if pendg:
                        active[slot] = ssd_group(pendg.pop(0), sets[slot])
                    else:
                        del active[slot]

        if STOP == 'D':
            return
        R2.reset()
        pmm_n[0] = 6
        oTt = R2.alloc("oT2", [128, 16, TB], BF16)
        mga = R2.alloc("mga", [128, 16, TB], BF16)
        mgb = R2.alloc("mgb", [128, 16, TB], BF16)
        dma(oTt.all(), oT_d.v(oT_d.ap.rearrange("(h p) t -> p h t", p=128)))

        def cons_ya(tag, pb):
            c = tag
            dma(gt.all(), g_d.v(g_d.ap[c * 128:(c + 1) * 128, :]))
            tt(mga.c(c), pb.all(), gt.all(), ALU.mult)

        dense(W["w_o_mla"], l, 0, 16, lambda kc: oTt.c(kc), [(c * 128, 128, c) for c in range(16)], cons_ya)

        def cons_yb(tag, pb):
            c = tag
            dma(gt.all(), g_d.v(g_d.ap[(16 + c) * 128:(17 + c) * 128, :]))
            t = next_tmp()
            tt(t.all(), pb.all(), gt.all(), ALU.mult)
            tt(mgb.c(c), t.all(), mga.c(c), ALU.add)

        dense(W["w_o_ssm"], l, 0, 32, lambda kc: ybT.c(kc), [(c * 128, 128, c) for c in range(16)], cons_yb)

        def cons_res(tag, pb):
            tt(X.c(tag), X.c(tag), pb.all(), ALU.add)

        dense(W["w_out"], l, 0, 16, lambda kc: mgb.c(kc), [(c * 128, 128, c) for c in range(16)], cons_res)

        if STOP == 'E':
            return
        R1.reset()
        R2.reset()
        uT = R1.alloc("uT", [128, 32, TB], BF16)
        sqt = R2.alloc("sq", [128, 16, TB], BF16)
        hT = R2.alloc("hT", [128, 16, TB], BF16)
        R2sq[0] = sqt
        rmsnorm_big(X, nlw, hT, 16, D_MODEL)
        for half in range(2):
            def cons_up(tag, pb):
                t = next_tmp()
                act(t.all(), pb.all(), AF.Relu)
                tt(uT.c(tag), t.all(), t.all(), ALU.mult)

            dense(W["w_up"], l, 0, 16, lambda kc: hT.c(kc),
                  [(half * 4096 + c * 128, 128, c) for c in range(32)], cons_up)
            for kq in range(2):
                dense(W["w_down"], l, half * 4096 + kq * 2048, 16, lambda kc, kq=kq: uT.c(kq * 16 + kc),
                      [(c * 128, 128, c) for c in range(16)], cons_res)

        if STOP == 'F':
            return
        R1.reset()
        eT = R1.alloc("eT", [128, 16, TB], F32)
        pst = R2.alloc("pst", [128, PLE_DIM], F32)
        pTb = R2.alloc("pTb", [128, 2, TB], BF16)
        rmsnorm_big(X, npw, hT, 16, D_MODEL)
        for j in range(4):
            dma(pst.all(), p_d.v(p_d.ap[l, t0 + j * 128:t0 + (j + 1) * 128, :]))
            for c in range(2):
                tr(ps[6].s(c * 128, (c + 1) * 128), pst.s(c * 128, (c + 1) * 128), ident_f.all())
            vcopy(pTb.v(pTb.t[:, :, j * 128:(j + 1) * 128]), ps[6].v(ps[6].t[:, 0:256].rearrange("p (c t) -> p c t", t=128)))

        def cons_e(tag, pb):
            act(eT.c(tag), pb.all(), AF.Copy)

        dense(W["w_ple"], l, 0, 2, lambda kc: pTb.c(kc), [(c * 128, 128, c) for c in range(16)], cons_e)

        def cons_pg(tag, pb):
            t = next_tmp()
            act(t.all(), pb.all(), AF.Sigmoid)
            tt(t.all(), t.all(), eT.c(tag), ALU.mult)
            tt(X.c(tag), X.c(tag), t.all(), ALU.add)

        dense(W["w_ple_gate"], l, 0, 16, lambda kc: hT.c(kc), [(c * 128, 128, c) for c in range(16)], cons_pg)

        if STOP == 'G':
            return
        pmm_n[0] = 4
        if not last:
            dma(xT_d.v(xT_d.ap[:, t0:t0 + TB].rearrange("(kc p) t -> p kc t", p=128)), X.all())
        else:
            R2.reset()
            ost = R2.alloc("ost", [128, D_MODEL], F32)
            for j in range(4):
                for q in range(4):
                    pb = next_pmm()
                    for i in range(4):
                        kc = q * 4 + i
                        tr(pb.s(i * 128, (i + 1) * 128), X.c(kc, j * 128, (j + 1) * 128), ident_f.all())
                    vcopy(ost.s(q * 512, (q + 1) * 512), pb.all())
                fin_ops.append(dma(out_d.v(out_d.ap[t0 + j * 128:t0 + (j + 1) * 128, :]), ost.all()))

    setup_consts()
    for l in range(L):
        if STOP == 'setup':
            break
        P.epoch = l
        layer_consts(l)
        if STOP == 'lconsts':
            break
        memset(hS.all(), 0.0)
        for tb in range(NBLK):
            block(l, tb)
    P.emit(final_waits=fin_ops)
    return nc, len(P.ops)


_CACHE = {}


def _invf_table():
    j = np.arange(0, QK_ROPE, 2, dtype=np.float32) / np.float32(QK_ROPE)
    inv = (np.float32(1.0) / (np.float32(10000.0) ** j)).astype(np.float32)
    t = np.zeros((1, 128), np.float32)
    t[0, 0:32] = inv
    t[0, 32:64] = inv
    return t


def kernel(**inputs):
    x = np.ascontiguousarray(inputs["x"], dtype=np.float32)
    B, S, _ = x.shape
    L = inputs["w_in"].shape[0]
    key = (L, S)
    if key not in _CACHE:
        _CACHE[key] = build(L=L, S=S)[0]
    nc = _CACHE[key]
    in_maps = []
    for b in range(B):
        m = {"x": x[b], "p": np.ascontiguousarray(inputs["p"][:, b]),
             "positions": np.ascontiguousarray(inputs["positions"][b:b + 1]).astype(np.int32),
             "invf": _invf_table()}
        for n in WNAMES:
            m[n] = np.ascontiguousarray(inputs[n], dtype=np.float32)
        in_maps.append(m)
    res = run_bass_kernel_spmd(nc, in_maps, core_ids=list(range(B)))
    return np.stack([res.results[b]["out"] for b in range(B)], axis=0).astype(np.float32)
```
